# Optimizing a Trainium2 kernel written in Bass

```python
import math
import jax, jax.numpy as jnp
from jax import lax
import numpy as np

D_MODEL = 4096
BATCH = 1
SEQ = 8192
DEPTH = 2

CHUNK = 64

MIX_HALF = D_MODEL // 2
HEAD_DIM = 128
N_ATTN_HEADS = MIX_HALF // HEAD_DIM
Q_BLOCK = 128
FORGET_BIAS = 3.0
POOL_WINDOWS = (2, 4, 8, 16)
POOL_GROUPS = len(POOL_WINDOWS)
POOL_GROUP_DIM = MIX_HALF // POOL_GROUPS
CONV_WIDTH = MIX_HALF
CONV_K = 3
SGU_WIDTH = MIX_HALF
SGU_CHUNK = 128
SGU_HEADS = 16
SGU_HEAD_DIM = SGU_WIDTH // SGU_HEADS
D_FF = ((8 * D_MODEL // 3 + 255) // 256) * 256
FFN_CONV_K = 3
EPS = 1e-6

EVEN_IN = 3 * MIX_HALF + N_ATTN_HEADS + MIX_HALF
ODD_IN = 3 * CONV_WIDTH + 2 * SGU_WIDTH

kernel_name = "hybrid_fox_pool_conv_sgu_trunk"


def rmsnorm(x, g):
    xf = x.astype(jnp.float32)
    y = xf * lax.rsqrt(jnp.mean(xf * xf, axis=-1, keepdims=True) + EPS)
    return (y * g.astype(jnp.float32)).astype(x.dtype)


def causal_dwconv(x, w):
    k = w.shape[0]
    s = x.shape[1]
    xp = jnp.pad(x, ((0, 0), (k - 1, 0), (0, 0)))
    y = w[0] * xp[:, 0:s]
    for i in range(1, k):
        y = y + w[i] * xp[:, i:i + s]
    return y


def forgetting_attention(q, k, v, f_logit, b_f, q_gain, k_gain):
    b, s, h, d = q.shape
    log_f = jax.nn.log_sigmoid(f_logit.astype(jnp.float32) + b_f.astype(jnp.float32))
    cum = jnp.cumsum(log_f, axis=1).transpose(0, 2, 1)
    q = rmsnorm(q, q_gain)
    k = rmsnorm(k, k_gain)
    scale = 1.0 / math.sqrt(d)
    nblk = s // Q_BLOCK
    q_blocks = q.reshape(b, nblk, Q_BLOCK, h, d).transpose(1, 0, 2, 3, 4)
    cum_blocks = cum.reshape(b, h, nblk, Q_BLOCK).transpose(2, 0, 1, 3)
    kpos = jnp.arange(s)

    def one_block(args):
        qb, cq, i = args
        qpos = i * Q_BLOCK + jnp.arange(Q_BLOCK)
        logits = jnp.einsum('bqhd,bkhd->bhqk', qb, k,
                            preferred_element_type=jnp.float32) * scale
        logits = logits + (cq[:, :, :, None] - cum[:, :, None, :])
        logits = jnp.where(kpos[None, :] <= qpos[:, None], logits, -1e30)
        p = jax.nn.softmax(logits, axis=-1)
        return jnp.einsum('bhqk,bkhd->bqhd', p.astype(v.dtype), v)

    out = lax.map(one_block, (q_blocks, cum_blocks, jnp.arange(nblk)))
    return out.transpose(1, 0, 2, 3, 4).reshape(b, s, h * d)


def multiscale_pool(z, pool_w, pool_scale):
    b, s, _ = z.shape
    zg = z.reshape(b, s, POOL_GROUPS, POOL_GROUP_DIM)
    cs = jnp.cumsum(zg.astype(jnp.float32), axis=1)
    pos = jnp.arange(1, s + 1, dtype=jnp.float32)
    outs = []
    for g, w in enumerate(POOL_WINDOWS):
        c = cs[:, :, g]
        c_prev = jnp.pad(c, ((0, 0), (w, 0), (0, 0)))[:, :s]
        cnt = jnp.minimum(pos, float(w))[None, :, None]
        pooled = (c - c_prev) / cnt - zg[:, :, g].astype(jnp.float32)
        outs.append(jnp.einsum('bsc,cd->bsd', pooled.astype(z.dtype), pool_w[g]))
    return jnp.concatenate(outs, axis=-1) * pool_scale


def short_gated_conv(v, gate_b, gate_c, conv_w):
    return gate_b * causal_dwconv(gate_c * v, conv_w)


def spatial_gating(z, norm_g, w_s, b_s):
    z = jax.nn.gelu(z, approximate=False)
    u, v = jnp.split(z, 2, axis=-1)
    v = rmsnorm(v, norm_g)
    b, s, _ = v.shape
    n = s // SGU_CHUNK
    vc = v.reshape(b, n, SGU_CHUNK, SGU_HEADS, SGU_HEAD_DIM)
    w = jnp.tril(w_s)
    mixed = jnp.einsum('hts,bnshc->bnthc', w, vc) + b_s.T[None, None, :, :, None]
    return u * mixed.reshape(b, s, SGU_WIDTH)


def even_mixer(h, w_in, b_f, q_gain, k_gain, pool_w, pool_scale, w_out):
    b, s, _ = h.shape
    p = h @ w_in
    q, k, v, f_logit, zp = jnp.split(
        p, [MIX_HALF, 2 * MIX_HALF, 3 * MIX_HALF, 3 * MIX_HALF + N_ATTN_HEADS], axis=-1)
    shp = (b, s, N_ATTN_HEADS, HEAD_DIM)
    a = forgetting_attention(q.reshape(shp), k.reshape(shp), v.reshape(shp),
                             f_logit, b_f, q_gain, k_gain)
    pm = multiscale_pool(zp, pool_w, pool_scale)
    return jnp.concatenate([a, pm], axis=-1) @ w_out


def odd_mixer(h, w_in, conv_w, sgu_norm, sgu_w, sgu_b, w_out):
    p = h @ w_in
    v, gate_b, gate_c, zs = jnp.split(p, [CONV_WIDTH, 2 * CONV_WIDTH, 3 * CONV_WIDTH], axis=-1)
    c_out = short_gated_conv(v, gate_b, gate_c, conv_w)
    d_out = spatial_gating(zs, sgu_norm, sgu_w, sgu_b)
    return jnp.concatenate([c_out, d_out], axis=-1) @ w_out


def conv_glu_ffn(h, w_up, conv_w, w_down):
    g, u = jnp.split(h @ w_up, 2, axis=-1)
    g = causal_dwconv(g, conv_w)
    return (jax.nn.silu(g) * u) @ w_down


def setup_inputs(seed: int = 0) -> dict:
    key = jax.random.key(seed)
    ks = jax.random.split(key, 24)
    f32 = jnp.float32

    def nrm(k, shape, scale):
        return jax.random.normal(k, shape, f32) * scale

    def gain(k, n):
        return 1.0 + 0.05 * jax.random.normal(k, (n,), f32)

    return {
        'x': jax.random.normal(ks[0], (BATCH, SEQ, D_MODEL), f32),
        'l0_mix_norm': gain(ks[1], D_MODEL),
        'l0_w_in': nrm(ks[2], (D_MODEL, EVEN_IN), D_MODEL ** -0.5),
        'l0_b_f': FORGET_BIAS + 0.5 * jax.random.normal(ks[3], (N_ATTN_HEADS,), f32),
        'l0_q_gain': gain(ks[4], HEAD_DIM),
        'l0_k_gain': gain(ks[5], HEAD_DIM),
        'l0_pool_w': nrm(ks[6], (POOL_GROUPS, POOL_GROUP_DIM, POOL_GROUP_DIM), POOL_GROUP_DIM ** -0.5),
        'l0_pool_scale': 1.0 + 0.1 * jax.random.normal(ks[7], (MIX_HALF,), f32),
        'l0_w_out': nrm(ks[8], (2 * MIX_HALF, D_MODEL), (2 * MIX_HALF) ** -0.5),
        'l0_ffn_norm': gain(ks[9], D_MODEL),
        'l0_ffn_w_up': nrm(ks[10], (D_MODEL, 2 * D_FF), D_MODEL ** -0.5),
        'l0_ffn_conv': nrm(ks[11], (FFN_CONV_K, D_FF), FFN_CONV_K ** -0.5),
        'l0_ffn_w_down': nrm(ks[12], (D_FF, D_MODEL), D_FF ** -0.5),
        'l1_mix_norm': gain(ks[13], D_MODEL),
        'l1_w_in': nrm(ks[14], (D_MODEL, ODD_IN), D_MODEL ** -0.5),
        'l1_conv_w': nrm(ks[15], (CONV_K, CONV_WIDTH), CONV_K ** -0.5),
        'l1_sgu_norm': gain(ks[16], SGU_WIDTH),
        'l1_sgu_w': nrm(ks[17], (SGU_HEADS, SGU_CHUNK, SGU_CHUNK), SGU_CHUNK ** -0.5),
        'l1_sgu_b': 1.0 + 0.1 * jax.random.normal(ks[18], (SGU_HEADS, SGU_CHUNK), f32),
        'l1_w_out': nrm(ks[19], (CONV_WIDTH + SGU_WIDTH, D_MODEL), (CONV_WIDTH + SGU_WIDTH) ** -0.5),
        'l1_ffn_norm': gain(ks[20], D_MODEL),
        'l1_ffn_w_up': nrm(ks[21], (D_MODEL, 2 * D_FF), D_MODEL ** -0.5),
        'l1_ffn_conv': nrm(ks[22], (FFN_CONV_K, D_FF), FFN_CONV_K ** -0.5),
        'l1_ffn_w_down': nrm(ks[23], (D_FF, D_MODEL), D_FF ** -0.5),
    }


def reference(x, l0_mix_norm, l0_w_in, l0_b_f, l0_q_gain, l0_k_gain, l0_pool_w,
              l0_pool_scale, l0_w_out, l0_ffn_norm, l0_ffn_w_up, l0_ffn_conv, l0_ffn_w_down,
              l1_mix_norm, l1_w_in, l1_conv_w, l1_sgu_norm, l1_sgu_w, l1_sgu_b, l1_w_out,
              l1_ffn_norm, l1_ffn_w_up, l1_ffn_conv, l1_ffn_w_down):
    mix_norms = (l0_mix_norm, l1_mix_norm)
    even_params = (l0_w_in, l0_b_f, l0_q_gain, l0_k_gain, l0_pool_w, l0_pool_scale, l0_w_out)
    odd_params = (l1_w_in, l1_conv_w, l1_sgu_norm, l1_sgu_w, l1_sgu_b, l1_w_out)
    ffn_params = ((l0_ffn_norm, l0_ffn_w_up, l0_ffn_conv, l0_ffn_w_down),
                  (l1_ffn_norm, l1_ffn_w_up, l1_ffn_conv, l1_ffn_w_down))
    for layer in range(DEPTH):
        h = rmsnorm(x, mix_norms[layer])
        if layer % 2 == 0:
            x = x + even_mixer(h, *even_params)
        else:
            x = x + odd_mixer(h, *odd_params)
        f_norm, f_up, f_conv, f_down = ffn_params[layer]
        x = x + conv_glu_ffn(rmsnorm(x, f_norm), f_up, f_conv, f_down)
    return x
```

```python
import numpy as np
import ml_dtypes
import concourse.bass as bass
import concourse.mybir as mybir
from concourse.bass_utils import run_bass_kernel_spmd

F32 = mybir.dt.float32
BF16 = mybir.dt.bfloat16
AF = mybir.ActivationFunctionType
ALU = mybir.AluOpType
NPBF = ml_dtypes.bfloat16

NCORES = 8
SEQ = 8192
D = 4096
T = SEQ // NCORES
KC_D = D // 128
MIX = 2048
NH = 16
DFF = 11008
NFC = DFF // 128
EPS = 1e-6
ENGS = ("pe", "act", "dve", "pool", "sp")


class Reg:
    __slots__ = ("w", "rs")

    def __init__(self):
        self.w = None
        self.rs = {}


class Op:
    __slots__ = ("fn", "deps", "signal", "dma_sem", "val", "ndma")

    def __init__(self, fn, deps, dma_sem=None, ndma=1):
        self.fn = fn
        self.deps = deps
        self.signal = False
        self.dma_sem = dma_sem
        self.val = None
        self.ndma = ndma


class Prog:
    def __init__(self, nc):
        self.nc = nc
        self.ops = {e: [] for e in ENGS}
        self.dma_cnt = {}
        self.dma_sems = {}
        self.eng_sems = {}

    def _collect(self, eng, reads, writes):
        deps = {}

        def add(tok):
            if tok is None:
                return
            c, s = tok
            if c == "pe" and eng == "pe":
                return
            if deps.get(c, -1) < s:
                deps[c] = s
        for r in reads:
            add(r.w)
        for w in writes:
            add(w.w)
            for c, s in w.rs.items():
                add((c, s))
        return deps

    def _commit(self, tok, reads, writes):
        c, s = tok
        for r in reads:
            if r.rs.get(c, -1) < s:
                r.rs[c] = s
        for w in writes:
            w.w = tok
            w.rs = {}

    def op(self, eng, fn, reads=(), writes=()):
        deps = self._collect(eng, reads, writes)
        idx = len(self.ops[eng])
        self.ops[eng].append(Op(fn, deps))
        tok = (eng, idx)
        self._commit(tok, reads, writes)
        return tok

    def dma(self, eng, semkey, fn, reads=(), writes=(), n=1):
        deps = self._collect(eng, reads, writes)
        cnt = self.dma_cnt.get(semkey, 0) + n
        self.dma_cnt[semkey] = cnt
        self.ops[eng].append(Op(fn, deps, dma_sem=semkey, ndma=n))
        tok = (("dma", semkey), cnt)
        self._commit(tok, reads, writes)
        return tok

    def wait_all_dma(self, eng):
        deps = {("dma", k): v for k, v in self.dma_cnt.items()}
        self.ops[eng].append(Op(None, deps))

    def emit(self):
        nc = self.nc
        for e in ENGS:
            for o in self.ops[e]:
                for c, s in o.deps.items():
                    if isinstance(c, str):
                        self.ops[c][s].signal = True
        for e in ENGS:
            n = 0
            for o in self.ops[e]:
                if o.dma_sem is None and o.signal:
                    n += 1
                    o.val = n
        for e in ENGS:
            self.eng_sems[e] = nc.alloc_semaphore(name=f"s_{e}")
        for k in self.dma_cnt:
            self.dma_sems[k] = nc.alloc_semaphore(name=f"d_{k}")
        prog = self

        def run(e, eng):
            waited = {}
            for o in prog.ops[e]:
                for c, s in o.deps.items():
                    if isinstance(c, str):
                        v = prog.ops[c][s].val
                        sem = prog.eng_sems[c]
                    else:
                        v = 16 * s
                        sem = prog.dma_sems[c[1]]
                    if waited.get(c, 0) < v:
                        eng.wait_ge(sem, v)
                        waited[c] = v
                if o.fn is None:
                    continue
                ins = o.fn(eng)
                if o.dma_sem is not None:
                    if not isinstance(ins, (list, tuple)):
                        ins = [ins]
                    assert len(ins) == o.ndma
                    for i in ins:
                        i.then_inc(prog.dma_sems[o.dma_sem], 16)
                elif o.signal:
                    ins.then_inc(prog.eng_sems[e], 1)

        with nc.Block() as block:
            @block.tensor
            def _(eng):
                run("pe", eng)

            @block.scalar
            def _(eng):
                run("act", eng)

            @block.vector
            def _(eng):
                run("dve", eng)

            @block.gpsimd
            def _(eng):
                run("pool", eng)

            @block.sync
            def _(eng):
                run("sp", eng)


class Cx:
    def __init__(self):
        self.nc = bass.Bass("TRN2", target_bir_lowering=False)
        self.P = Prog(self.nc)
        self.n = 0
        self.ps = [self.nc.alloc_psum_tensor(f"ps{i}", [128, 512], F32) for i in range(8)]
        self.ps_regs = [Reg() for _ in range(8)]
        self.pe_defer = []
        self.ones = self.sb([128, 128], BF16, "ones")
        self.ones_r = Reg()
        self.P.op("pool", lambda e: e.memset(self.ones[:, :], 1.0), writes=[self.ones_r])

    def uid(self, s):
        self.n += 1
        return f"{s}{self.n}"

    def sb(self, shape, dt, name="t"):
        return self.nc.alloc_sbuf_tensor(self.uid(name), list(shape), dt)

    def din(self, name, shape, dt=F32):
        return self.nc.dram_tensor(name, list(shape), dt, kind="ExternalInput").ap()

    def dout(self, name, shape, dt=F32):
        return self.nc.dram_tensor(name, list(shape), dt, kind="ExternalOutput").ap()

    def flush_pe(self):
        d = self.pe_defer
        self.pe_defer = []
        for f in d:
            f()

    def load(self, dram_ap, shape, dt=F32, name="c", eng="sp"):
        t = self.sb(shape, dt, name)
        r = Reg()
        sl = tuple(slice(None) for _ in shape)
        self.P.dma(eng, self.uid("ld"), lambda e: e.dma_start(out=t[sl], in_=dram_ap), writes=[r])
        return t, r


class Ring:
    def __init__(self, cx, n, shape, dt, name="r"):
        self.tiles = [cx.sb(shape, dt, name) for _ in range(n)]
        self.regs = [Reg() for _ in range(n)]
        self.keys = [cx.uid(name + "k") for _ in range(n)]
        self.i = 0
        self.n = n

    @classmethod
    def over(cls, cx, tiles, regs, name="r"):
        o = cls.__new__(cls)
        o.tiles = tiles
        o.regs = regs
        o.keys = [cx.uid(name + "k") for _ in tiles]
        o.i = 0
        o.n = len(tiles)
        return o

    def next(self):
        j = self.i % self.n
        self.i += 1
        return self.tiles[j], self.regs[j], self.keys[j]


def load_norm(cx, xT, gcol, gcol_r, Tn, h, h_regs, rings=None, x_regs=None):
    P = cx.P
    nb = (Tn + 511) // 512
    xs = Ring(cx, 3, [128, Tn], F32, "xs") if rings is None else rings[0]
    sq = Ring(cx, 2, [128, Tn], BF16, "sq") if rings is None else rings[1]
    rstd = cx.sb([128, Tn], F32, "rstd")
    rstd_r = Reg()
    for c in range(KC_D):
        t, r, k = xs.next()
        P.dma("sp", k, lambda e, t=t, c=c: e.dma_start(out=t[:, 0:Tn], in_=xT[c * 128:(c + 1) * 128, :]), reads=([x_regs[c]] if x_regs else []), writes=[r])
        s, sr, _ = sq.next()
        P.op("act", lambda e, t=t, s=s: e.activation(out=s[:, 0:Tn], in_=t[:, 0:Tn], func=AF.Square), reads=[r], writes=[sr])
        for j in range(nb):
            n = min(512, Tn - j * 512)
            P.op("pe", lambda e, s=s, j=j, n=n, c=c: e.matmul(cx.ps[j][:, 0:n], lhsT=cx.ones[:, :], rhs=s[:, j * 512:j * 512 + n],
                                                          start=(c == 0), stop=(c == KC_D - 1)),
                 reads=[sr, cx.ones_r], writes=[cx.ps_regs[j]])
    for j in range(nb):
        n = min(512, Tn - j * 512)
        P.op("act", lambda e, j=j, n=n: e.activation(out=rstd[:, j * 512:j * 512 + n], in_=cx.ps[j][:, 0:n], func=AF.Sqrt,
                                                   bias=cx.eps_t[:, 0:1], scale=1.0 / D),
             reads=[cx.ps_regs[j], cx.eps_r], writes=[rstd_r])
    P.op("dve", lambda e: e.reciprocal(out=rstd[:, :], in_=rstd[:, :]), reads=[rstd_r], writes=[rstd_r])
    for c in range(KC_D):
        t, r, k = xs.next()
        P.dma("sp", k, lambda e, t=t, c=c: e.dma_start(out=t[:, 0:Tn], in_=xT[c * 128:(c + 1) * 128, :]), reads=([x_regs[c]] if x_regs else []), writes=[r])
        P.op("dve", lambda e, t=t, c=c: e.scalar_tensor_tensor(out=h[:, c, :], in0=t[:, 0:Tn], scalar=gcol[:, c:c + 1], in1=rstd[:, :],
                                                              op0=ALU.mult, op1=ALU.mult),
             reads=[r, gcol_r, rstd_r], writes=[h_regs[c]])
    return xs, sq


class WStream:
    def __init__(self, cx, kcmax, gw, nslots=2):
        self.cx = cx
        self.gw = gw
        self.kcmax = kcmax
        self.nparts = (kcmax + 7) // 8
        self.slots = [cx.sb([128, kcmax, gw], BF16, "ws") for _ in range(nslots)]
        self.regs = [[Reg() for _ in range(self.nparts)] for _ in range(nslots)]
        self.keys = [[cx.uid("wk") for _ in range(self.nparts)] for _ in range(nslots)]
        self.i = 0
        self.nslots = nslots

    def load(self, W, kc_n, segs):
        cx = self.cx
        s = self.i % self.nslots
        self.i += 1
        Wv = W.rearrange("(kc p) n -> p kc n", p=128)
        for q in range((kc_n + 7) // 8):
            k0, k1 = q * 8, min(kc_n, q * 8 + 8)

            def fn(e, s=s, k0=k0, k1=k1):
                return [e.dma_start(out=self.slots[s][:, k0:k1, d0:d0 + w], in_=Wv[:, k0:k1, c0:c0 + w]) for (d0, c0, w) in segs]
            cx.P.dma("pool", self.keys[s][q], fn, writes=[self.regs[s][q]], n=len(segs))
        return s


def gemm(cx, ws, W, kc_n, groups, subtiles):
    P = cx.P
    unit = cx.unit if hasattr(cx, "unit") else 0
    for gi, g in enumerate(groups):
        s = ws.load(W, kc_n, g["segs"])
        slot = ws.slots[s]
        for (stn, ht, hr, tok0, n) in subtiles:
            if stn == "halo" and not g.get("halo"):
                continue
            bset = [0, 1, 2, 3] if unit % 2 == 0 else [4, 5, 6, 7]
            unit += 1
            if g["kind"] == "fm":
                widths = g["widths"]
                for kc in range(kc_n):
                    for mi, wd in enumerate(widths):
                        b = bset[mi]
                        P.op("pe", lambda e, b=b, kc=kc, mi=mi, wd=wd, ht=ht, tok0=tok0, n=n, slot=slot: e.matmul(
                            cx.ps[b][0:wd, 0:n], lhsT=slot[:, kc, mi * 128:mi * 128 + wd], rhs=ht[:, kc, tok0:tok0 + n],
                            start=(kc == 0), stop=(kc == kc_n - 1)),
                            reads=[ws.regs[s][kc // 8], hr[kc]], writes=[cx.ps_regs[b]])
                nb = len(widths)
            else:
                ntb = n // 128
                gwid = g["gw"]
                for kc in range(kc_n):
                    for tb in range(ntb):
                        b = bset[tb]
                        P.op("pe", lambda e, b=b, kc=kc, tb=tb, ht=ht, tok0=tok0, slot=slot, gwid=gwid: e.matmul(
                            cx.ps[b][:, 0:gwid], lhsT=ht[:, kc, tok0 + tb * 128:tok0 + (tb + 1) * 128], rhs=slot[:, kc, 0:gwid],
                            start=(kc == 0), stop=(kc == kc_n - 1)),
                            reads=[ws.regs[s][kc // 8], hr[kc]], writes=[cx.ps_regs[b]])
                nb = ntb
            cx.flush_pe()
            g["epi"](gi, g, stn, tok0, n, bset)
    cx.flush_pe()
    cx.unit = unit


def build_l1():
    cx = Cx()
    P = cx.P
    nc = cx.nc
    xT = cx.din("xT", [D, T])
    xh = cx.din("xh", [D, 16])
    W = cx.din("w_in", [D, 8208])
    gcol_d = cx.din("gcol", [128, KC_D])
    qk_d = cx.din("qkg", [128, 2])
    bf_d = cx.din("bf", [16, 1])
    pw_d = cx.din("pool_w", [4, 512, 512])
    psc_d = cx.din("pscol", [128, 16])
    icn_d = cx.din("invcnt", [128, 4, 16])
    qT = cx.dout("qT", [NH, 128, T], BF16)
    kT = cx.dout("kT", [NH, 128, T], BF16)
    Vo = cx.dout("V", [T, MIX], BF16)
    lf = cx.dout("logf", [NH, T])
    pmT = cx.dout("pmT", [MIX, T], BF16)

    cx.eps_t = cx.sb([128, 1], F32, "eps")
    cx.eps_r = Reg()
    P.op("pool", lambda e: e.memset(cx.eps_t[:, :], EPS), writes=[cx.eps_r])
    gcol, gcol_r = cx.load(gcol_d, [128, KC_D])
    qkg, qkg_r = cx.load(qk_d, [128, 2])
    bft, bft_r = cx.load(bf_d, [16, 1])
    psc, psc_r = cx.load(psc_d, [128, 16])
    icn, icn_r = cx.load(icn_d, [128, 4, 16])
    P.op("dve", lambda e: e.tensor_scalar(out=qkg[:, 0:1], in0=qkg[:, 0:1], scalar1=float(128 ** -0.5), scalar2=None, op0=ALU.mult),
         reads=[qkg_r], writes=[qkg_r])
    P.op("dve", lambda e: e.tensor_scalar(out=bft[:, :], in0=bft[:, :], scalar1=-1.0, scalar2=None, op0=ALU.mult),
         reads=[bft_r], writes=[bft_r])

    h = cx.sb([128, KC_D, T], BF16, "h")
    h_regs = [Reg() for _ in range(KC_D)]
    hh = cx.sb([128, KC_D, 16], BF16, "hh")
    hh_regs = [Reg() for _ in range(KC_D)]
    zb = [cx.sb([128, 16 + T], F32, "zb") for _ in range(2)]
    zb_r = [Reg() for _ in range(2)]
    pa = [cx.sb([128, 16 + T], F32, "pa") for _ in range(1)]
    pa_r = [Reg() for _ in range(1)]
    pbb = [cx.sb([128, 16 + T], F32, "pb") for _ in range(1)]
    pb_r = [Reg() for _ in range(1)]
    sqring = Ring(cx, 2, [128, T], BF16, "sq")
    xsring = Ring.over(cx, [zb[0], zb[1], pa[0]], [zb_r[0], zb_r[1], pa_r[0]], "xs")
    load_norm(cx, xh, gcol, gcol_r, 16, hh, hh_regs, rings=(xsring, sqring))
    load_norm(cx, xT, gcol, gcol_r, T, h, h_regs, rings=(xsring, sqring))

    ws = WStream(cx, KC_D, 384)
    subtiles = [("halo", hh, hh_regs, 0, 16), ("s0", h, h_regs, 0, 512), ("s1", h, h_regs, 512, 512)]

    pooled = cx.sb([128, 16, T], BF16, "pooled")
    pooled_regs = [Reg() for _ in range(16)]
    pcnt = [0]

    def epi_zp(gi, g, stn, tok0, n, bset):
        grp = g["pg"]
        wlen = (2, 4, 8, 16)[grp]
        for mi in range(2):
            off = 0 if stn == "halo" else 16 + tok0
            P.op("act", lambda e, mi=mi, off=off, n=n, b=bset[mi]: e.activation(out=zb[mi][:, off:off + n], in_=cx.ps[b][:, 0:n], func=AF.Copy),
                 reads=[cx.ps_regs[bset[mi]]], writes=[zb_r[mi]])
        if stn != "s1":
            return
        L = 16 + T
        for mi in range(2):
            ch = g["ch0"] + mi
            j = 0
            src, src_r = zb[mi], zb_r[mi]
            bufs = [(pa[j], pa_r[j]), (pbb[j], pb_r[j])]
            sh = 1
            bi = 0
            while sh < wlen:
                dst, dst_r = bufs[bi]
                eng = "pool" if (mi % 2 == 0) else "dve"
                P.op(eng, lambda e, dst=dst, src=src, sh=sh: e.tensor_tensor(out=dst[:, sh:L], in0=src[:, sh:L], in1=src[:, 0:L - sh], op=ALU.add),
                     reads=[src_r], writes=[dst_r])
                src, src_r = dst, dst_r
                bi ^= 1
                sh *= 2
            P.op("dve", lambda e, src=src, mi=mi, ch=ch, wlen=wlen: e.scalar_tensor_tensor(
                out=pooled[:, ch, :], in0=src[:, 16:L], scalar=1.0 / wlen, in1=zb[mi][:, 16:L], op0=ALU.mult, op1=ALU.subtract),
                reads=[src_r, zb_r[mi]], writes=[pooled_regs[ch]])
            dst, dst_r = bufs[bi]
            P.op("dve", lambda e, dst=dst, src=src, grp=grp: e.tensor_tensor(out=dst[:, 0:16], in0=src[:, 16:32], in1=icn[:, grp, :], op=ALU.mult),
                 reads=[src_r, icn_r], writes=[dst_r])
            P.op("dve", lambda e, dst=dst, mi=mi, ch=ch: e.tensor_tensor(out=pooled[:, ch, 0:16], in0=dst[:, 0:16], in1=zb[mi][:, 16:32], op=ALU.subtract),
                 reads=[dst_r, zb_r[mi]], writes=[pooled_regs[ch]])

    lfr = Ring(cx, 2, [16, 512], F32, "lf")

    def epi_f(gi, g, stn, tok0, n, bset):
        t, r, k = lfr.next()
        b = bset[0]
        P.op("act", lambda e: e.activation(out=t[:, :], in_=cx.ps[b][0:16, 0:n], func=AF.Exp, bias=bft[:, 0:1], scale=-1.0),
             reads=[cx.ps_regs[b], bft_r], writes=[r])
        P.op("act", lambda e: e.activation(out=t[:, :], in_=t[:, :], func=AF.Ln, bias=1.0, scale=1.0), reads=[r], writes=[r])
        P.op("dve", lambda e: e.tensor_scalar(out=t[:, :], in0=t[:, :], scalar1=-1.0, scalar2=None, op0=ALU.mult), reads=[r], writes=[r])
        P.dma("sp", k, lambda e: e.dma_start(out=lf[:, tok0:tok0 + n], in_=t[:, :]), reads=[r])

    vr = Ring(cx, 4, [128, 256], BF16, "vo")

    def epi_v(gi, g, stn, tok0, n, bset):
        c0 = g["c0"]
        for tb in range(4):
            t, r, k = vr.next()
            b = bset[tb]
            P.op("act", lambda e, t=t, b=b: e.activation(out=t[:, :], in_=cx.ps[b][:, 0:256], func=AF.Copy), reads=[cx.ps_regs[b]], writes=[r])
            r0 = tok0 + tb * 128
            P.dma("sp", k, lambda e, t=t, r0=r0: e.dma_start(out=Vo[r0:r0 + 128, c0:c0 + 256], in_=t[:, :]), reads=[r])

    sqr = Ring(cx, 3, [128, 512], BF16, "qsq")
    rtr = Ring(cx, 3, [128, 512], F32, "qrt")
    qor = Ring(cx, 3, [128, 512], BF16, "qo")

    def epi_qk(gi, g, stn, tok0, n, bset):
        which = g["which"]
        dst = qT if which == 0 else kT
        nbk = bset[3]
        for mi in range(len(g["widths"])):
            hd = g["h0"] + mi
            b = bset[mi]
            s, sr, _ = sqr.next()
            P.op("act", lambda e, s=s, b=b: e.activation(out=s[:, :], in_=cx.ps[b][:, :], func=AF.Square), reads=[cx.ps_regs[b]], writes=[sr])

            def later(s=s, sr=sr, b=b, hd=hd, nbk=nbk):
                P.op("pe", lambda e: e.matmul(cx.ps[nbk][:, :], lhsT=cx.ones[:, :], rhs=s[:, :], start=True, stop=True),
                     reads=[sr, cx.ones_r], writes=[cx.ps_regs[nbk]])
                rt, rr, _ = rtr.next()
                P.op("act", lambda e: e.activation(out=rt[:, :], in_=cx.ps[nbk][:, :], func=AF.Sqrt, bias=cx.eps_t[:, 0:1], scale=1.0 / 128),
                     reads=[cx.ps_regs[nbk], cx.eps_r], writes=[rr])
                P.op("dve", lambda e: e.reciprocal(out=rt[:, :], in_=rt[:, :]), reads=[rr], writes=[rr])
                o, orr, k = qor.next()
                P.op("dve", lambda e: e.scalar_tensor_tensor(out=o[:, :], in0=cx.ps[b][:, :], scalar=qkg[:, which:which + 1], in1=rt[:, :],
                                                             op0=ALU.mult, op1=ALU.mult),
                     reads=[cx.ps_regs[b], qkg_r, rr], writes=[orr])
                P.dma("sp", k, lambda e: e.dma_start(out=dst[hd, :, tok0:tok0 + n], in_=o[:, :]), reads=[orr])
            cx.pe_defer.append(later)

    groups = []
    for cp in range(8):
        c0 = 6160 + cp * 256
        groups.append(dict(kind="fm", segs=[(0, c0, 256)], widths=[128] * 2, epi=epi_zp, halo=True, pg=cp // 2, ch0=cp * 2))
    groups.append(dict(kind="fm", segs=[(0, 6144, 16)], widths=[16], epi=epi_f))
    for vg in range(8):
        groups.append(dict(kind="tm", segs=[(0, 4096 + vg * 256, 256)], gw=256, epi=epi_v, c0=vg * 256))
    for which in range(2):
        for h0 in range(0, 16, 3):
            nh = min(3, 16 - h0)
            c0 = which * 2048 + h0 * 128
            groups.append(dict(kind="fm", segs=[(0, c0, nh * 128)], widths=[128] * nh, epi=epi_qk, which=which, h0=h0))
    gemm(cx, ws, W, KC_D, groups, subtiles)

    pmr = Ring(cx, 3, [128, 512], BF16, "pmo")
    for pg in range(4):
        def epi_pm(gi, g, stn, tok0, n, bset, pg=pg):
            for mi in range(2):
                ch = pg * 4 + g["m0"] + mi
                o, orr, k = pmr.next()
                b = bset[mi]
                P.op("act", lambda e, o=o, b=b, ch=ch: e.activation(out=o[:, :], in_=cx.ps[b][:, :], func=AF.Copy, scale=psc[:, ch:ch + 1]),
                     reads=[cx.ps_regs[b], psc_r], writes=[orr])
                P.dma("sp", k, lambda e, o=o, ch=ch: e.dma_start(out=pmT[ch * 128:(ch + 1) * 128, tok0:tok0 + n], in_=o[:, :]), reads=[orr])
        pview = pooled[:, pg * 4:(pg + 1) * 4, :]
        gemm(cx, ws, pw_d[pg], 4, [dict(kind="fm", segs=[(0, m0 * 128, 256)], widths=[128] * 2, epi=epi_pm, m0=m0) for m0 in (0, 2)],
             [("s0", pview, pooled_regs[pg * 4:(pg + 1) * 4], 0, 512), ("s1", pview, pooled_regs[pg * 4:(pg + 1) * 4], 512, 512)])
    P.wait_all_dma("sp")
    P.emit()
    return nc


def colmajor(v, n=128):
    return np.ascontiguousarray(np.asarray(v, np.float32).reshape(-1, n).T)


def run(nc, in_maps, trace=False):
    res = run_bass_kernel_spmd(nc, in_maps, core_ids=list(range(NCORES)), trace=trace)
    return res


def launch1(inp, trace=False):
    x = np.asarray(inp["x"], np.float32)[0]
    nc = build_l1()
    w_in = np.ascontiguousarray(np.asarray(inp["l0_w_in"], np.float32))
    gcol = colmajor(inp["l0_mix_norm"])
    qkg = np.stack([np.asarray(inp["l0_q_gain"], np.float32), np.asarray(inp["l0_k_gain"], np.float32)], 1)
    bf = np.asarray(inp["l0_b_f"], np.float32).reshape(16, 1)
    pw = np.ascontiguousarray(np.asarray(inp["l0_pool_w"], np.float32))
    psc = colmajor(inp["l0_pool_scale"])
    maps = []
    for c in range(NCORES):
        xs = x[c * T:(c + 1) * T]
        xh = x[c * T - 16:c * T] if c > 0 else np.zeros((16, D), np.float32)
        icn = np.zeros((128, 4, 16), np.float32)
        for g, w in enumerate((2, 4, 8, 16)):
            pos = np.arange(16) + 1 + c * T
            icn[:, g, :] = 1.0 / np.minimum(pos, w)
        maps.append(dict(xT=np.ascontiguousarray(xs.T), xh=np.ascontiguousarray(xh.T), w_in=w_in, gcol=gcol,
                         qkg=np.ascontiguousarray(qkg), bf=bf, pool_w=pw, pscol=psc, invcnt=icn))
    return run(nc, maps, trace)


HPC = NH // NCORES
NQT = SEQ // 512
NKB = SEQ // 128


def build_l2a():
    cx = Cx()
    P = cx.P
    qT = cx.din("qT", [HPC, 128, SEQ], BF16)
    kT = cx.din("kT", [HPC, 128, SEQ], BF16)
    Vd = cx.din("V", [HPC, 128, NKB, 128], BF16)
    lf6 = cx.din("lf6", [HPC, 6, SEQ])
    coef_d = cx.din("coef", [6, 8])
    tri_d = cx.din("tri", [128, 128])
    aT = cx.dout("aT", [HPC, 128, SEQ], BF16)
    coef, coef_r = cx.load(coef_d, [6, 8])
    tri, tri_r = cx.load(tri_d, [128, 128])
    qs, ks, vs, qa, ka = [], [], [], [], []
    for hh in range(HPC):
        q_t, q_r = cx.load(qT[hh], [128, SEQ], BF16, "q")
        k_t, k_r = cx.load(kT[hh], [128, SEQ], BF16, "k")
        v_t, v_r = cx.load(Vd[hh], [128, NKB, 128], BF16, "v")
        qs.append((q_t, q_r)); ks.append((k_t, k_r)); vs.append((v_t, v_r))
    SG = 2048
    lft = cx.sb([6, SG], F32, "lft"); lft_r = Reg()
    c6 = cx.sb([6, SG], F32, "c6"); c6_r = Reg()
    r1 = cx.sb([6, SG], F32, "r1"); r1_r = Reg()
    hi = cx.sb([6, SG], BF16, "hi"); hi_r = Reg()
    mid = cx.sb([6, SG], BF16, "mid"); mid_r = Reg()
    lo = cx.sb([6, SG], BF16, "lo"); lo_r = Reg()
    tmp = cx.sb([6, SG], F32, "tmpa"); tmp_r = Reg()
    carry = cx.sb([6, 1], F32, "carry"); carry_r = Reg()
    qa_t = cx.sb([6, SEQ], BF16, "qa"); qa_r = Reg()
    ka_t = cx.sb([6, SEQ], BF16, "ka"); ka_r = Reg()
    aug_done = [False] * HPC

    def build_aug(hh):
        for sg in range(SEQ // SG):
            t0 = sg * SG
            P.dma("sp", "lftk", lambda e, hh=hh, t0=t0: e.dma_start(out=lft[:, :], in_=lf6[hh, :, t0:t0 + SG]), writes=[lft_r])
            P.op("pool", lambda e: e.memset(tmp[:, :], 1.0), writes=[tmp_r])
            if sg == 0:
                P.op("dve", lambda e: e.tensor_tensor_scan(out=c6[:, :], data0=tmp[:, :], data1=lft[:, :], initial=0.0, op0=ALU.mult, op1=ALU.add),
                     reads=[lft_r, tmp_r], writes=[c6_r])
            else:
                P.op("dve", lambda e: e.tensor_tensor_scan(out=c6[:, :], data0=tmp[:, :], data1=lft[:, :], initial=carry[:, 0:1], op0=ALU.mult, op1=ALU.add),
                     reads=[lft_r, tmp_r, carry_r], writes=[c6_r])
            P.op("dve", lambda e: e.tensor_copy(out=carry[:, :], in_=c6[:, SG - 1:SG]), reads=[c6_r], writes=[carry_r])
            P.op("dve", lambda e: e.tensor_copy(out=hi[:, :], in_=c6[:, :]), reads=[c6_r], writes=[hi_r])
            P.op("dve", lambda e: e.tensor_tensor(out=r1[:, :], in0=c6[:, :], in1=hi[:, :], op=ALU.subtract), reads=[c6_r, hi_r], writes=[r1_r])
            P.op("dve", lambda e: e.tensor_copy(out=mid[:, :], in_=r1[:, :]), reads=[r1_r], writes=[mid_r])
            P.op("dve", lambda e: e.tensor_tensor(out=r1[:, :], in0=r1[:, :], in1=mid[:, :], op=ALU.subtract), reads=[r1_r, mid_r], writes=[r1_r])
            P.op("dve", lambda e: e.tensor_copy(out=lo[:, :], in_=r1[:, :]), reads=[r1_r], writes=[lo_r])
            for (dst, dst_r, o) in ((qa_t, qa_r, 0), (ka_t, ka_r, 4)):
                P.op("dve", lambda e, o=o: e.tensor_scalar(out=tmp[:, :], in0=hi[:, :], scalar1=coef[:, o:o + 1], scalar2=coef[:, o + 3:o + 4], op0=ALU.mult, op1=ALU.add),
                     reads=[hi_r, coef_r], writes=[tmp_r])
                P.op("dve", lambda e, o=o: e.scalar_tensor_tensor(out=tmp[:, :], in0=mid[:, :], scalar=coef[:, o + 1:o + 2], in1=tmp[:, :], op0=ALU.mult, op1=ALU.add),
                     reads=[mid_r, coef_r, tmp_r], writes=[tmp_r])
                P.op("dve", lambda e, o=o, dst=dst, t0=t0: e.scalar_tensor_tensor(out=dst[:, t0:t0 + SG], in0=lo[:, :], scalar=coef[:, o + 2:o + 3], in1=tmp[:, :], op0=ALU.mult, op1=ALU.add),
                     reads=[lo_r, coef_r, tmp_r], writes=[dst_r])
    LA = 2
    pr = Ring(cx, LA + 2, [128, 512], BF16, "pT")
    rdr = Ring(cx, 2, [128, 512], F32, "rden")
    aor = Ring(cx, 2, [128, 512], BF16, "ao")
    sbanks = [0, 1, 6, 7]
    blocks = []
    unit = 0
    for hh in range(HPC):
        for qt in range(NQT):
            ob = 2 + (unit % 2) * 2
            unit += 1
            nkb = 4 * qt + 4
            for kb in range(nkb):
                blocks.append(dict(hh=hh, qt=qt, kb=kb, nkb=nkb, ob=ob, db=ob + 1))

    def stage_a(i):
        bl = blocks[i]
        hh, qt, kb = bl["hh"], bl["qt"], bl["kb"]
        if not aug_done[hh]:
            build_aug(hh)
            aug_done[hh] = True
        q_t, q_r = qs[hh]; k_t, k_r = ks[hh]
        q0 = qt * 512
        j = kb - 4 * qt
        c0 = 128 * j if j > 0 else 0
        n = 512 - c0
        sbank = sbanks[i % len(sbanks)]
        bl.update(c0=c0, n=n, j=j)
        P.op("pe", lambda e: e.matmul(cx.ps[sbank][:, 0:n], lhsT=k_t[:, kb * 128:(kb + 1) * 128], rhs=q_t[:, q0 + c0:q0 + 512], start=True, stop=False),
             reads=[k_r, q_r], writes=[cx.ps_regs[sbank]])
        P.op("pe", lambda e: e.matmul(cx.ps[sbank][:, 0:n], lhsT=ka_t[:, kb * 128:(kb + 1) * 128], rhs=qa_t[:, q0 + c0:q0 + 512], start=False, stop=True),
             reads=[ka_r, qa_r], writes=[cx.ps_regs[sbank]])
        pt, pt_r, _ = pr.next()
        bl.update(pt=pt, pt_r=pt_r)
        P.op("act", lambda e: e.activation(out=pt[:, 0:n], in_=cx.ps[sbank][:, 0:n], func=AF.Exp), reads=[cx.ps_regs[sbank]], writes=[pt_r])
        if j >= 0:
            P.op("pool", lambda e: e.tensor_tensor(out=pt[:, 0:128], in0=pt[:, 0:128], in1=tri[:, :], op=ALU.mult), reads=[pt_r, tri_r], writes=[pt_r])

    def stage_b(i):
        bl = blocks[i]
        hh, qt, kb, nkb, ob, db = bl["hh"], bl["qt"], bl["kb"], bl["nkb"], bl["ob"], bl["db"]
        c0, n, pt, pt_r = bl["c0"], bl["n"], bl["pt"], bl["pt_r"]
        v_t, v_r = vs[hh]
        q0 = qt * 512
        P.op("pe", lambda e: e.matmul(cx.ps[ob][:, c0:512], lhsT=v_t[:, kb, :], rhs=pt[:, 0:n], start=(kb == 0), stop=(kb == nkb - 1)),
             reads=[v_r, pt_r], writes=[cx.ps_regs[ob]])
        P.op("pe", lambda e: e.matmul(cx.ps[db][:, c0:512], lhsT=cx.ones[:, :], rhs=pt[:, 0:n], start=(kb == 0), stop=(kb == nkb - 1)),
             reads=[cx.ones_r, pt_r], writes=[cx.ps_regs[db]])
        if kb == nkb - 1:
            rd, rd_r, _ = rdr.next()
            P.op("dve", lambda e: e.reciprocal(out=rd[:, :], in_=cx.ps[db][:, :]), reads=[cx.ps_regs[db]], writes=[rd_r])
            ao, ao_r, k = aor.next()
            P.op("dve", lambda e: e.tensor_tensor(out=ao[:, :], in0=cx.ps[ob][:, :], in1=rd[:, :], op=ALU.mult), reads=[cx.ps_regs[ob], rd_r], writes=[ao_r])
            P.dma("sp", k, lambda e: e.dma_start(out=aT[hh, :, q0:q0 + 512], in_=ao[:, :]), reads=[ao_r])
    for i in range(len(blocks) + LA):
        if i < len(blocks):
            stage_a(i)
        if i - LA >= 0:
            stage_b(i - LA)
    P.wait_all_dma("sp")
    P.emit()
    return cx.nc


def launch2a(inp, l1res, trace=False):
    nc = build_l2a()
    qT = np.concatenate([np.asarray(l1res[c]["qT"]) for c in range(NCORES)], axis=2)
    kT = np.concatenate([np.asarray(l1res[c]["kT"]) for c in range(NCORES)], axis=2)
    V = np.concatenate([np.asarray(l1res[c]["V"]) for c in range(NCORES)], axis=0)
    lf = np.concatenate([np.asarray(l1res[c]["logf"]) for c in range(NCORES)], axis=1)
    coef = np.zeros((6, 8), np.float32)
    coef[0, 0] = 1; coef[1, 1] = 1; coef[2, 2] = 1; coef[3:6, 3] = 1
    coef[3, 4] = -1; coef[4, 5] = -1; coef[5, 6] = -1; coef[0:3, 7] = 1
    tri = np.triu(np.ones((128, 128), np.float32))
    maps = []
    for c in range(NCORES):
        hs = slice(c * HPC, (c + 1) * HPC)
        Vh = V.reshape(NKB, 128, NH, 128)[:, :, hs].transpose(2, 1, 0, 3)
        maps.append(dict(qT=np.ascontiguousarray(qT[hs]), kT=np.ascontiguousarray(kT[hs]), V=np.ascontiguousarray(Vh),
                         lf6=np.ascontiguousarray(np.repeat(lf[hs][:, None, :], 6, axis=1)), coef=coef, tri=tri))
    return run(nc, maps, trace)


def build_bc(layer):
    cx = Cx()
    P = cx.P
    xT = cx.din("xT", [D, T])
    hc = cx.din("hcatT", [D, T], BF16)
    w_out = cx.din("w_out", [D, D])
    fg_d = cx.din("fgcol", [128, KC_D])
    w_up = cx.din("w_up", [D, 2 * DFF])
    x1T = cx.dout("x1T", [D, T])
    gT = cx.dout("gT", [DFF, T], BF16)
    uT = cx.dout("uT", [DFF, T], BF16)
    cx.eps_t = cx.sb([128, 1], F32, "eps")
    cx.eps_r = Reg()
    P.op("pool", lambda e: e.memset(cx.eps_t[:, :], EPS), writes=[cx.eps_r])
    fg, fg_r = cx.load(fg_d, [128, KC_D])
    h = cx.sb([128, KC_D, T], BF16, "h")
    h_regs = [Reg() for _ in range(KC_D)]
    hv = hc.rearrange("(c p) t -> p c t", p=128)
    if layer == 0:
        for q in range(4):
            P.dma("sp", cx.uid("hl"), lambda e, q=q: e.dma_start(out=h[:, q * 8:(q + 1) * 8, :], in_=hv[:, q * 8:(q + 1) * 8, :]),
                  writes=h_regs[q * 8:(q + 1) * 8])
    else:
        for q in range(2, 4):
            P.dma("sp", cx.uid("hl"), lambda e, q=q: e.dma_start(out=h[:, q * 8:(q + 1) * 8, :], in_=hv[:, q * 8:(q + 1) * 8, :]),
                  writes=h_regs[q * 8:(q + 1) * 8])
        cv_d = cx.din("cvx", [MIX, 2 + T], BF16)
        gb_d = cx.din("gbT", [MIX, T], BF16)
        cw_d = cx.din("cwcol", [128, 16, 3])
        cw, cw_r = cx.load(cw_d, [128, 16, 3])
        cvr = Ring(cx, 2, [128, 2 + T], BF16, "cv")
        gbr = Ring(cx, 2, [128, T], BF16, "gb")
        yr = Ring(cx, 2, [128, T], F32, "cy")
        for j in range(16):
            cvt, cvt_r, k1 = cvr.next()
            gbt, gbt_r, k2 = gbr.next()
            y, y_r, _ = yr.next()
            P.dma("sp", k1, lambda e, cvt=cvt, j=j: e.dma_start(out=cvt[:, :], in_=cv_d[j * 128:(j + 1) * 128, :]), writes=[cvt_r])
            P.dma("sp", k2, lambda e, gbt=gbt, j=j: e.dma_start(out=gbt[:, :], in_=gb_d[j * 128:(j + 1) * 128, :]), writes=[gbt_r])
            P.op("dve", lambda e, y=y, cvt=cvt, j=j: e.tensor_scalar(out=y[:, :], in0=cvt[:, 2:2 + T], scalar1=cw[:, j, 2:3], scalar2=None, op0=ALU.mult),
                 reads=[cvt_r, cw_r], writes=[y_r])
            P.op("dve", lambda e, y=y, cvt=cvt, j=j: e.scalar_tensor_tensor(out=y[:, :], in0=cvt[:, 1:1 + T], scalar=cw[:, j, 1:2], in1=y[:, :], op0=ALU.mult, op1=ALU.add),
                 reads=[cvt_r, cw_r, y_r], writes=[y_r])
            P.op("dve", lambda e, y=y, cvt=cvt, j=j: e.scalar_tensor_tensor(out=y[:, :], in0=cvt[:, 0:T], scalar=cw[:, j, 0:1], in1=y[:, :], op0=ALU.mult, op1=ALU.add),
                 reads=[cvt_r, cw_r, y_r], writes=[y_r])
            P.op("dve", lambda e, y=y, gbt=gbt, j=j: e.tensor_tensor(out=h[:, j, :], in0=y[:, :], in1=gbt[:, :], op=ALU.mult),
                 reads=[y_r, gbt_r], writes=[h_regs[j]])
    ws = WStream(cx, KC_D, 384)
    subtiles = [("s0", h, h_regs, 0, 512), ("s1", h, h_regs, 512, 512)]
    x1_regs = [Reg() for _ in range(KC_D)]
    xr = Ring(cx, 3, [128, 512], F32, "xc")
    orr_ = Ring(cx, 3, [128, 512], F32, "xo")

    def epi_res(gi, g, stn, tok0, n, bset):
        for mi in range(len(g["widths"])):
            ch = g["ch0"] + mi
            b = bset[mi]
            xt, xt_r, k = xr.next()
            P.dma("sp", k, lambda e, xt=xt, ch=ch: e.dma_start(out=xt[:, :], in_=xT[ch * 128:(ch + 1) * 128, tok0:tok0 + n]), writes=[xt_r])
            o, o_r, k2 = orr_.next()
            P.op("dve", lambda e, o=o, xt=xt, b=b: e.tensor_tensor(out=o[:, :], in0=cx.ps[b][:, :], in1=xt[:, :], op=ALU.add),
                 reads=[cx.ps_regs[b], xt_r], writes=[o_r])
            P.dma("sp", k2, lambda e, o=o, ch=ch: e.dma_start(out=x1T[ch * 128:(ch + 1) * 128, tok0:tok0 + n], in_=o[:, :]),
                  reads=[o_r], writes=[x1_regs[ch]])
    groups = []
    for ch0 in range(0, KC_D, 3):
        nchk = min(3, KC_D - ch0)
        groups.append(dict(kind="fm", segs=[(0, ch0 * 128, nchk * 128)], widths=[128] * nchk, epi=epi_res, ch0=ch0))
    gemm(cx, ws, w_out, KC_D, groups, subtiles)
    load_norm(cx, x1T, fg, fg_r, T, h, h_regs, x_regs=x1_regs)
    gur = Ring(cx, 4, [128, 512], BF16, "gu")

    def epi_gu(gi, g, stn, tok0, n, bset):
        for mi in range(len(g["widths"])):
            ch = g["ch0"] + mi
            dst = gT if ch < NFC else uT
            row = (ch % NFC) * 128
            b = bset[mi]
            o, o_r, k = gur.next()
            P.op("act", lambda e, o=o, b=b: e.activation(out=o[:, :], in_=cx.ps[b][:, :], func=AF.Copy), reads=[cx.ps_regs[b]], writes=[o_r])
            P.dma("sp", k, lambda e, o=o, dst=dst, row=row: e.dma_start(out=dst[row:row + 128, tok0:tok0 + n], in_=o[:, :]), reads=[o_r])
    groups = []
    for ch0 in range(0, 2 * NFC, 3):
        nchk = min(3, 2 * NFC - ch0)
        groups.append(dict(kind="fm", segs=[(0, ch0 * 128, nchk * 128)], widths=[128] * nchk, epi=epi_gu, ch0=ch0))
    gemm(cx, ws, w_up, KC_D, groups, subtiles)
    P.wait_all_dma("sp")
    P.emit()
    return cx.nc


QSZ = (22, 22, 21, 21)
QOFF = (0, 22, 44, 65)


def build_d():
    cx = Cx()
    P = cx.P
    x1T = cx.din("x1T", [D, T])
    gx = cx.din("gx", [DFF, 2 + T], BF16)
    uT = cx.din("uT", [DFF, T], BF16)
    cw_d = cx.din("cwcol", [128, NFC, 3])
    w_dn = cx.din("w_down", [DFF, D])
    x2T = cx.dout("x2T", [D, T])
    cw, cw_r = cx.load(cw_d, [128, NFC, 3])
    acts = [cx.sb([128, 22, T], BF16, "act") for _ in range(2)]
    act_regs = [[Reg() for _ in range(22)] for _ in range(2)]
    ws = WStream(cx, 22, 384, nslots=3)
    gr = Ring(cx, 3, [128, 2 + T], BF16, "g")
    ur = Ring(cx, 3, [128, T], BF16, "u")
    yr = Ring(cx, 3, [128, T], F32, "y")
    xr = Ring(cx, 4, [128, 512], F32, "xc")
    orr_ = Ring(cx, 4, [128, 512], F32, "xo")
    x2_regs = [[Reg(), Reg()] for _ in range(KC_D)]

    def prologue(qi):
        act = acts[qi % 2]
        for kc in range(QSZ[qi]):
            ch = QOFF[qi] + kc
            gt, gt_r, k1 = gr.next()
            ut, ut_r, k2 = ur.next()
            y, y_r, _ = yr.next()
            P.dma("sp", k1, lambda e, gt=gt, ch=ch: e.dma_start(out=gt[:, :], in_=gx[ch * 128:(ch + 1) * 128, :]), writes=[gt_r])
            P.dma("sp", k2, lambda e, ut=ut, ch=ch: e.dma_start(out=ut[:, :], in_=uT[ch * 128:(ch + 1) * 128, :]), writes=[ut_r])
            P.op("pool", lambda e, y=y, gt=gt, ch=ch: e.tensor_scalar(out=y[:, :], in0=gt[:, 2:2 + T], scalar1=cw[:, ch, 2:3], scalar2=None, op0=ALU.mult),
                 reads=[gt_r, cw_r], writes=[y_r])
            P.op("dve", lambda e, y=y, gt=gt, ch=ch: e.scalar_tensor_tensor(out=y[:, :], in0=gt[:, 1:1 + T], scalar=cw[:, ch, 1:2], in1=y[:, :], op0=ALU.mult, op1=ALU.add),
                 reads=[gt_r, cw_r, y_r], writes=[y_r])
            P.op("dve", lambda e, y=y, gt=gt, ch=ch: e.scalar_tensor_tensor(out=y[:, :], in0=gt[:, 0:T], scalar=cw[:, ch, 0:1], in1=y[:, :], op0=ALU.mult, op1=ALU.add),
                 reads=[gt_r, cw_r, y_r], writes=[y_r])
            P.op("act", lambda e, y=y: e.activation(out=y[:, :], in_=y[:, :], func=AF.Silu), reads=[y_r], writes=[y_r])
            P.op("pool", lambda e, y=y, ut=ut, kc=kc, act=act: e.tensor_tensor(out=act[:, kc, :], in0=y[:, :], in1=ut[:, :], op=ALU.mult),
                 reads=[y_r, ut_r], writes=[act_regs[qi % 2][kc]])

    prologue(0)
    for qi in range(4):
        if qi + 1 < 4:
            prologue(qi + 1)
        src = x1T if qi == 0 else x2T

        def epi_res(gi, g, stn, tok0, n, bset, src=src, qi=qi):
            sti = 0 if stn == "s0" else 1
            for mi in range(len(g["widths"])):
                ch = g["ch0"] + mi
                b = bset[mi]
                xt, xt_r, k = xr.next()
                P.dma("sp", k, lambda e, xt=xt, ch=ch: e.dma_start(out=xt[:, :], in_=src[ch * 128:(ch + 1) * 128, tok0:tok0 + n]),
                      reads=([x2_regs[ch][sti]] if qi > 0 else []), writes=[xt_r])
                o, o_r, k2 = orr_.next()
                P.op("dve", lambda e, o=o, xt=xt, b=b: e.tensor_tensor(out=o[:, :], in0=cx.ps[b][:, :], in1=xt[:, :], op=ALU.add),
                     reads=[cx.ps_regs[b], xt_r], writes=[o_r])
                P.dma("sp", k2, lambda e, o=o, ch=ch: e.dma_start(out=x2T[ch * 128:(ch + 1) * 128, tok0:tok0 + n], in_=o[:, :]),
                      reads=[o_r], writes=[x2_regs[ch][sti]])
        groups = []
        for ch0 in range(0, KC_D, 3):
            nchk = min(3, KC_D - ch0)
            groups.append(dict(kind="fm", segs=[(0, ch0 * 128, nchk * 128)], widths=[128] * nchk, epi=epi_res, ch0=ch0))
        a = acts[qi % 2]
        ar = act_regs[qi % 2]
        gemm(cx, ws, w_dn[QOFF[qi] * 128:(QOFF[qi] + QSZ[qi]) * 128, :], QSZ[qi], groups, [("s0", a, ar, 0, 512), ("s1", a, ar, 512, 512)])
    P.wait_all_dma("sp")
    P.emit()
    return cx.nc


def build_e():
    cx = Cx()
    P = cx.P
    x2T = cx.din("xT", [D, T])
    g_d = cx.din("gcol", [128, KC_D])
    W = cx.din("w_in", [D, 10240])
    wT_d = cx.din("sgu_wT", [128, 16, 128])
    tri_d = cx.din("tri", [128, 128])
    bsb_d = cx.din("bsb", [128, 16, 128])
    ngb_d = cx.din("ngb", [128, MIX])
    cvT = cx.dout("cvT", [MIX, T], BF16)
    gbT = cx.dout("gbT", [MIX, T], BF16)
    dT = cx.dout("dT", [MIX, T], BF16)
    cx.eps_t = cx.sb([128, 1], F32, "eps")
    cx.eps_r = Reg()
    P.op("pool", lambda e: e.memset(cx.eps_t[:, :], EPS), writes=[cx.eps_r])
    gcol, gcol_r = cx.load(g_d, [128, KC_D])
    tri, tri_r = cx.load(tri_d, [128, 128])
    bsb, bsb_r = cx.load(bsb_d, [128, 16, 128])
    ngb, ngb_r = cx.load(ngb_d, [128, MIX])
    wtm, wtm_r = cx.load(wT_d, [128, 16, 128], BF16, "wtm", eng="pool")
    for hh in range(16):
        P.op("pool", lambda e, hh=hh: e.tensor_tensor(out=wtm[:, hh, :], in0=wtm[:, hh, :], in1=tri[:, :], op=ALU.mult),
             reads=[wtm_r, tri_r], writes=[wtm_r])
    vgel = [cx.sb([128, MIX], BF16, "vgel") for _ in range(8)]
    vgel_r = [Reg() for _ in range(8)]
    ug = cx.sb([128, 16, T], BF16, "ug")
    ug_r = [Reg() for _ in range(16)]
    h = cx.sb([128, KC_D, T], BF16, "h")
    h_regs = [Reg() for _ in range(KC_D)]
    load_norm(cx, x2T, gcol, gcol_r, T, h, h_regs, rings=(Ring(cx, 2, [128, T], F32, "xs"), Ring(cx, 2, [128, T], BF16, "sq")))
    ws = WStream(cx, KC_D, 256)
    subtiles = [("s0", h, h_regs, 0, 512), ("s1", h, h_regs, 512, 512)]

    def epi_zv(gi, g, stn, tok0, n, bset):
        c0 = g["c0"]
        for tb in range(4):
            tbg = tok0 // 128 + tb
            b = bset[tb]
            P.op("act", lambda e, tbg=tbg, b=b: e.activation(out=vgel[tbg][:, c0:c0 + 256], in_=cx.ps[b][:, 0:256], func=AF.Gelu),
                 reads=[cx.ps_regs[b]], writes=[vgel_r[tbg]])

    def epi_zu(gi, g, stn, tok0, n, bset):
        for mi in range(2):
            ch = g["ch0"] + mi
            b = bset[mi]
            P.op("act", lambda e, ch=ch, b=b: e.activation(out=ug[:, ch, tok0:tok0 + n], in_=cx.ps[b][:, :], func=AF.Gelu),
                 reads=[cx.ps_regs[b]], writes=[ug_r[ch]])
    tmr = Ring(cx, 2, [128, 512], F32, "cvtmp")
    cvr = Ring(cx, 3, [128, 512], BF16, "cvo")

    def epi_cv(gi, g, stn, tok0, n, bset):
        j = g["j"]
        tm, tm_r, _ = tmr.next()
        P.op("act", lambda e: e.activation(out=tm[:, :], in_=cx.ps[bset[0]][:, :], func=AF.Copy), reads=[cx.ps_regs[bset[0]]], writes=[tm_r])
        o, o_r, k = cvr.next()
        P.op("dve", lambda e: e.tensor_tensor(out=o[:, :], in0=cx.ps[bset[1]][:, :], in1=tm[:, :], op=ALU.mult),
             reads=[cx.ps_regs[bset[1]], tm_r], writes=[o_r])
        P.dma("sp", k, lambda e: e.dma_start(out=cvT[j * 128:(j + 1) * 128, tok0:tok0 + n], in_=o[:, :]), reads=[o_r])

    def epi_gb(gi, g, stn, tok0, n, bset):
        for mi in range(2):
            ch = g["ch0"] + mi
            b = bset[mi]
            o, o_r, k = cvr.next()
            P.op("act", lambda e, o=o, b=b: e.activation(out=o[:, :], in_=cx.ps[b][:, :], func=AF.Copy), reads=[cx.ps_regs[b]], writes=[o_r])
            P.dma("sp", k, lambda e, o=o, ch=ch: e.dma_start(out=gbT[ch * 128:(ch + 1) * 128, tok0:tok0 + n], in_=o[:, :]), reads=[o_r])
    groups = []
    for vg in range(8):
        groups.append(dict(kind="tm", segs=[(0, 8192 + vg * 256, 256)], gw=256, epi=epi_zv, c0=vg * 256))
    for ch0 in range(0, 16, 2):
        groups.append(dict(kind="fm", segs=[(0, 6144 + ch0 * 128, 256)], widths=[128] * 2, epi=epi_zu, ch0=ch0))
    for j in range(16):
        groups.append(dict(kind="fm", segs=[(0, j * 128, 128), (128, 4096 + j * 128, 128)], widths=[128] * 2, epi=epi_cv, j=j))
    for ch0 in range(0, 16, 2):
        groups.append(dict(kind="fm", segs=[(0, 2048 + ch0 * 128, 256)], widths=[128] * 2, epi=epi_gb, ch0=ch0))
    gemm(cx, ws, W, KC_D, groups, subtiles)
    def hflat(c):
        return h[:, c:c + 2, :].rearrange("p a b -> p (a b)"), [h_regs[c], h_regs[c + 1]]
    junk, junk_r = hflat(0)
    vns = [hflat(2), hflat(4)]
    dous = [hflat(6), hflat(8)]
    ss = cx.sb([128, 8], F32, "ss")
    ss_r = Reg()
    for tbg in range(8):
        P.op("act", lambda e, tbg=tbg: e.activation(out=junk, in_=vgel[tbg][:, :], func=AF.Square, accum_out=ss[:, tbg:tbg + 1]),
             reads=[vgel_r[tbg]], writes=junk_r + [ss_r])
    P.op("act", lambda e: e.activation(out=ss[:, :], in_=ss[:, :], func=AF.Sqrt, bias=cx.eps_t[:, 0:1], scale=1.0 / MIX),
         reads=[ss_r, cx.eps_r], writes=[ss_r])
    P.op("dve", lambda e: e.reciprocal(out=ss[:, :], in_=ss[:, :]), reads=[ss_r], writes=[ss_r])
    t1r = Ring(cx, 3, [128, 128], F32, "t1")
    dkeys = [cx.uid("dk"), cx.uid("dk")]
    dTv = dT.rearrange("(c p) t -> p c t", p=128)
    for tbg in range(8):
        vn, vn_r = vns[tbg % 2]
        do, do_r = dous[tbg % 2]
        P.op("dve", lambda e, vn=vn, tbg=tbg: e.scalar_tensor_tensor(out=vn, in0=vgel[tbg][:, :], scalar=ss[:, tbg:tbg + 1], in1=ngb[:, :],
                                                                    op0=ALU.mult, op1=ALU.mult),
             reads=[vgel_r[tbg], ss_r, ngb_r], writes=vn_r)
        bset = [0, 1, 2, 3] if tbg % 2 == 0 else [4, 5, 6, 7]
        for hh in range(16):
            b = bset[hh // 4]
            cc = (hh % 4) * 128
            P.op("pe", lambda e, vn=vn, hh=hh, b=b, cc=cc: e.matmul(cx.ps[b][:, cc:cc + 128], lhsT=vn[:, hh * 128:(hh + 1) * 128], rhs=wtm[:, hh, :],
                                                                   start=True, stop=True),
                 reads=vn_r + [wtm_r], writes=[cx.ps_regs[b]])
        for hh in range(16):
            b = bset[hh // 4]
            cc = (hh % 4) * 128
            t1, t1_r, _ = t1r.next()
            P.op("dve", lambda e, t1=t1, hh=hh, b=b, cc=cc: e.tensor_tensor(out=t1[:, :], in0=cx.ps[b][:, cc:cc + 128], in1=bsb[:, hh, :], op=ALU.add),
                 reads=[cx.ps_regs[b], bsb_r], writes=[t1_r])
            P.op("pool", lambda e, t1=t1, hh=hh, do=do, tbg=tbg: e.tensor_tensor(out=do[:, hh * 128:(hh + 1) * 128], in0=t1[:, :],
                                                                               in1=ug[:, hh, tbg * 128:(tbg + 1) * 128], op=ALU.mult),
                 reads=[t1_r, ug_r[hh]], writes=do_r)
        P.dma("sp", dkeys[tbg % 2], lambda e, do=do, tbg=tbg: e.dma_start(out=dTv[:, :, tbg * 128:(tbg + 1) * 128],
                                                                        in_=do.rearrange("p (c t) -> p c t", t=128)), reads=do_r)
    P.wait_all_dma("sp")
    P.emit()
    return cx.nc


def _cat(res, key, axis):
    return np.concatenate([np.asarray(res[c][key]) for c in range(NCORES)], axis=axis)


def _halo(full, c, n):
    if c == 0:
        return np.ascontiguousarray(np.concatenate([np.zeros((full.shape[0], n), full.dtype), full[:, :T]], axis=1))
    return np.ascontiguousarray(full[:, c * T - n:(c + 1) * T])


def launch_bc(layer, xT_l, hcat_l, w_out, fnorm, w_up, extra=None):
    nc = build_bc(layer)
    w_out = np.ascontiguousarray(np.asarray(w_out, np.float32))
    w_up = np.ascontiguousarray(np.asarray(w_up, np.float32))
    fg = colmajor(fnorm)
    maps = []
    for c in range(NCORES):
        m = dict(xT=xT_l[c], hcatT=hcat_l[c], w_out=w_out, fgcol=fg, w_up=w_up)
        if extra is not None:
            m.update(extra[c])
        maps.append(m)
    return run(nc, maps).results


def launch_d(x1_l, res_bc, conv, w_down):
    nc = build_d()
    gfull = _cat(res_bc, "gT", 1)
    cw = np.ascontiguousarray(np.asarray(conv, np.float32).T.reshape(NFC, 128, 3).transpose(1, 0, 2))
    w_down = np.ascontiguousarray(np.asarray(w_down, np.float32))
    maps = [dict(x1T=x1_l[c], gx=_halo(gfull, c, 2), uT=np.asarray(res_bc[c]["uT"]), cwcol=cw, w_down=w_down) for c in range(NCORES)]
    return run(nc, maps).results


def kernel(**inp):
    x = np.asarray(inp["x"], np.float32)[0]
    xT_l = [np.ascontiguousarray(x[c * T:(c + 1) * T].T) for c in range(NCORES)]
    tri = np.triu(np.ones((128, 128), np.float32))
    r1 = launch1(inp).results
    r2 = launch2a(inp, r1).results
    aT = np.concatenate([np.asarray(r2[c]["aT"]) for c in range(NCORES)], axis=0).reshape(MIX, SEQ)
    hcat_l = [np.ascontiguousarray(np.concatenate([aT[:, c * T:(c + 1) * T], np.asarray(r1[c]["pmT"])], axis=0)) for c in range(NCORES)]
    rbc = launch_bc(0, xT_l, hcat_l, inp["l0_w_out"], inp["l0_ffn_norm"], inp["l0_ffn_w_up"])
    x1_l = [np.asarray(rbc[c]["x1T"]) for c in range(NCORES)]
    rd = launch_d(x1_l, rbc, inp["l0_ffn_conv"], inp["l0_ffn_w_down"])
    x2_l = [np.asarray(rd[c]["x2T"]) for c in range(NCORES)]
    nce = build_e()
    w_in1 = np.ascontiguousarray(np.asarray(inp["l1_w_in"], np.float32))
    wT = np.ascontiguousarray(np.asarray(inp["l1_sgu_w"], np.float32).transpose(2, 0, 1))
    bsb = np.ascontiguousarray(np.broadcast_to(np.asarray(inp["l1_sgu_b"], np.float32)[None], (128, 16, 128)))
    ngb = np.ascontiguousarray(np.broadcast_to(np.asarray(inp["l1_sgu_norm"], np.float32)[None], (128, MIX)))
    g1 = colmajor(inp["l1_mix_norm"])
    re_ = run(nce, [dict(xT=x2_l[c], gcol=g1, w_in=w_in1, sgu_wT=wT, tri=tri, bsb=bsb, ngb=ngb) for c in range(NCORES)]).results
    cvfull = _cat(re_, "cvT", 1)
    cw1 = np.ascontiguousarray(np.asarray(inp["l1_conv_w"], np.float32).T.reshape(16, 128, 3).transpose(1, 0, 2))
    hcat1 = [np.ascontiguousarray(np.concatenate([np.zeros((MIX, T), NPBF), np.asarray(re_[c]["dT"])], axis=0)) for c in range(NCORES)]
    extra = [dict(cvx=_halo(cvfull, c, 2), gbT=np.asarray(re_[c]["gbT"]), cwcol=cw1) for c in range(NCORES)]
    rbc1 = launch_bc(1, x2_l, hcat1, inp["l1_w_out"], inp["l1_ffn_norm"], inp["l1_ffn_w_up"], extra)
    x3_l = [np.asarray(rbc1[c]["x1T"]) for c in range(NCORES)]
    rd1 = launch_d(x3_l, rbc1, inp["l1_ffn_conv"], inp["l1_ffn_w_down"])
    out = np.concatenate([np.asarray(rd1[c]["x2T"]).T for c in range(NCORES)], axis=0)
    return np.ascontiguousarray(out[None].astype(np.float32))
```

```python
import numpy as np
import ml_dtypes
import concourse.bass as bass
import concourse.mybir as mybir
from concourse.bass_utils import run_bass_kernel_spmd

F32 = mybir.dt.float32
BF16 = mybir.dt.bfloat16
AF = mybir.ActivationFunctionType
ALU = mybir.AluOpType
NPBF = ml_dtypes.bfloat16

NCORES = 8
SEQ = 8192
D = 4096
T = SEQ // NCORES
KC_D = D // 128
MIX = 2048
NH = 16
DFF = 11008
NFC = DFF // 128
EPS = 1e-6
ENGS = ("pe", "act", "dve", "pool", "sp")


class Reg:
    __slots__ = ("w", "rs")

    def __init__(self):
        self.w = None
        self.rs = {}


class Op:
    __slots__ = ("fn", "deps", "signal", "dma_sem", "val", "ndma")

    def __init__(self, fn, deps, dma_sem=None, ndma=1):
        self.fn = fn
        self.deps = deps
        self.signal = False
        self.dma_sem = dma_sem
        self.val = None
        self.ndma = ndma


class Prog:
    def __init__(self, nc):
        self.nc = nc
        self.ops = {e: [] for e in ENGS}
        self.dma_cnt = {}
        self.dma_sems = {}
        self.eng_sems = {}

    def _collect(self, eng, reads, writes):
        deps = {}

        def add(tok):
            if tok is None:
                return
            c, s = tok
            if c == "pe" and eng == "pe":
                return
            if deps.get(c, -1) < s:
                deps[c] = s
        for r in reads:
            add(r.w)
        for w in writes:
            add(w.w)
            for c, s in w.rs.items():
                add((c, s))
        return deps

    def _commit(self, tok, reads, writes):
        c, s = tok
        for r in reads:
            if r.rs.get(c, -1) < s:
                r.rs[c] = s
        for w in writes:
            w.w = tok
            w.rs = {}

    def op(self, eng, fn, reads=(), writes=()):
        deps = self._collect(eng, reads, writes)
        idx = len(self.ops[eng])
        self.ops[eng].append(Op(fn, deps))
        tok = (eng, idx)
        self._commit(tok, reads, writes)
        return tok

    def dma(self, eng, semkey, fn, reads=(), writes=(), n=1):
        deps = self._collect(eng, reads, writes)
        cnt = self.dma_cnt.get(semkey, 0) + n
        self.dma_cnt[semkey] = cnt
        self.ops[eng].append(Op(fn, deps, dma_sem=semkey, ndma=n))
        tok = (("dma", semkey), cnt)
        self._commit(tok, reads, writes)
        return tok

    def wait_all_dma(self, eng):
        deps = {("dma", k): v for k, v in self.dma_cnt.items()}
        self.ops[eng].append(Op(None, deps))

    def emit(self):
        nc = self.nc
        for e in ENGS:
            for o in self.ops[e]:
                for c, s in o.deps.items():
                    if isinstance(c, str):
                        self.ops[c][s].signal = True
        for e in ENGS:
            n = 0
            for o in self.ops[e]:
                if o.dma_sem is None and o.signal:
                    n += 1
                    o.val = n
        for e in ENGS:
            self.eng_sems[e] = nc.alloc_semaphore(name=f"s_{e}")
        for k in self.dma_cnt:
            self.dma_sems[k] = nc.alloc_semaphore(name=f"d_{k}")
        prog = self

        def run(e, eng):
            waited = {}
            for o in prog.ops[e]:
                for c, s in o.deps.items():
                    if isinstance(c, str):
                        v = prog.ops[c][s].val
                        sem = prog.eng_sems[c]
                    else:
                        v = 16 * s
                        sem = prog.dma_sems[c[1]]
                    if waited.get(c, 0) < v:
                        eng.wait_ge(sem, v)
                        waited[c] = v
                if o.fn is None:
                    continue
                ins = o.fn(eng)
                if o.dma_sem is not None:
                    if not isinstance(ins, (list, tuple)):
                        ins = [ins]
                    assert len(ins) == o.ndma
                    for i in ins:
                        i.then_inc(prog.dma_sems[o.dma_sem], 16)
                elif o.signal:
                    ins.then_inc(prog.eng_sems[e], 1)

        with nc.Block() as block:
            @block.tensor
            def _(eng):
                run("pe", eng)

            @block.scalar
            def _(eng):
                run("act", eng)

            @block.vector
            def _(eng):
                run("dve", eng)

            @block.gpsimd
            def _(eng):
                run("pool", eng)

            @block.sync
            def _(eng):
                run("sp", eng)


class Cx:
    def __init__(self):
        self.nc = bass.Bass("TRN2", target_bir_lowering=False)
        self.P = Prog(self.nc)
        self.n = 0
        self.ps = [self.nc.alloc_psum_tensor(f"ps{i}", [128, 512], F32) for i in range(8)]
        self.ps_regs = [Reg() for _ in range(8)]
        self.pe_defer = []
        self.ones = self.sb([128, 128], BF16, "ones")
        self.ones_r = Reg()
        self.P.op("pool", lambda e: e.memset(self.ones[:, :], 1.0), writes=[self.ones_r])

    def uid(self, s):
        self.n += 1
        return f"{s}{self.n}"

    def sb(self, shape, dt, name="t"):
        return self.nc.alloc_sbuf_tensor(self.uid(name), list(shape), dt)

    def din(self, name, shape, dt=F32):
        return self.nc.dram_tensor(name, list(shape), dt, kind="ExternalInput").ap()

    def dout(self, name, shape, dt=F32):
        return self.nc.dram_tensor(name, list(shape), dt, kind="ExternalOutput").ap()

    def flush_pe(self):
        d = self.pe_defer
        self.pe_defer = []
        for f in d:
            f()

    def load(self, dram_ap, shape, dt=F32, name="c", eng="sp"):
        t = self.sb(shape, dt, name)
        r = Reg()
        sl = tuple(slice(None) for _ in shape)
        self.P.dma(eng, self.uid("ld"), lambda e: e.dma_start(out=t[sl], in_=dram_ap), writes=[r])
        return t, r


class Ring:
    def __init__(self, cx, n, shape, dt, name="r"):
        self.tiles = [cx.sb(shape, dt, name) for _ in range(n)]
        self.regs = [Reg() for _ in range(n)]
        self.keys = [cx.uid(name + "k") for _ in range(n)]
        self.i = 0
        self.n = n

    @classmethod
    def over(cls, cx, tiles, regs, name="r"):
        o = cls.__new__(cls)
        o.tiles = tiles
        o.regs = regs
        o.keys = [cx.uid(name + "k") for _ in tiles]
        o.i = 0
        o.n = len(tiles)
        return o

    def next(self):
        j = self.i % self.n
        self.i += 1
        return self.tiles[j], self.regs[j], self.keys[j]


def load_norm(cx, xT, gcol, gcol_r, Tn, h, h_regs, rings=None, x_regs=None):
    P = cx.P
    nb = (Tn + 511) // 512
    xs = Ring(cx, 3, [128, Tn], F32, "xs") if rings is None else rings[0]
    sq = Ring(cx, 2, [128, Tn], BF16, "sq") if rings is None else rings[1]
    rstd = cx.sb([128, Tn], F32, "rstd")
    rstd_r = Reg()
    for c in range(KC_D):
        t, r, k = xs.next()
        P.dma("sp", k, lambda e, t=t, c=c: e.dma_start(out=t[:, 0:Tn], in_=xT[c * 128:(c + 1) * 128, :]), reads=([x_regs[c]] if x_regs else []), writes=[r])
        s, sr, _ = sq.next()
        P.op("act", lambda e, t=t, s=s: e.activation(out=s[:, 0:Tn], in_=t[:, 0:Tn], func=AF.Square), reads=[r], writes=[sr])
        for j in range(nb):
            n = min(512, Tn - j * 512)
            P.op("pe", lambda e, s=s, j=j, n=n, c=c: e.matmul(cx.ps[j][:, 0:n], lhsT=cx.ones[:, :], rhs=s[:, j * 512:j * 512 + n],
                                                          start=(c == 0), stop=(c == KC_D - 1)),
                 reads=[sr, cx.ones_r], writes=[cx.ps_regs[j]])
    for j in range(nb):
        n = min(512, Tn - j * 512)
        P.op("act", lambda e, j=j, n=n: e.activation(out=rstd[:, j * 512:j * 512 + n], in_=cx.ps[j][:, 0:n], func=AF.Sqrt,
                                                   bias=cx.eps_t[:, 0:1], scale=1.0 / D),
             reads=[cx.ps_regs[j], cx.eps_r], writes=[rstd_r])
    P.op("dve", lambda e: e.reciprocal(out=rstd[:, :], in_=rstd[:, :]), reads=[rstd_r], writes=[rstd_r])
    for c in range(KC_D):
        t, r, k = xs.next()
        P.dma("sp", k, lambda e, t=t, c=c: e.dma_start(out=t[:, 0:Tn], in_=xT[c * 128:(c + 1) * 128, :]), reads=([x_regs[c]] if x_regs else []), writes=[r])
        P.op("dve", lambda e, t=t, c=c: e.scalar_tensor_tensor(out=h[:, c, :], in0=t[:, 0:Tn], scalar=gcol[:, c:c + 1], in1=rstd[:, :],
                                                              op0=ALU.mult, op1=ALU.mult),
             reads=[r, gcol_r, rstd_r], writes=[h_regs[c]])
    return xs, sq


class WStream:
    def __init__(self, cx, kcmax, gw, nslots=2):
        self.cx = cx
        self.gw = gw
        self.kcmax = kcmax
        self.nparts = (kcmax + 7) // 8
        self.slots = [cx.sb([128, kcmax, gw], BF16, "ws") for _ in range(nslots)]
        self.regs = [[Reg() for _ in range(self.nparts)] for _ in range(nslots)]
        self.keys = [[cx.uid("wk") for _ in range(self.nparts)] for _ in range(nslots)]
        self.i = 0
        self.nslots = nslots

    def load(self, W, kc_n, segs):
        cx = self.cx
        s = self.i % self.nslots
        self.i += 1
        Wv = W.rearrange("(kc p) n -> p kc n", p=128)
        for q in range((kc_n + 7) // 8):
            k0, k1 = q * 8, min(kc_n, q * 8 + 8)

            def fn(e, s=s, k0=k0, k1=k1):
                return [e.dma_start(out=self.slots[s][:, k0:k1, d0:d0 + w], in_=Wv[:, k0:k1, c0:c0 + w]) for (d0, c0, w) in segs]
            cx.P.dma("pool", self.keys[s][q], fn, writes=[self.regs[s][q]], n=len(segs))
        return s


def gemm(cx, ws, W, kc_n, groups, subtiles):
    P = cx.P
    unit = cx.unit if hasattr(cx, "unit") else 0
    for gi, g in enumerate(groups):
        s = ws.load(W, kc_n, g["segs"])
        slot = ws.slots[s]
        for (stn, ht, hr, tok0, n) in subtiles:
            if stn == "halo" and not g.get("halo"):
                continue
            bset = [0, 1, 2, 3] if unit % 2 == 0 else [4, 5, 6, 7]
            unit += 1
            if g["kind"] == "fm":
                widths = g["widths"]
                for kc in range(kc_n):
                    for mi, wd in enumerate(widths):
                        b = bset[mi]
                        P.op("pe", lambda e, b=b, kc=kc, mi=mi, wd=wd, ht=ht, tok0=tok0, n=n, slot=slot: e.matmul(
                            cx.ps[b][0:wd, 0:n], lhsT=slot[:, kc, mi * 128:mi * 128 + wd], rhs=ht[:, kc, tok0:tok0 + n],
                            start=(kc == 0), stop=(kc == kc_n - 1)),
                            reads=[ws.regs[s][kc // 8], hr[kc]], writes=[cx.ps_regs[b]])
                nb = len(widths)
            else:
                ntb = n // 128
                gwid = g["gw"]
                for kc in range(kc_n):
                    for tb in range(ntb):
                        b = bset[tb]
                        P.op("pe", lambda e, b=b, kc=kc, tb=tb, ht=ht, tok0=tok0, slot=slot, gwid=gwid: e.matmul(
                            cx.ps[b][:, 0:gwid], lhsT=ht[:, kc, tok0 + tb * 128:tok0 + (tb + 1) * 128], rhs=slot[:, kc, 0:gwid],
                            start=(kc == 0), stop=(kc == kc_n - 1)),
                            reads=[ws.regs[s][kc // 8], hr[kc]], writes=[cx.ps_regs[b]])
                nb = ntb
            cx.flush_pe()
            g["epi"](gi, g, stn, tok0, n, bset)
            if getattr(cx, "unit_hook", None):
                cx.unit_hook()
    cx.flush_pe()
    cx.unit = unit


def build_l1():
    cx = Cx()
    P = cx.P
    nc = cx.nc
    xT = cx.din("xT", [D, T])
    xh = cx.din("xh", [D, 16])
    W = cx.din("w_in", [D, 8208])
    gcol_d = cx.din("gcol", [128, KC_D])
    qk_d = cx.din("qkg", [128, 2])
    bf_d = cx.din("bf", [16, 1])
    pw_d = cx.din("pool_w", [4, 512, 512])
    psc_d = cx.din("pscol", [128, 16])
    icn_d = cx.din("invcnt", [128, 4, 16])
    qT = cx.dout("qT", [NH, 128, T], BF16)
    kT = cx.dout("kT", [NH, 128, T], BF16)
    Vo = cx.dout("V", [T, MIX], BF16)
    lf = cx.dout("logf", [NH, T])
    pmT = cx.dout("pmT", [MIX, T], BF16)

    cx.eps_t = cx.sb([128, 1], F32, "eps")
    cx.eps_r = Reg()
    P.op("pool", lambda e: e.memset(cx.eps_t[:, :], EPS), writes=[cx.eps_r])
    gcol, gcol_r = cx.load(gcol_d, [128, KC_D])
    qkg, qkg_r = cx.load(qk_d, [128, 2])
    bft, bft_r = cx.load(bf_d, [16, 1])
    psc, psc_r = cx.load(psc_d, [128, 16])
    icn, icn_r = cx.load(icn_d, [128, 4, 16])
    P.op("dve", lambda e: e.tensor_scalar(out=qkg[:, 0:1], in0=qkg[:, 0:1], scalar1=float(128 ** -0.5), scalar2=None, op0=ALU.mult),
         reads=[qkg_r], writes=[qkg_r])
    P.op("dve", lambda e: e.tensor_scalar(out=bft[:, :], in0=bft[:, :], scalar1=-1.0, scalar2=None, op0=ALU.mult),
         reads=[bft_r], writes=[bft_r])

    h = cx.sb([128, KC_D, T], BF16, "h")
    h_regs = [Reg() for _ in range(KC_D)]
    hh = cx.sb([128, KC_D, 16], BF16, "hh")
    hh_regs = [Reg() for _ in range(KC_D)]
    zb = [cx.sb([128, 16 + T], F32, "zb") for _ in range(2)]
    zb_r = [Reg() for _ in range(2)]
    pa = [cx.sb([128, 16 + T], F32, "pa") for _ in range(1)]
    pa_r = [Reg() for _ in range(1)]
    pbb = [cx.sb([128, 16 + T], F32, "pb") for _ in range(1)]
    pb_r = [Reg() for _ in range(1)]
    sqring = Ring(cx, 2, [128, T], BF16, "sq")
    xsring = Ring.over(cx, [zb[0], zb[1], pa[0]], [zb_r[0], zb_r[1], pa_r[0]], "xs")
    load_norm(cx, xh, gcol, gcol_r, 16, hh, hh_regs, rings=(xsring, sqring))
    load_norm(cx, xT, gcol, gcol_r, T, h, h_regs, rings=(xsring, sqring))

    ws = WStream(cx, KC_D, 384)
    subtiles = [("halo", hh, hh_regs, 0, 16), ("s0", h, h_regs, 0, 512), ("s1", h, h_regs, 512, 512)]

    pooled = cx.sb([128, 16, T], BF16, "pooled")
    pooled_regs = [Reg() for _ in range(16)]
    pcnt = [0]

    def epi_zp(gi, g, stn, tok0, n, bset):
        grp = g["pg"]
        wlen = (2, 4, 8, 16)[grp]
        for mi in range(2):
            off = 0 if stn == "halo" else 16 + tok0
            P.op("act", lambda e, mi=mi, off=off, n=n, b=bset[mi]: e.activation(out=zb[mi][:, off:off + n], in_=cx.ps[b][:, 0:n], func=AF.Copy),
                 reads=[cx.ps_regs[bset[mi]]], writes=[zb_r[mi]])
        if stn != "s1":
            return
        L = 16 + T
        for mi in range(2):
            ch = g["ch0"] + mi
            j = 0
            src, src_r = zb[mi], zb_r[mi]
            bufs = [(pa[j], pa_r[j]), (pbb[j], pb_r[j])]
            sh = 1
            bi = 0
            while sh < wlen:
                dst, dst_r = bufs[bi]
                eng = "dve"
                P.op(eng, lambda e, dst=dst, src=src, sh=sh: e.tensor_tensor(out=dst[:, sh:L], in0=src[:, sh:L], in1=src[:, 0:L - sh], op=ALU.add),
                     reads=[src_r], writes=[dst_r])
                src, src_r = dst, dst_r
                bi ^= 1
                sh *= 2
            P.op("dve", lambda e, src=src, mi=mi, ch=ch, wlen=wlen: e.scalar_tensor_tensor(
                out=pooled[:, ch, :], in0=src[:, 16:L], scalar=1.0 / wlen, in1=zb[mi][:, 16:L], op0=ALU.mult, op1=ALU.subtract),
                reads=[src_r, zb_r[mi]], writes=[pooled_regs[ch]])
            dst, dst_r = bufs[bi]
            P.op("dve", lambda e, dst=dst, src=src, grp=grp: e.tensor_tensor(out=dst[:, 0:16], in0=src[:, 16:32], in1=icn[:, grp, :], op=ALU.mult),
                 reads=[src_r, icn_r], writes=[dst_r])
            P.op("dve", lambda e, dst=dst, mi=mi, ch=ch: e.tensor_tensor(out=pooled[:, ch, 0:16], in0=dst[:, 0:16], in1=zb[mi][:, 16:32], op=ALU.subtract),
                 reads=[dst_r, zb_r[mi]], writes=[pooled_regs[ch]])

    lfr = Ring(cx, 2, [16, 512], F32, "lf")

    def epi_f(gi, g, stn, tok0, n, bset):
        t, r, k = lfr.next()
        b = bset[0]
        P.op("act", lambda e: e.activation(out=t[:, :], in_=cx.ps[b][0:16, 0:n], func=AF.Exp, bias=bft[:, 0:1], scale=-1.0),
             reads=[cx.ps_regs[b], bft_r], writes=[r])
        P.op("act", lambda e: e.activation(out=t[:, :], in_=t[:, :], func=AF.Ln, bias=1.0, scale=1.0), reads=[r], writes=[r])
        P.op("dve", lambda e: e.tensor_scalar(out=t[:, :], in0=t[:, :], scalar1=-1.0, scalar2=None, op0=ALU.mult), reads=[r], writes=[r])
        P.dma("sp", k, lambda e: e.dma_start(out=lf[:, tok0:tok0 + n], in_=t[:, :]), reads=[r])

    vr = Ring(cx, 4, [128, 256], BF16, "vo")

    def epi_v(gi, g, stn, tok0, n, bset):
        c0 = g["c0"]
        for tb in range(4):
            t, r, k = vr.next()
            b = bset[tb]
            P.op("act", lambda e, t=t, b=b: e.activation(out=t[:, :], in_=cx.ps[b][:, 0:256], func=AF.Copy), reads=[cx.ps_regs[b]], writes=[r])
            r0 = tok0 + tb * 128
            P.dma("sp", k, lambda e, t=t, r0=r0: e.dma_start(out=Vo[r0:r0 + 128, c0:c0 + 256], in_=t[:, :]), reads=[r])

    sqr = Ring(cx, 3, [128, 512], BF16, "qsq")
    rtr = Ring(cx, 3, [128, 512], F32, "qrt")
    qor = Ring(cx, 3, [128, 512], BF16, "qo")

    def epi_qk(gi, g, stn, tok0, n, bset):
        which = g["which"]
        dst = qT if which == 0 else kT
        nbk = bset[3]
        for mi in range(len(g["widths"])):
            hd = g["h0"] + mi
            b = bset[mi]
            s, sr, _ = sqr.next()
            P.op("act", lambda e, s=s, b=b: e.activation(out=s[:, :], in_=cx.ps[b][:, :], func=AF.Square), reads=[cx.ps_regs[b]], writes=[sr])

            def later(s=s, sr=sr, b=b, hd=hd, nbk=nbk):
                P.op("pe", lambda e: e.matmul(cx.ps[nbk][:, :], lhsT=cx.ones[:, :], rhs=s[:, :], start=True, stop=True),
                     reads=[sr, cx.ones_r], writes=[cx.ps_regs[nbk]])
                rt, rr, _ = rtr.next()
                P.op("act", lambda e: e.activation(out=rt[:, :], in_=cx.ps[nbk][:, :], func=AF.Sqrt, bias=cx.eps_t[:, 0:1], scale=1.0 / 128),
                     reads=[cx.ps_regs[nbk], cx.eps_r], writes=[rr])
                P.op("dve", lambda e: e.reciprocal(out=rt[:, :], in_=rt[:, :]), reads=[rr], writes=[rr])
                o, orr, k = qor.next()
                P.op("dve", lambda e: e.scalar_tensor_tensor(out=o[:, :], in0=cx.ps[b][:, :], scalar=qkg[:, which:which + 1], in1=rt[:, :],
                                                             op0=ALU.mult, op1=ALU.mult),
                     reads=[cx.ps_regs[b], qkg_r, rr], writes=[orr])
                P.dma("sp", k, lambda e: e.dma_start(out=dst[hd, :, tok0:tok0 + n], in_=o[:, :]), reads=[orr])
            cx.pe_defer.append(later)

    groups = []
    for cp in range(8):
        c0 = 6160 + cp * 256
        groups.append(dict(kind="fm", segs=[(0, c0, 256)], widths=[128] * 2, epi=epi_zp, halo=True, pg=cp // 2, ch0=cp * 2))
    groups.append(dict(kind="fm", segs=[(0, 6144, 16)], widths=[16], epi=epi_f))
    for vg in range(8):
        groups.append(dict(kind="tm", segs=[(0, 4096 + vg * 256, 256)], gw=256, epi=epi_v, c0=vg * 256))
    for which in range(2):
        for h0 in range(0, 16, 3):
            nh = min(3, 16 - h0)
            c0 = which * 2048 + h0 * 128
            groups.append(dict(kind="fm", segs=[(0, c0, nh * 128)], widths=[128] * nh, epi=epi_qk, which=which, h0=h0))
    gemm(cx, ws, W, KC_D, groups, subtiles)

    pmr = Ring(cx, 3, [128, 512], BF16, "pmo")
    for pg in range(4):
        def epi_pm(gi, g, stn, tok0, n, bset, pg=pg):
            for mi in range(2):
                ch = pg * 4 + g["m0"] + mi
                o, orr, k = pmr.next()
                b = bset[mi]
                P.op("act", lambda e, o=o, b=b, ch=ch: e.activation(out=o[:, :], in_=cx.ps[b][:, :], func=AF.Copy, scale=psc[:, ch:ch + 1]),
                     reads=[cx.ps_regs[b], psc_r], writes=[orr])
                P.dma("sp", k, lambda e, o=o, ch=ch: e.dma_start(out=pmT[ch * 128:(ch + 1) * 128, tok0:tok0 + n], in_=o[:, :]), reads=[orr])
        pview = pooled[:, pg * 4:(pg + 1) * 4, :]
        gemm(cx, ws, pw_d[pg], 4, [dict(kind="fm", segs=[(0, m0 * 128, 256)], widths=[128] * 2, epi=epi_pm, m0=m0) for m0 in (0, 2)],
             [("s0", pview, pooled_regs[pg * 4:(pg + 1) * 4], 0, 512), ("s1", pview, pooled_regs[pg * 4:(pg + 1) * 4], 512, 512)])
    P.wait_all_dma("sp")
    P.emit()
    return nc


def colmajor(v, n=128):
    return np.ascontiguousarray(np.asarray(v, np.float32).reshape(-1, n).T)


def run(nc, in_maps, trace=False):
    res = run_bass_kernel_spmd(nc, in_maps, core_ids=list(range(NCORES)), trace=trace)
    return res


def launch1(inp, trace=False):
    x = np.asarray(inp["x"], np.float32)[0]
    nc = build_l1()
    w_in = np.ascontiguousarray(np.asarray(inp["l0_w_in"], np.float32))
    gcol = colmajor(inp["l0_mix_norm"])
    qkg = np.stack([np.asarray(inp["l0_q_gain"], np.float32), np.asarray(inp["l0_k_gain"], np.float32)], 1)
    bf = np.asarray(inp["l0_b_f"], np.float32).reshape(16, 1)
    pw = np.ascontiguousarray(np.asarray(inp["l0_pool_w"], np.float32))
    psc = colmajor(inp["l0_pool_scale"])
    maps = []
    for c in range(NCORES):
        xs = x[c * T:(c + 1) * T]
        xh = x[c * T - 16:c * T] if c > 0 else np.zeros((16, D), np.float32)
        icn = np.zeros((128, 4, 16), np.float32)
        for g, w in enumerate((2, 4, 8, 16)):
            pos = np.arange(16) + 1 + c * T
            icn[:, g, :] = 1.0 / np.minimum(pos, w)
        maps.append(dict(xT=np.ascontiguousarray(xs.T), xh=np.ascontiguousarray(xh.T), w_in=w_in, gcol=gcol,
                         qkg=np.ascontiguousarray(qkg), bf=bf, pool_w=pw, pscol=psc, invcnt=icn))
    return run(nc, maps, trace)


HPC = NH // NCORES
NQT = SEQ // 512
NKB = SEQ // 128


def build_l2a():
    cx = Cx()
    P = cx.P
    qT = cx.din("qT", [HPC, 128, SEQ], BF16)
    kT = cx.din("kT", [HPC, 128, SEQ], BF16)
    Vd = cx.din("V", [HPC, 128, NKB, 128], BF16)
    lf6 = cx.din("lf6", [HPC, 6, SEQ])
    coef_d = cx.din("coef", [6, 8])
    tri_d = cx.din("tri", [128, 128])
    aT = cx.dout("aT", [HPC, 128, SEQ], BF16)
    coef, coef_r = cx.load(coef_d, [6, 8])
    tri, tri_r = cx.load(tri_d, [128, 128])
    qs, ks, vs, qa, ka = [], [], [], [], []
    for hh in range(HPC):
        q_t, q_r = cx.load(qT[hh], [128, SEQ], BF16, "q")
        k_t, k_r = cx.load(kT[hh], [128, SEQ], BF16, "k")
        v_t, v_r = cx.load(Vd[hh], [128, NKB, 128], BF16, "v")
        qs.append((q_t, q_r)); ks.append((k_t, k_r)); vs.append((v_t, v_r))
    SG = 2048
    lft = cx.sb([6, SG], F32, "lft"); lft_r = Reg()
    c6 = cx.sb([6, SG], F32, "c6"); c6_r = Reg()
    r1 = cx.sb([6, SG], F32, "r1"); r1_r = Reg()
    hi = cx.sb([6, SG], BF16, "hi"); hi_r = Reg()
    mid = cx.sb([6, SG], BF16, "mid"); mid_r = Reg()
    lo = cx.sb([6, SG], BF16, "lo"); lo_r = Reg()
    tmp = cx.sb([6, SG], F32, "tmpa"); tmp_r = Reg()
    carry = cx.sb([6, 1], F32, "carry"); carry_r = Reg()
    qa_t = cx.sb([6, SEQ], BF16, "qa"); qa_r = Reg()
    ka_t = cx.sb([6, SEQ], BF16, "ka"); ka_r = Reg()
    aug_done = [False] * HPC

    def build_aug(hh):
        for sg in range(SEQ // SG):
            t0 = sg * SG
            P.dma("sp", "lftk", lambda e, hh=hh, t0=t0: e.dma_start(out=lft[:, :], in_=lf6[hh, :, t0:t0 + SG]), writes=[lft_r])
            P.op("pool", lambda e: e.memset(tmp[:, :], 1.0), writes=[tmp_r])
            if sg == 0:
                P.op("dve", lambda e: e.tensor_tensor_scan(out=c6[:, :], data0=tmp[:, :], data1=lft[:, :], initial=0.0, op0=ALU.mult, op1=ALU.add),
                     reads=[lft_r, tmp_r], writes=[c6_r])
            else:
                P.op("dve", lambda e: e.tensor_tensor_scan(out=c6[:, :], data0=tmp[:, :], data1=lft[:, :], initial=carry[:, 0:1], op0=ALU.mult, op1=ALU.add),
                     reads=[lft_r, tmp_r, carry_r], writes=[c6_r])
            P.op("dve", lambda e: e.tensor_copy(out=carry[:, :], in_=c6[:, SG - 1:SG]), reads=[c6_r], writes=[carry_r])
            P.op("dve", lambda e: e.tensor_copy(out=hi[:, :], in_=c6[:, :]), reads=[c6_r], writes=[hi_r])
            P.op("dve", lambda e: e.tensor_tensor(out=r1[:, :], in0=c6[:, :], in1=hi[:, :], op=ALU.subtract), reads=[c6_r, hi_r], writes=[r1_r])
            P.op("dve", lambda e: e.tensor_copy(out=mid[:, :], in_=r1[:, :]), reads=[r1_r], writes=[mid_r])
            P.op("dve", lambda e: e.tensor_tensor(out=r1[:, :], in0=r1[:, :], in1=mid[:, :], op=ALU.subtract), reads=[r1_r, mid_r], writes=[r1_r])
            P.op("dve", lambda e: e.tensor_copy(out=lo[:, :], in_=r1[:, :]), reads=[r1_r], writes=[lo_r])
            for (dst, dst_r, o) in ((qa_t, qa_r, 0), (ka_t, ka_r, 4)):
                P.op("dve", lambda e, o=o: e.tensor_scalar(out=tmp[:, :], in0=hi[:, :], scalar1=coef[:, o:o + 1], scalar2=coef[:, o + 3:o + 4], op0=ALU.mult, op1=ALU.add),
                     reads=[hi_r, coef_r], writes=[tmp_r])
                P.op("dve", lambda e, o=o: e.scalar_tensor_tensor(out=tmp[:, :], in0=mid[:, :], scalar=coef[:, o + 1:o + 2], in1=tmp[:, :], op0=ALU.mult, op1=ALU.add),
                     reads=[mid_r, coef_r, tmp_r], writes=[tmp_r])
                P.op("dve", lambda e, o=o, dst=dst, t0=t0: e.scalar_tensor_tensor(out=dst[:, t0:t0 + SG], in0=lo[:, :], scalar=coef[:, o + 2:o + 3], in1=tmp[:, :], op0=ALU.mult, op1=ALU.add),
                     reads=[lo_r, coef_r, tmp_r], writes=[dst_r])
    LA = 2
    pr = Ring(cx, LA + 2, [128, 512], BF16, "pT")
    rdr = Ring(cx, 2, [128, 512], F32, "rden")
    aor = Ring(cx, 2, [128, 512], BF16, "ao")
    sbanks = [0, 1, 6, 7]
    blocks = []
    unit = 0
    for hh in range(HPC):
        for qt in range(NQT):
            ob = 2 + (unit % 2) * 2
            unit += 1
            nkb = 4 * qt + 4
            for kb in range(nkb):
                blocks.append(dict(hh=hh, qt=qt, kb=kb, nkb=nkb, ob=ob, db=ob + 1))

    def stage_a(i):
        bl = blocks[i]
        hh, qt, kb = bl["hh"], bl["qt"], bl["kb"]
        if not aug_done[hh]:
            build_aug(hh)
            aug_done[hh] = True
        q_t, q_r = qs[hh]; k_t, k_r = ks[hh]
        q0 = qt * 512
        j = kb - 4 * qt
        c0 = 128 * j if j > 0 else 0
        n = 512 - c0
        sbank = sbanks[i % len(sbanks)]
        bl.update(c0=c0, n=n, j=j)
        P.op("pe", lambda e: e.matmul(cx.ps[sbank][:, 0:n], lhsT=k_t[:, kb * 128:(kb + 1) * 128], rhs=q_t[:, q0 + c0:q0 + 512], start=True, stop=False),
             reads=[k_r, q_r], writes=[cx.ps_regs[sbank]])
        P.op("pe", lambda e: e.matmul(cx.ps[sbank][:, 0:n], lhsT=ka_t[:, kb * 128:(kb + 1) * 128], rhs=qa_t[:, q0 + c0:q0 + 512], start=False, stop=True),
             reads=[ka_r, qa_r], writes=[cx.ps_regs[sbank]])
        pt, pt_r, _ = pr.next()
        bl.update(pt=pt, pt_r=pt_r)
        P.op("act", lambda e: e.activation(out=pt[:, 0:n], in_=cx.ps[sbank][:, 0:n], func=AF.Exp), reads=[cx.ps_regs[sbank]], writes=[pt_r])
        if j >= 0:
            P.op("pool", lambda e: e.tensor_tensor(out=pt[:, 0:128], in0=pt[:, 0:128], in1=tri[:, :], op=ALU.mult), reads=[pt_r, tri_r], writes=[pt_r])

    def stage_b(i):
        bl = blocks[i]
        hh, qt, kb, nkb, ob, db = bl["hh"], bl["qt"], bl["kb"], bl["nkb"], bl["ob"], bl["db"]
        c0, n, pt, pt_r = bl["c0"], bl["n"], bl["pt"], bl["pt_r"]
        v_t, v_r = vs[hh]
        q0 = qt * 512
        P.op("pe", lambda e: e.matmul(cx.ps[ob][:, c0:512], lhsT=v_t[:, kb, :], rhs=pt[:, 0:n], start=(kb == 0), stop=(kb == nkb - 1)),
             reads=[v_r, pt_r], writes=[cx.ps_regs[ob]])
        P.op("pe", lambda e: e.matmul(cx.ps[db][:, c0:512], lhsT=cx.ones[:, :], rhs=pt[:, 0:n], start=(kb == 0), stop=(kb == nkb - 1)),
             reads=[cx.ones_r, pt_r], writes=[cx.ps_regs[db]])
        if kb == nkb - 1:
            rd, rd_r, _ = rdr.next()
            P.op("dve", lambda e: e.reciprocal(out=rd[:, :], in_=cx.ps[db][:, :]), reads=[cx.ps_regs[db]], writes=[rd_r])
            ao, ao_r, k = aor.next()
            P.op("dve", lambda e: e.tensor_tensor(out=ao[:, :], in0=cx.ps[ob][:, :], in1=rd[:, :], op=ALU.mult), reads=[cx.ps_regs[ob], rd_r], writes=[ao_r])
            P.dma("sp", k, lambda e: e.dma_start(out=aT[hh, :, q0:q0 + 512], in_=ao[:, :]), reads=[ao_r])
    for i in range(len(blocks) + LA):
        if i < len(blocks):
            stage_a(i)
        if i - LA >= 0:
            stage_b(i - LA)
    P.wait_all_dma("sp")
    P.emit()
    return cx.nc


def launch2a(inp, l1res, trace=False):
    nc = build_l2a()
    qT = np.concatenate([np.asarray(l1res[c]["qT"]) for c in range(NCORES)], axis=2)
    kT = np.concatenate([np.asarray(l1res[c]["kT"]) for c in range(NCORES)], axis=2)
    V = np.concatenate([np.asarray(l1res[c]["V"]) for c in range(NCORES)], axis=0)
    lf = np.concatenate([np.asarray(l1res[c]["logf"]) for c in range(NCORES)], axis=1)
    coef = np.zeros((6, 8), np.float32)
    coef[0, 0] = 1; coef[1, 1] = 1; coef[2, 2] = 1; coef[3:6, 3] = 1
    coef[3, 4] = -1; coef[4, 5] = -1; coef[5, 6] = -1; coef[0:3, 7] = 1
    tri = np.triu(np.ones((128, 128), np.float32))
    maps = []
    for c in range(NCORES):
        hs = slice(c * HPC, (c + 1) * HPC)
        Vh = V.reshape(NKB, 128, NH, 128)[:, :, hs].transpose(2, 1, 0, 3)
        maps.append(dict(qT=np.ascontiguousarray(qT[hs]), kT=np.ascontiguousarray(kT[hs]), V=np.ascontiguousarray(Vh),
                         lf6=np.ascontiguousarray(np.repeat(lf[hs][:, None, :], 6, axis=1)), coef=coef, tri=tri))
    return run(nc, maps, trace)


def build_bc(layer):
    cx = Cx()
    P = cx.P
    xT = cx.din("xT", [D, T])
    hc = cx.din("hcatT", [D, T], BF16)
    w_out = cx.din("w_out", [D, D])
    fg_d = cx.din("fgcol", [128, KC_D])
    w_up = cx.din("w_up", [D, 2 * DFF])
    x1T = cx.dout("x1T", [D, T])
    gT = cx.dout("gT", [DFF, T], BF16)
    uT = cx.dout("uT", [DFF, T], BF16)
    cx.eps_t = cx.sb([128, 1], F32, "eps")
    cx.eps_r = Reg()
    P.op("pool", lambda e: e.memset(cx.eps_t[:, :], EPS), writes=[cx.eps_r])
    fg, fg_r = cx.load(fg_d, [128, KC_D])
    h = cx.sb([128, KC_D, T], BF16, "h")
    h_regs = [Reg() for _ in range(KC_D)]
    hv = hc.rearrange("(c p) t -> p c t", p=128)
    if layer == 0:
        for q in range(4):
            P.dma("sp", cx.uid("hl"), lambda e, q=q: e.dma_start(out=h[:, q * 8:(q + 1) * 8, :], in_=hv[:, q * 8:(q + 1) * 8, :]),
                  writes=h_regs[q * 8:(q + 1) * 8])
    else:
        for q in range(2, 4):
            P.dma("sp", cx.uid("hl"), lambda e, q=q: e.dma_start(out=h[:, q * 8:(q + 1) * 8, :], in_=hv[:, q * 8:(q + 1) * 8, :]),
                  writes=h_regs[q * 8:(q + 1) * 8])
        cv_d = cx.din("cvx", [MIX, 2 + T], BF16)
        gb_d = cx.din("gbT", [MIX, T], BF16)
        cw_d = cx.din("cwcol", [128, 16, 3])
        cw, cw_r = cx.load(cw_d, [128, 16, 3])
        cvr = Ring(cx, 2, [128, 2 + T], BF16, "cv")
        gbr = Ring(cx, 2, [128, T], BF16, "gb")
        yr = Ring(cx, 2, [128, T], F32, "cy")
        for j in range(16):
            cvt, cvt_r, k1 = cvr.next()
            gbt, gbt_r, k2 = gbr.next()
            y, y_r, _ = yr.next()
            P.dma("sp", k1, lambda e, cvt=cvt, j=j: e.dma_start(out=cvt[:, :], in_=cv_d[j * 128:(j + 1) * 128, :]), writes=[cvt_r])
            P.dma("sp", k2, lambda e, gbt=gbt, j=j: e.dma_start(out=gbt[:, :], in_=gb_d[j * 128:(j + 1) * 128, :]), writes=[gbt_r])
            P.op("dve", lambda e, y=y, cvt=cvt, j=j: e.tensor_scalar(out=y[:, :], in0=cvt[:, 2:2 + T], scalar1=cw[:, j, 2:3], scalar2=None, op0=ALU.mult),
                 reads=[cvt_r, cw_r], writes=[y_r])
            P.op("dve", lambda e, y=y, cvt=cvt, j=j: e.scalar_tensor_tensor(out=y[:, :], in0=cvt[:, 1:1 + T], scalar=cw[:, j, 1:2], in1=y[:, :], op0=ALU.mult, op1=ALU.add),
                 reads=[cvt_r, cw_r, y_r], writes=[y_r])
            P.op("dve", lambda e, y=y, cvt=cvt, j=j: e.scalar_tensor_tensor(out=y[:, :], in0=cvt[:, 0:T], scalar=cw[:, j, 0:1], in1=y[:, :], op0=ALU.mult, op1=ALU.add),
                 reads=[cvt_r, cw_r, y_r], writes=[y_r])
            P.op("dve", lambda e, y=y, gbt=gbt, j=j: e.tensor_tensor(out=h[:, j, :], in0=y[:, :], in1=gbt[:, :], op=ALU.mult),
                 reads=[y_r, gbt_r], writes=[h_regs[j]])
    ws = WStream(cx, KC_D, 384)
    subtiles = [("s0", h, h_regs, 0, 512), ("s1", h, h_regs, 512, 512)]
    x1_regs = [Reg() for _ in range(KC_D)]
    xr = Ring(cx, 3, [128, 512], F32, "xc")
    orr_ = Ring(cx, 3, [128, 512], F32, "xo")

    def epi_res(gi, g, stn, tok0, n, bset):
        for mi in range(len(g["widths"])):
            ch = g["ch0"] + mi
            b = bset[mi]
            xt, xt_r, k = xr.next()
            P.dma("sp", k, lambda e, xt=xt, ch=ch: e.dma_start(out=xt[:, :], in_=xT[ch * 128:(ch + 1) * 128, tok0:tok0 + n]), writes=[xt_r])
            o, o_r, k2 = orr_.next()
            P.op("dve", lambda e, o=o, xt=xt, b=b: e.tensor_tensor(out=o[:, :], in0=cx.ps[b][:, :], in1=xt[:, :], op=ALU.add),
                 reads=[cx.ps_regs[b], xt_r], writes=[o_r])
            P.dma("sp", k2, lambda e, o=o, ch=ch: e.dma_start(out=x1T[ch * 128:(ch + 1) * 128, tok0:tok0 + n], in_=o[:, :]),
                  reads=[o_r], writes=[x1_regs[ch]])
    groups = []
    for ch0 in range(0, KC_D, 3):
        nchk = min(3, KC_D - ch0)
        groups.append(dict(kind="fm", segs=[(0, ch0 * 128, nchk * 128)], widths=[128] * nchk, epi=epi_res, ch0=ch0))
    gemm(cx, ws, w_out, KC_D, groups, subtiles)
    load_norm(cx, x1T, fg, fg_r, T, h, h_regs, x_regs=x1_regs)
    gur = Ring(cx, 4, [128, 512], BF16, "gu")

    def epi_gu(gi, g, stn, tok0, n, bset):
        for mi in range(len(g["widths"])):
            ch = g["ch0"] + mi
            dst = gT if ch < NFC else uT
            row = (ch % NFC) * 128
            b = bset[mi]
            o, o_r, k = gur.next()
            P.op("act", lambda e, o=o, b=b: e.activation(out=o[:, :], in_=cx.ps[b][:, :], func=AF.Copy), reads=[cx.ps_regs[b]], writes=[o_r])
            P.dma("sp", k, lambda e, o=o, dst=dst, row=row: e.dma_start(out=dst[row:row + 128, tok0:tok0 + n], in_=o[:, :]), reads=[o_r])
    groups = []
    for ch0 in range(0, 2 * NFC, 3):
        nchk = min(3, 2 * NFC - ch0)
        groups.append(dict(kind="fm", segs=[(0, ch0 * 128, nchk * 128)], widths=[128] * nchk, epi=epi_gu, ch0=ch0))
    gemm(cx, ws, w_up, KC_D, groups, subtiles)
    P.wait_all_dma("sp")
    P.emit()
    return cx.nc


QSZ = (22, 22, 21, 21)
QOFF = (0, 22, 44, 65)


def build_d():
    cx = Cx()
    P = cx.P
    x1T = cx.din("x1T", [D, T])
    gx = cx.din("gx", [DFF, 2 + T], BF16)
    uT = cx.din("uT", [DFF, T], BF16)
    cw_d = cx.din("cwcol", [128, NFC, 3])
    w_dn = cx.din("w_down", [DFF, D])
    x2T = cx.dout("x2T", [D, T])
    cw, cw_r = cx.load(cw_d, [128, NFC, 3])
    acts = [cx.sb([128, 22, T], BF16, "act") for _ in range(2)]
    act_regs = [[Reg() for _ in range(22)] for _ in range(2)]
    ws = WStream(cx, 22, 384, nslots=3)
    gr = Ring(cx, 3, [128, 2 + T], BF16, "g")
    ur = Ring(cx, 3, [128, T], BF16, "u")
    yr = Ring(cx, 3, [128, T], F32, "y")
    xr = Ring(cx, 4, [128, 512], F32, "xc")
    orr_ = Ring(cx, 4, [128, 512], F32, "xo")
    x2_regs = [[Reg(), Reg()] for _ in range(KC_D)]

    def chunk(qi, kc):
        act = acts[qi % 2]
        ch = QOFF[qi] + kc
        gt, gt_r, k1 = gr.next()
        ut, ut_r, k2 = ur.next()
        y, y_r, _ = yr.next()
        P.dma("sp", k1, lambda e: e.dma_start(out=gt[:, :], in_=gx[ch * 128:(ch + 1) * 128, :]), writes=[gt_r])
        P.dma("sp", k2, lambda e: e.dma_start(out=ut[:, :], in_=uT[ch * 128:(ch + 1) * 128, :]), writes=[ut_r])
        P.op("act", lambda e: e.activation(out=y[:, :], in_=gt[:, 2:2 + T], func=AF.Copy, scale=cw[:, ch, 2:3]),
             reads=[gt_r, cw_r], writes=[y_r])
        P.op("dve", lambda e: e.scalar_tensor_tensor(out=y[:, :], in0=gt[:, 1:1 + T], scalar=cw[:, ch, 1:2], in1=y[:, :], op0=ALU.mult, op1=ALU.add),
             reads=[gt_r, cw_r, y_r], writes=[y_r])
        P.op("dve", lambda e: e.scalar_tensor_tensor(out=y[:, :], in0=gt[:, 0:T], scalar=cw[:, ch, 0:1], in1=y[:, :], op0=ALU.mult, op1=ALU.add),
             reads=[gt_r, cw_r, y_r], writes=[y_r])
        P.op("act", lambda e: e.activation(out=y[:, :], in_=y[:, :], func=AF.Silu), reads=[y_r], writes=[y_r])
        P.op("dve", lambda e: e.tensor_tensor(out=act[:, kc, :], in0=y[:, :], in1=ut[:, :], op=ALU.mult),
             reads=[y_r, ut_r], writes=[act_regs[qi % 2][kc]])

    pending = []

    def hook():
        if pending:
            qi_, kc_ = pending.pop(0)
            chunk(qi_, kc_)
    cx.unit_hook = hook
    for kc in range(QSZ[0]):
        chunk(0, kc)
    for qi in range(4):
        while pending:
            hook()
        if qi + 1 < 4:
            pending.extend((qi + 1, kc) for kc in range(QSZ[qi + 1]))
        src = x1T if qi == 0 else x2T

        def epi_res(gi, g, stn, tok0, n, bset, src=src, qi=qi):
            sti = 0 if stn == "s0" else 1
            for mi in range(len(g["widths"])):
                ch = g["ch0"] + mi
                b = bset[mi]
                xt, xt_r, k = xr.next()
                P.dma("sp", k, lambda e, xt=xt, ch=ch: e.dma_start(out=xt[:, :], in_=src[ch * 128:(ch + 1) * 128, tok0:tok0 + n]),
                      reads=([x2_regs[ch][sti]] if qi > 0 else []), writes=[xt_r])
                o, o_r, k2 = orr_.next()
                P.op("dve", lambda e, o=o, xt=xt, b=b: e.tensor_tensor(out=o[:, :], in0=cx.ps[b][:, :], in1=xt[:, :], op=ALU.add),
                     reads=[cx.ps_regs[b], xt_r], writes=[o_r])
                P.dma("sp", k2, lambda e, o=o, ch=ch: e.dma_start(out=x2T[ch * 128:(ch + 1) * 128, tok0:tok0 + n], in_=o[:, :]),
                      reads=[o_r], writes=[x2_regs[ch][sti]])
        groups = []
        for ch0 in range(0, KC_D, 3):
            nchk = min(3, KC_D - ch0)
            groups.append(dict(kind="fm", segs=[(0, ch0 * 128, nchk * 128)], widths=[128] * nchk, epi=epi_res, ch0=ch0))
        a = acts[qi % 2]
        ar = act_regs[qi % 2]
        gemm(cx, ws, w_dn[QOFF[qi] * 128:(QOFF[qi] + QSZ[qi]) * 128, :], QSZ[qi], groups, [("s0", a, ar, 0, 512), ("s1", a, ar, 512, 512)])
    P.wait_all_dma("sp")
    P.emit()
    return cx.nc


def build_e():
    cx = Cx()
    P = cx.P
    x2T = cx.din("xT", [D, T])
    g_d = cx.din("gcol", [128, KC_D])
    W = cx.din("w_in", [D, 10240])
    wT_d = cx.din("sgu_wT", [128, 16, 128])
    tri_d = cx.din("tri", [128, 128])
    bsb_d = cx.din("bsb", [128, 16, 128])
    ngb_d = cx.din("ngb", [128, MIX])
    cvT = cx.dout("cvT", [MIX, T], BF16)
    gbT = cx.dout("gbT", [MIX, T], BF16)
    dT = cx.dout("dT", [MIX, T], BF16)
    cx.eps_t = cx.sb([128, 1], F32, "eps")
    cx.eps_r = Reg()
    P.op("pool", lambda e: e.memset(cx.eps_t[:, :], EPS), writes=[cx.eps_r])
    gcol, gcol_r = cx.load(g_d, [128, KC_D])
    tri, tri_r = cx.load(tri_d, [128, 128])
    bsb, bsb_r = cx.load(bsb_d, [128, 16, 128])
    ngb, ngb_r = cx.load(ngb_d, [128, MIX])
    wtm, wtm_r = cx.load(wT_d, [128, 16, 128], BF16, "wtm", eng="pool")
    for hh in range(16):
        P.op("pool", lambda e, hh=hh: e.tensor_tensor(out=wtm[:, hh, :], in0=wtm[:, hh, :], in1=tri[:, :], op=ALU.mult),
             reads=[wtm_r, tri_r], writes=[wtm_r])
    vgel = [cx.sb([128, MIX], BF16, "vgel") for _ in range(8)]
    vgel_r = [Reg() for _ in range(8)]
    ug = cx.sb([128, 16, T], BF16, "ug")
    ug_r = [Reg() for _ in range(16)]
    h = cx.sb([128, KC_D, T], BF16, "h")
    h_regs = [Reg() for _ in range(KC_D)]
    load_norm(cx, x2T, gcol, gcol_r, T, h, h_regs, rings=(Ring(cx, 2, [128, T], F32, "xs"), Ring(cx, 2, [128, T], BF16, "sq")))
    ws = WStream(cx, KC_D, 256)
    subtiles = [("s0", h, h_regs, 0, 512), ("s1", h, h_regs, 512, 512)]

    def epi_zv(gi, g, stn, tok0, n, bset):
        c0 = g["c0"]
        for tb in range(4):
            tbg = tok0 // 128 + tb
            b = bset[tb]
            P.op("act", lambda e, tbg=tbg, b=b: e.activation(out=vgel[tbg][:, c0:c0 + 256], in_=cx.ps[b][:, 0:256], func=AF.Gelu),
                 reads=[cx.ps_regs[b]], writes=[vgel_r[tbg]])

    def epi_zu(gi, g, stn, tok0, n, bset):
        for mi in range(2):
            ch = g["ch0"] + mi
            b = bset[mi]
            P.op("act", lambda e, ch=ch, b=b: e.activation(out=ug[:, ch, tok0:tok0 + n], in_=cx.ps[b][:, :], func=AF.Gelu),
                 reads=[cx.ps_regs[b]], writes=[ug_r[ch]])
    tmr = Ring(cx, 2, [128, 512], F32, "cvtmp")
    cvr = Ring(cx, 3, [128, 512], BF16, "cvo")

    def epi_cv(gi, g, stn, tok0, n, bset):
        j = g["j"]
        tm, tm_r, _ = tmr.next()
        P.op("act", lambda e: e.activation(out=tm[:, :], in_=cx.ps[bset[0]][:, :], func=AF.Copy), reads=[cx.ps_regs[bset[0]]], writes=[tm_r])
        o, o_r, k = cvr.next()
        P.op("dve", lambda e: e.tensor_tensor(out=o[:, :], in0=cx.ps[bset[1]][:, :], in1=tm[:, :], op=ALU.mult),
             reads=[cx.ps_regs[bset[1]], tm_r], writes=[o_r])
        P.dma("sp", k, lambda e: e.dma_start(out=cvT[j * 128:(j + 1) * 128, tok0:tok0 + n], in_=o[:, :]), reads=[o_r])

    def epi_gb(gi, g, stn, tok0, n, bset):
        for mi in range(2):
            ch = g["ch0"] + mi
            b = bset[mi]
            o, o_r, k = cvr.next()
            P.op("act", lambda e, o=o, b=b: e.activation(out=o[:, :], in_=cx.ps[b][:, :], func=AF.Copy), reads=[cx.ps_regs[b]], writes=[o_r])
            P.dma("sp", k, lambda e, o=o, ch=ch: e.dma_start(out=gbT[ch * 128:(ch + 1) * 128, tok0:tok0 + n], in_=o[:, :]), reads=[o_r])
    groups = []
    for vg in range(8):
        groups.append(dict(kind="tm", segs=[(0, 8192 + vg * 256, 256)], gw=256, epi=epi_zv, c0=vg * 256))
    for ch0 in range(0, 16, 2):
        groups.append(dict(kind="fm", segs=[(0, 6144 + ch0 * 128, 256)], widths=[128] * 2, epi=epi_zu, ch0=ch0))
    for j in range(16):
        groups.append(dict(kind="fm", segs=[(0, j * 128, 128), (128, 4096 + j * 128, 128)], widths=[128] * 2, epi=epi_cv, j=j))
    for ch0 in range(0, 16, 2):
        groups.append(dict(kind="fm", segs=[(0, 2048 + ch0 * 128, 256)], widths=[128] * 2, epi=epi_gb, ch0=ch0))
    gemm(cx, ws, W, KC_D, groups, subtiles)
    def hflat(c):
        return h[:, c:c + 2, :].rearrange("p a b -> p (a b)"), [h_regs[c], h_regs[c + 1]]
    junk, junk_r = hflat(0)
    vns = [hflat(2), hflat(4)]
    dous = [hflat(6), hflat(8)]
    ss = cx.sb([128, 8], F32, "ss")
    ss_r = Reg()
    for tbg in range(8):
        P.op("act", lambda e, tbg=tbg: e.activation(out=junk, in_=vgel[tbg][:, :], func=AF.Square, accum_out=ss[:, tbg:tbg + 1]),
             reads=[vgel_r[tbg]], writes=junk_r + [ss_r])
    P.op("act", lambda e: e.activation(out=ss[:, :], in_=ss[:, :], func=AF.Sqrt, bias=cx.eps_t[:, 0:1], scale=1.0 / MIX),
         reads=[ss_r, cx.eps_r], writes=[ss_r])
    P.op("dve", lambda e: e.reciprocal(out=ss[:, :], in_=ss[:, :]), reads=[ss_r], writes=[ss_r])
    t1r = Ring(cx, 3, [128, 128], F32, "t1")
    dkeys = [cx.uid("dk"), cx.uid("dk")]
    dTv = dT.rearrange("(c p) t -> p c t", p=128)
    for tbg in range(8):
        vn, vn_r = vns[tbg % 2]
        do, do_r = dous[tbg % 2]
        P.op("dve", lambda e, vn=vn, tbg=tbg: e.scalar_tensor_tensor(out=vn, in0=vgel[tbg][:, :], scalar=ss[:, tbg:tbg + 1], in1=ngb[:, :],
                                                                    op0=ALU.mult, op1=ALU.mult),
             reads=[vgel_r[tbg], ss_r, ngb_r], writes=vn_r)
        bset = [0, 1, 2, 3] if tbg % 2 == 0 else [4, 5, 6, 7]
        for hh in range(16):
            b = bset[hh // 4]
            cc = (hh % 4) * 128
            P.op("pe", lambda e, vn=vn, hh=hh, b=b, cc=cc: e.matmul(cx.ps[b][:, cc:cc + 128], lhsT=vn[:, hh * 128:(hh + 1) * 128], rhs=wtm[:, hh, :],
                                                                   start=True, stop=True),
                 reads=vn_r + [wtm_r], writes=[cx.ps_regs[b]])
        for hh in range(16):
            b = bset[hh // 4]
            cc = (hh % 4) * 128
            t1, t1_r, _ = t1r.next()
            P.op("dve", lambda e, t1=t1, hh=hh, b=b, cc=cc: e.tensor_tensor(out=t1[:, :], in0=cx.ps[b][:, cc:cc + 128], in1=bsb[:, hh, :], op=ALU.add),
                 reads=[cx.ps_regs[b], bsb_r], writes=[t1_r])
            P.op("pool", lambda e, t1=t1, hh=hh, do=do, tbg=tbg: e.tensor_tensor(out=do[:, hh * 128:(hh + 1) * 128], in0=t1[:, :],
                                                                               in1=ug[:, hh, tbg * 128:(tbg + 1) * 128], op=ALU.mult),
                 reads=[t1_r, ug_r[hh]], writes=do_r)
        P.dma("sp", dkeys[tbg % 2], lambda e, do=do, tbg=tbg: e.dma_start(out=dTv[:, :, tbg * 128:(tbg + 1) * 128],
                                                                        in_=do.rearrange("p (c t) -> p c t", t=128)), reads=do_r)
    P.wait_all_dma("sp")
    P.emit()
    return cx.nc


def _cat(res, key, axis):
    return np.concatenate([np.asarray(res[c][key]) for c in range(NCORES)], axis=axis)


def _halo(full, c, n):
    if c == 0:
        return np.ascontiguousarray(np.concatenate([np.zeros((full.shape[0], n), full.dtype), full[:, :T]], axis=1))
    return np.ascontiguousarray(full[:, c * T - n:(c + 1) * T])


def launch_bc(layer, xT_l, hcat_l, w_out, fnorm, w_up, extra=None):
    nc = build_bc(layer)
    w_out = np.ascontiguousarray(np.asarray(w_out, np.float32))
    w_up = np.ascontiguousarray(np.asarray(w_up, np.float32))
    fg = colmajor(fnorm)
    maps = []
    for c in range(NCORES):
        m = dict(xT=xT_l[c], hcatT=hcat_l[c], w_out=w_out, fgcol=fg, w_up=w_up)
        if extra is not None:
            m.update(extra[c])
        maps.append(m)
    return run(nc, maps).results


def launch_d(x1_l, res_bc, conv, w_down):
    nc = build_d()
    gfull = _cat(res_bc, "gT", 1)
    cw = np.ascontiguousarray(np.asarray(conv, np.float32).T.reshape(NFC, 128, 3).transpose(1, 0, 2))
    w_down = np.ascontiguousarray(np.asarray(w_down, np.float32))
    maps = [dict(x1T=x1_l[c], gx=_halo(gfull, c, 2), uT=np.asarray(res_bc[c]["uT"]), cwcol=cw, w_down=w_down) for c in range(NCORES)]
    return run(nc, maps).results


def kernel(**inp):
    x = np.asarray(inp["x"], np.float32)[0]
    xT_l = [np.ascontiguousarray(x[c * T:(c + 1) * T].T) for c in range(NCORES)]
    tri = np.triu(np.ones((128, 128), np.float32))
    r1 = launch1(inp).results
    r2 = launch2a(inp, r1).results
    aT = np.concatenate([np.asarray(r2[c]["aT"]) for c in range(NCORES)], axis=0).reshape(MIX, SEQ)
    hcat_l = [np.ascontiguousarray(np.concatenate([aT[:, c * T:(c + 1) * T], np.asarray(r1[c]["pmT"])], axis=0)) for c in range(NCORES)]
    rbc = launch_bc(0, xT_l, hcat_l, inp["l0_w_out"], inp["l0_ffn_norm"], inp["l0_ffn_w_up"])
    x1_l = [np.asarray(rbc[c]["x1T"]) for c in range(NCORES)]
    rd = launch_d(x1_l, rbc, inp["l0_ffn_conv"], inp["l0_ffn_w_down"])
    x2_l = [np.asarray(rd[c]["x2T"]) for c in range(NCORES)]
    nce = build_e()
    w_in1 = np.ascontiguousarray(np.asarray(inp["l1_w_in"], np.float32))
    wT = np.ascontiguousarray(np.asarray(inp["l1_sgu_w"], np.float32).transpose(2, 0, 1))
    bsb = np.ascontiguousarray(np.broadcast_to(np.asarray(inp["l1_sgu_b"], np.float32)[None], (128, 16, 128)))
    ngb = np.ascontiguousarray(np.broadcast_to(np.asarray(inp["l1_sgu_norm"], np.float32)[None], (128, MIX)))
    g1 = colmajor(inp["l1_mix_norm"])
    re_ = run(nce, [dict(xT=x2_l[c], gcol=g1, w_in=w_in1, sgu_wT=wT, tri=tri, bsb=bsb, ngb=ngb) for c in range(NCORES)]).results
    cvfull = _cat(re_, "cvT", 1)
    cw1 = np.ascontiguousarray(np.asarray(inp["l1_conv_w"], np.float32).T.reshape(16, 128, 3).transpose(1, 0, 2))
    hcat1 = [np.ascontiguousarray(np.concatenate([np.zeros((MIX, T), NPBF), np.asarray(re_[c]["dT"])], axis=0)) for c in range(NCORES)]
    extra = [dict(cvx=_halo(cvfull, c, 2), gbT=np.asarray(re_[c]["gbT"]), cwcol=cw1) for c in range(NCORES)]
    rbc1 = launch_bc(1, x2_l, hcat1, inp["l1_w_out"], inp["l1_ffn_norm"], inp["l1_ffn_w_up"], extra)
    x3_l = [np.asarray(rbc1[c]["x1T"]) for c in range(NCORES)]
    rd1 = launch_d(x3_l, rbc1, inp["l1_ffn_conv"], inp["l1_ffn_w_down"])
    out = np.concatenate([np.asarray(rd1[c]["x2T"]).T for c in range(NCORES)], axis=0)
    return np.ascontiguousarray(out[None].astype(np.float32))
```

```python
import numpy as np
import ml_dtypes
import concourse.bass as bass
import concourse.mybir as mybir
from concourse.bass_utils import run_bass_kernel_spmd

F32 = mybir.dt.float32
BF16 = mybir.dt.bfloat16
AF = mybir.ActivationFunctionType
ALU = mybir.AluOpType
NPBF = ml_dtypes.bfloat16

NCORES = 8
SEQ = 8192
D = 4096
T = SEQ // NCORES
KC_D = D // 128
MIX = 2048
NH = 16
DFF = 11008
NFC = DFF // 128
EPS = 1e-6
ENGS = ("pe", "act", "dve", "pool", "sp")


class Reg:
    __slots__ = ("w", "rs")

    def __init__(self):
        self.w = None
        self.rs = {}


class Op:
    __slots__ = ("fn", "deps", "signal", "dma_sem", "val", "ndma")

    def __init__(self, fn, deps, dma_sem=None, ndma=1):
        self.fn = fn
        self.deps = deps
        self.signal = False
        self.dma_sem = dma_sem
        self.val = None
        self.ndma = ndma


class Prog:
    def __init__(self, nc):
        self.nc = nc
        self.ops = {e: [] for e in ENGS}
        self.dma_cnt = {}
        self.dma_sems = {}
        self.eng_sems = {}

    def _collect(self, eng, reads, writes):
        deps = {}

        def add(tok):
            if tok is None:
                return
            c, s = tok
            if c == "pe" and eng == "pe":
                return
            if deps.get(c, -1) < s:
                deps[c] = s
        for r in reads:
            add(r.w)
        for w in writes:
            add(w.w)
            for c, s in w.rs.items():
                add((c, s))
        return deps

    def _commit(self, tok, reads, writes):
        c, s = tok
        for r in reads:
            if r.rs.get(c, -1) < s:
                r.rs[c] = s
        for w in writes:
            w.w = tok
            w.rs = {}

    def op(self, eng, fn, reads=(), writes=()):
        deps = self._collect(eng, reads, writes)
        idx = len(self.ops[eng])
        self.ops[eng].append(Op(fn, deps))
        tok = (eng, idx)
        self._commit(tok, reads, writes)
        return tok

    def dma(self, eng, semkey, fn, reads=(), writes=(), n=1):
        deps = self._collect(eng, reads, writes)
        cnt = self.dma_cnt.get(semkey, 0) + n
        self.dma_cnt[semkey] = cnt
        self.ops[eng].append(Op(fn, deps, dma_sem=semkey, ndma=n))
        tok = (("dma", semkey), cnt)
        self._commit(tok, reads, writes)
        return tok

    def wait_all_dma(self, eng):
        deps = {("dma", k): v for k, v in self.dma_cnt.items()}
        self.ops[eng].append(Op(None, deps))

    def emit(self):
        nc = self.nc
        for e in ENGS:
            for o in self.ops[e]:
                for c, s in o.deps.items():
                    if isinstance(c, str):
                        self.ops[c][s].signal = True
        for e in ENGS:
            n = 0
            for o in self.ops[e]:
                if o.dma_sem is None and o.signal:
                    n += 1
                    o.val = n
        for e in ENGS:
            self.eng_sems[e] = nc.alloc_semaphore(name=f"s_{e}")
        for k in self.dma_cnt:
            self.dma_sems[k] = nc.alloc_semaphore(name=f"d_{k}")
        prog = self

        def run(e, eng):
            waited = {}
            for o in prog.ops[e]:
                for c, s in o.deps.items():
                    if isinstance(c, str):
                        v = prog.ops[c][s].val
                        sem = prog.eng_sems[c]
                    else:
                        v = 16 * s
                        sem = prog.dma_sems[c[1]]
                    if waited.get(c, 0) < v:
                        eng.wait_ge(sem, v)
                        waited[c] = v
                if o.fn is None:
                    continue
                ins = o.fn(eng)
                if o.dma_sem is not None:
                    if not isinstance(ins, (list, tuple)):
                        ins = [ins]
                    assert len(ins) == o.ndma
                    for i in ins:
                        i.then_inc(prog.dma_sems[o.dma_sem], 16)
                elif o.signal:
                    ins.then_inc(prog.eng_sems[e], 1)

        with nc.Block() as block:
            @block.tensor
            def _(eng):
                run("pe", eng)

            @block.scalar
            def _(eng):
                run("act", eng)

            @block.vector
            def _(eng):
                run("dve", eng)

            @block.gpsimd
            def _(eng):
                run("pool", eng)

            @block.sync
            def _(eng):
                run("sp", eng)


class Cx:
    def __init__(self):
        self.nc = bass.Bass("TRN2", target_bir_lowering=False)
        self.P = Prog(self.nc)
        self.n = 0
        self.ps = [self.nc.alloc_psum_tensor(f"ps{i}", [128, 512], F32) for i in range(8)]
        self.ps_regs = [Reg() for _ in range(8)]
        self.pe_defer = []
        self.ones = self.sb([128, 128], BF16, "ones")
        self.ones_r = Reg()
        self.P.op("pool", lambda e: e.memset(self.ones[:, :], 1.0), writes=[self.ones_r])

    def uid(self, s):
        self.n += 1
        return f"{s}{self.n}"

    def sb(self, shape, dt, name="t"):
        return self.nc.alloc_sbuf_tensor(self.uid(name), list(shape), dt)

    def din(self, name, shape, dt=F32):
        return self.nc.dram_tensor(name, list(shape), dt, kind="ExternalInput").ap()

    def dout(self, name, shape, dt=F32):
        return self.nc.dram_tensor(name, list(shape), dt, kind="ExternalOutput").ap()

    def flush_pe(self):
        d = self.pe_defer
        self.pe_defer = []
        for f in d:
            f()

    def load(self, dram_ap, shape, dt=F32, name="c", eng="sp"):
        t = self.sb(shape, dt, name)
        r = Reg()
        sl = tuple(slice(None) for _ in shape)
        self.P.dma(eng, self.uid("ld"), lambda e: e.dma_start(out=t[sl], in_=dram_ap), writes=[r])
        return t, r


class Ring:
    def __init__(self, cx, n, shape, dt, name="r"):
        self.tiles = [cx.sb(shape, dt, name) for _ in range(n)]
        self.regs = [Reg() for _ in range(n)]
        self.keys = [cx.uid(name + "k") for _ in range(n)]
        self.i = 0
        self.n = n

    @classmethod
    def over(cls, cx, tiles, regs, name="r"):
        o = cls.__new__(cls)
        o.tiles = tiles
        o.regs = regs
        o.keys = [cx.uid(name + "k") for _ in tiles]
        o.i = 0
        o.n = len(tiles)
        return o

    def next(self):
        j = self.i % self.n
        self.i += 1
        return self.tiles[j], self.regs[j], self.keys[j]


def load_norm(cx, xT, gcol, gcol_r, Tn, h, h_regs, rings=None, x_regs=None):
    P = cx.P
    nb = (Tn + 511) // 512
    xs = Ring(cx, 3, [128, Tn], F32, "xs") if rings is None else rings[0]
    sq = Ring(cx, 2, [128, Tn], BF16, "sq") if rings is None else rings[1]
    rstd = cx.sb([128, Tn], F32, "rstd")
    rstd_r = Reg()
    for c in range(KC_D):
        t, r, k = xs.next()
        P.dma("sp", k, lambda e, t=t, c=c: e.dma_start(out=t[:, 0:Tn], in_=xT[c * 128:(c + 1) * 128, :]), reads=([x_regs[c]] if x_regs else []), writes=[r])
        s, sr, _ = sq.next()
        P.op("act", lambda e, t=t, s=s: e.activation(out=s[:, 0:Tn], in_=t[:, 0:Tn], func=AF.Square), reads=[r], writes=[sr])
        for j in range(nb):
            n = min(512, Tn - j * 512)
            P.op("pe", lambda e, s=s, j=j, n=n, c=c: e.matmul(cx.ps[j][:, 0:n], lhsT=cx.ones[:, :], rhs=s[:, j * 512:j * 512 + n],
                                                          start=(c == 0), stop=(c == KC_D - 1)),
                 reads=[sr, cx.ones_r], writes=[cx.ps_regs[j]])
    for j in range(nb):
        n = min(512, Tn - j * 512)
        P.op("act", lambda e, j=j, n=n: e.activation(out=rstd[:, j * 512:j * 512 + n], in_=cx.ps[j][:, 0:n], func=AF.Sqrt,
                                                   bias=cx.eps_t[:, 0:1], scale=1.0 / D),
             reads=[cx.ps_regs[j], cx.eps_r], writes=[rstd_r])
    P.op("dve", lambda e: e.reciprocal(out=rstd[:, :], in_=rstd[:, :]), reads=[rstd_r], writes=[rstd_r])
    for c in range(KC_D):
        t, r, k = xs.next()
        P.dma("sp", k, lambda e, t=t, c=c: e.dma_start(out=t[:, 0:Tn], in_=xT[c * 128:(c + 1) * 128, :]), reads=([x_regs[c]] if x_regs else []), writes=[r])
        P.op("dve", lambda e, t=t, c=c: e.scalar_tensor_tensor(out=h[:, c, :], in0=t[:, 0:Tn], scalar=gcol[:, c:c + 1], in1=rstd[:, :],
                                                              op0=ALU.mult, op1=ALU.mult),
             reads=[r, gcol_r, rstd_r], writes=[h_regs[c]])
    return xs, sq


class WStream:
    def __init__(self, cx, kcmax, gw, nslots=2):
        self.cx = cx
        self.gw = gw
        self.kcmax = kcmax
        self.nparts = (kcmax + 7) // 8
        self.slots = [cx.sb([128, kcmax, gw], BF16, "ws") for _ in range(nslots)]
        self.regs = [[Reg() for _ in range(self.nparts)] for _ in range(nslots)]
        self.keys = [[cx.uid("wk") for _ in range(self.nparts)] for _ in range(nslots)]
        self.i = 0
        self.nslots = nslots

    def load(self, W, kc_n, segs):
        cx = self.cx
        s = self.i % self.nslots
        self.i += 1
        Wv = W.rearrange("(kc p) n -> p kc n", p=128)
        for q in range((kc_n + 7) // 8):
            k0, k1 = q * 8, min(kc_n, q * 8 + 8)

            def fn(e, s=s, k0=k0, k1=k1):
                return [e.dma_start(out=self.slots[s][:, k0:k1, d0:d0 + w], in_=Wv[:, k0:k1, c0:c0 + w]) for (d0, c0, w) in segs]
            cx.P.dma("pool", self.keys[s][q], fn, writes=[self.regs[s][q]], n=len(segs))
        return s


def gemm(cx, ws, W, kc_n, groups, subtiles):
    P = cx.P
    unit = cx.unit if hasattr(cx, "unit") else 0
    for gi, g in enumerate(groups):
        s = ws.load(W, kc_n, g["segs"])
        slot = ws.slots[s]
        for (stn, ht, hr, tok0, n) in subtiles:
            if stn == "halo" and not g.get("halo"):
                continue
            bset = [0, 1, 2, 3] if unit % 2 == 0 else [4, 5, 6, 7]
            unit += 1
            if g["kind"] == "fm":
                widths = g["widths"]
                for kc in range(kc_n):
                    for mi, wd in enumerate(widths):
                        b = bset[mi]
                        P.op("pe", lambda e, b=b, kc=kc, mi=mi, wd=wd, ht=ht, tok0=tok0, n=n, slot=slot: e.matmul(
                            cx.ps[b][0:wd, 0:n], lhsT=slot[:, kc, mi * 128:mi * 128 + wd], rhs=ht[:, kc, tok0:tok0 + n],
                            start=(kc == 0), stop=(kc == kc_n - 1)),
                            reads=[ws.regs[s][kc // 8], hr[kc]], writes=[cx.ps_regs[b]])
                nb = len(widths)
            else:
                ntb = n // 128
                gwid = g["gw"]
                for kc in range(kc_n):
                    for tb in range(ntb):
                        b = bset[tb]
                        P.op("pe", lambda e, b=b, kc=kc, tb=tb, ht=ht, tok0=tok0, slot=slot, gwid=gwid: e.matmul(
                            cx.ps[b][:, 0:gwid], lhsT=ht[:, kc, tok0 + tb * 128:tok0 + (tb + 1) * 128], rhs=slot[:, kc, 0:gwid],
                            start=(kc == 0), stop=(kc == kc_n - 1)),
                            reads=[ws.regs[s][kc // 8], hr[kc]], writes=[cx.ps_regs[b]])
                nb = ntb
            cx.flush_pe()
            g["epi"](gi, g, stn, tok0, n, bset)
            if getattr(cx, "unit_hook", None):
                cx.unit_hook()
    cx.flush_pe()
    cx.unit = unit


def build_l1():
    cx = Cx()
    P = cx.P
    nc = cx.nc
    xT = cx.din("xT", [D, T])
    xh = cx.din("xh", [D, 16])
    W = cx.din("w_in", [D, 8208])
    gcol_d = cx.din("gcol", [128, KC_D])
    qk_d = cx.din("qkg", [128, 2])
    bf_d = cx.din("bf", [16, 1])
    pw_d = cx.din("pool_w", [4, 512, 512])
    psc_d = cx.din("pscol", [128, 16])
    icn_d = cx.din("invcnt", [128, 4, 16])
    qT = cx.dout("qT", [NH, 128, T], BF16)
    kT = cx.dout("kT", [NH, 128, T], BF16)
    Vo = cx.dout("V", [T, MIX], BF16)
    lf = cx.dout("logf", [NH, T])
    pmT = cx.dout("pmT", [MIX, T], BF16)

    cx.eps_t = cx.sb([128, 1], F32, "eps")
    cx.eps_r = Reg()
    P.op("pool", lambda e: e.memset(cx.eps_t[:, :], EPS), writes=[cx.eps_r])
    gcol, gcol_r = cx.load(gcol_d, [128, KC_D])
    qkg, qkg_r = cx.load(qk_d, [128, 2])
    bft, bft_r = cx.load(bf_d, [16, 1])
    psc, psc_r = cx.load(psc_d, [128, 16])
    icn, icn_r = cx.load(icn_d, [128, 4, 16])
    P.op("dve", lambda e: e.tensor_scalar(out=qkg[:, 0:1], in0=qkg[:, 0:1], scalar1=float(128 ** -0.5), scalar2=None, op0=ALU.mult),
         reads=[qkg_r], writes=[qkg_r])
    P.op("dve", lambda e: e.tensor_scalar(out=bft[:, :], in0=bft[:, :], scalar1=-1.0, scalar2=None, op0=ALU.mult),
         reads=[bft_r], writes=[bft_r])

    h = cx.sb([128, KC_D, T], BF16, "h")
    h_regs = [Reg() for _ in range(KC_D)]
    hh = cx.sb([128, KC_D, 16], BF16, "hh")
    hh_regs = [Reg() for _ in range(KC_D)]
    zb = [cx.sb([128, 16 + T], F32, "zb") for _ in range(2)]
    zb_r = [Reg() for _ in range(2)]
    pa = [cx.sb([128, 16 + T], F32, "pa") for _ in range(1)]
    pa_r = [Reg() for _ in range(1)]
    pbb = [cx.sb([128, 16 + T], F32, "pb") for _ in range(1)]
    pb_r = [Reg() for _ in range(1)]
    sqring = Ring(cx, 2, [128, T], BF16, "sq")
    xsring = Ring.over(cx, [zb[0], zb[1], pa[0]], [zb_r[0], zb_r[1], pa_r[0]], "xs")
    load_norm(cx, xh, gcol, gcol_r, 16, hh, hh_regs, rings=(xsring, sqring))
    load_norm(cx, xT, gcol, gcol_r, T, h, h_regs, rings=(xsring, sqring))

    ws = WStream(cx, KC_D, 384)
    subtiles = [("halo", hh, hh_regs, 0, 16), ("s0", h, h_regs, 0, 512), ("s1", h, h_regs, 512, 512)]

    pooled = cx.sb([128, 16, T], BF16, "pooled")
    pooled_regs = [Reg() for _ in range(16)]
    pcnt = [0]

    def epi_zp(gi, g, stn, tok0, n, bset):
        grp = g["pg"]
        wlen = (2, 4, 8, 16)[grp]
        for mi in range(2):
            off = 0 if stn == "halo" else 16 + tok0
            P.op("act", lambda e, mi=mi, off=off, n=n, b=bset[mi]: e.activation(out=zb[mi][:, off:off + n], in_=cx.ps[b][:, 0:n], func=AF.Copy),
                 reads=[cx.ps_regs[bset[mi]]], writes=[zb_r[mi]])
        if stn != "s1":
            return
        L = 16 + T
        for mi in range(2):
            ch = g["ch0"] + mi
            j = 0
            src, src_r = zb[mi], zb_r[mi]
            bufs = [(pa[j], pa_r[j]), (pbb[j], pb_r[j])]
            sh = 1
            bi = 0
            while sh < wlen:
                dst, dst_r = bufs[bi]
                eng = "dve"
                P.op(eng, lambda e, dst=dst, src=src, sh=sh: e.tensor_tensor(out=dst[:, sh:L], in0=src[:, sh:L], in1=src[:, 0:L - sh], op=ALU.add),
                     reads=[src_r], writes=[dst_r])
                src, src_r = dst, dst_r
                bi ^= 1
                sh *= 2
            P.op("dve", lambda e, src=src, mi=mi, ch=ch, wlen=wlen: e.scalar_tensor_tensor(
                out=pooled[:, ch, :], in0=src[:, 16:L], scalar=1.0 / wlen, in1=zb[mi][:, 16:L], op0=ALU.mult, op1=ALU.subtract),
                reads=[src_r, zb_r[mi]], writes=[pooled_regs[ch]])
            dst, dst_r = bufs[bi]
            P.op("dve", lambda e, dst=dst, src=src, grp=grp: e.tensor_tensor(out=dst[:, 0:16], in0=src[:, 16:32], in1=icn[:, grp, :], op=ALU.mult),
                 reads=[src_r, icn_r], writes=[dst_r])
            P.op("dve", lambda e, dst=dst, mi=mi, ch=ch: e.tensor_tensor(out=pooled[:, ch, 0:16], in0=dst[:, 0:16], in1=zb[mi][:, 16:32], op=ALU.subtract),
                 reads=[dst_r, zb_r[mi]], writes=[pooled_regs[ch]])

    lfr = Ring(cx, 2, [16, 512], F32, "lf")

    def epi_f(gi, g, stn, tok0, n, bset):
        t, r, k = lfr.next()
        b = bset[0]
        P.op("act", lambda e: e.activation(out=t[:, :], in_=cx.ps[b][0:16, 0:n], func=AF.Exp, bias=bft[:, 0:1], scale=-1.0),
             reads=[cx.ps_regs[b], bft_r], writes=[r])
        P.op("act", lambda e: e.activation(out=t[:, :], in_=t[:, :], func=AF.Ln, bias=1.0, scale=1.0), reads=[r], writes=[r])
        P.op("dve", lambda e: e.tensor_scalar(out=t[:, :], in0=t[:, :], scalar1=-1.0, scalar2=None, op0=ALU.mult), reads=[r], writes=[r])
        P.dma("sp", k, lambda e: e.dma_start(out=lf[:, tok0:tok0 + n], in_=t[:, :]), reads=[r])

    vr = Ring(cx, 4, [128, 256], BF16, "vo")

    def epi_v(gi, g, stn, tok0, n, bset):
        c0 = g["c0"]
        for tb in range(4):
            t, r, k = vr.next()
            b = bset[tb]
            P.op("act", lambda e, t=t, b=b: e.activation(out=t[:, :], in_=cx.ps[b][:, 0:256], func=AF.Copy), reads=[cx.ps_regs[b]], writes=[r])
            r0 = tok0 + tb * 128
            P.dma("sp", k, lambda e, t=t, r0=r0: e.dma_start(out=Vo[r0:r0 + 128, c0:c0 + 256], in_=t[:, :]), reads=[r])

    sqr = Ring(cx, 3, [128, 512], BF16, "qsq")
    rtr = Ring(cx, 3, [128, 512], F32, "qrt")
    qor = Ring(cx, 3, [128, 512], BF16, "qo")

    def epi_qk(gi, g, stn, tok0, n, bset):
        which = g["which"]
        dst = qT if which == 0 else kT
        nbk = bset[3]
        for mi in range(len(g["widths"])):
            hd = g["h0"] + mi
            b = bset[mi]
            s, sr, _ = sqr.next()
            P.op("act", lambda e, s=s, b=b: e.activation(out=s[:, :], in_=cx.ps[b][:, :], func=AF.Square), reads=[cx.ps_regs[b]], writes=[sr])

            def later(s=s, sr=sr, b=b, hd=hd, nbk=nbk):
                P.op("pe", lambda e: e.matmul(cx.ps[nbk][:, :], lhsT=cx.ones[:, :], rhs=s[:, :], start=True, stop=True),
                     reads=[sr, cx.ones_r], writes=[cx.ps_regs[nbk]])
                rt, rr, _ = rtr.next()
                P.op("act", lambda e: e.activation(out=rt[:, :], in_=cx.ps[nbk][:, :], func=AF.Sqrt, bias=cx.eps_t[:, 0:1], scale=1.0 / 128),
                     reads=[cx.ps_regs[nbk], cx.eps_r], writes=[rr])
                P.op("dve", lambda e: e.reciprocal(out=rt[:, :], in_=rt[:, :]), reads=[rr], writes=[rr])
                o, orr, k = qor.next()
                P.op("dve", lambda e: e.scalar_tensor_tensor(out=o[:, :], in0=cx.ps[b][:, :], scalar=qkg[:, which:which + 1], in1=rt[:, :],
                                                             op0=ALU.mult, op1=ALU.mult),
                     reads=[cx.ps_regs[b], qkg_r, rr], writes=[orr])
                P.dma("sp", k, lambda e: e.dma_start(out=dst[hd, :, tok0:tok0 + n], in_=o[:, :]), reads=[orr])
            cx.pe_defer.append(later)

    groups = []
    for cp in range(8):
        c0 = 6160 + cp * 256
        groups.append(dict(kind="fm", segs=[(0, c0, 256)], widths=[128] * 2, epi=epi_zp, halo=True, pg=cp // 2, ch0=cp * 2))
    groups.append(dict(kind="fm", segs=[(0, 6144, 16)], widths=[16], epi=epi_f))
    for vg in range(8):
        groups.append(dict(kind="tm", segs=[(0, 4096 + vg * 256, 256)], gw=256, epi=epi_v, c0=vg * 256))
    for which in range(2):
        for h0 in range(0, 16, 3):
            nh = min(3, 16 - h0)
            c0 = which * 2048 + h0 * 128
            groups.append(dict(kind="fm", segs=[(0, c0, nh * 128)], widths=[128] * nh, epi=epi_qk, which=which, h0=h0))
    gemm(cx, ws, W, KC_D, groups, subtiles)

    pmr = Ring(cx, 3, [128, 512], BF16, "pmo")
    for pg in range(4):
        def epi_pm(gi, g, stn, tok0, n, bset, pg=pg):
            for mi in range(2):
                ch = pg * 4 + g["m0"] + mi
                o, orr, k = pmr.next()
                b = bset[mi]
                P.op("act", lambda e, o=o, b=b, ch=ch: e.activation(out=o[:, :], in_=cx.ps[b][:, :], func=AF.Copy, scale=psc[:, ch:ch + 1]),
                     reads=[cx.ps_regs[b], psc_r], writes=[orr])
                P.dma("sp", k, lambda e, o=o, ch=ch: e.dma_start(out=pmT[ch * 128:(ch + 1) * 128, tok0:tok0 + n], in_=o[:, :]), reads=[orr])
        pview = pooled[:, pg * 4:(pg + 1) * 4, :]
        gemm(cx, ws, pw_d[pg], 4, [dict(kind="fm", segs=[(0, m0 * 128, 256)], widths=[128] * 2, epi=epi_pm, m0=m0) for m0 in (0, 2)],
             [("s0", pview, pooled_regs[pg * 4:(pg + 1) * 4], 0, 512), ("s1", pview, pooled_regs[pg * 4:(pg + 1) * 4], 512, 512)])
    P.wait_all_dma("sp")
    P.emit()
    return nc


def colmajor(v, n=128):
    return np.ascontiguousarray(np.asarray(v, np.float32).reshape(-1, n).T)


def run(nc, in_maps, trace=False):
    res = run_bass_kernel_spmd(nc, in_maps, core_ids=list(range(NCORES)), trace=trace)
    return res


def launch1(inp, trace=False):
    x = np.asarray(inp["x"], np.float32)[0]
    nc = build_l1()
    w_in = np.ascontiguousarray(np.asarray(inp["l0_w_in"], np.float32))
    gcol = colmajor(inp["l0_mix_norm"])
    qkg = np.stack([np.asarray(inp["l0_q_gain"], np.float32), np.asarray(inp["l0_k_gain"], np.float32)], 1)
    bf = np.asarray(inp["l0_b_f"], np.float32).reshape(16, 1)
    pw = np.ascontiguousarray(np.asarray(inp["l0_pool_w"], np.float32))
    psc = colmajor(inp["l0_pool_scale"])
    maps = []
    for c in range(NCORES):
        xs = x[c * T:(c + 1) * T]
        xh = x[c * T - 16:c * T] if c > 0 else np.zeros((16, D), np.float32)
        icn = np.zeros((128, 4, 16), np.float32)
        for g, w in enumerate((2, 4, 8, 16)):
            pos = np.arange(16) + 1 + c * T
            icn[:, g, :] = 1.0 / np.minimum(pos, w)
        maps.append(dict(xT=np.ascontiguousarray(xs.T), xh=np.ascontiguousarray(xh.T), w_in=w_in, gcol=gcol,
                         qkg=np.ascontiguousarray(qkg), bf=bf, pool_w=pw, pscol=psc, invcnt=icn))
    return run(nc, maps, trace)


HPC = NH // NCORES
NQT = SEQ // 512
NKB = SEQ // 128


def build_l2a():
    cx = Cx()
    P = cx.P
    qT = cx.din("qT", [HPC, 128, SEQ], BF16)
    kT = cx.din("kT", [HPC, 128, SEQ], BF16)
    Vd = cx.din("V", [HPC, 128, NKB, 128], BF16)
    lf6 = cx.din("lf6", [HPC, 6, SEQ])
    coef_d = cx.din("coef", [6, 8])
    tri_d = cx.din("tri", [128, 128])
    aT = cx.dout("aT", [HPC, 128, SEQ], BF16)
    coef, coef_r = cx.load(coef_d, [6, 8])
    tri, tri_r = cx.load(tri_d, [128, 128])
    qs, ks, vs, qa, ka = [], [], [], [], []
    for hh in range(HPC):
        q_t, q_r = cx.load(qT[hh], [128, SEQ], BF16, "q")
        k_t, k_r = cx.load(kT[hh], [128, SEQ], BF16, "k")
        v_t, v_r = cx.load(Vd[hh], [128, NKB, 128], BF16, "v")
        qs.append((q_t, q_r)); ks.append((k_t, k_r)); vs.append((v_t, v_r))
    SG = 2048
    lft = cx.sb([6, SG], F32, "lft"); lft_r = Reg()
    c6 = cx.sb([6, SG], F32, "c6"); c6_r = Reg()
    r1 = cx.sb([6, SG], F32, "r1"); r1_r = Reg()
    hi = cx.sb([6, SG], BF16, "hi"); hi_r = Reg()
    mid = cx.sb([6, SG], BF16, "mid"); mid_r = Reg()
    lo = cx.sb([6, SG], BF16, "lo"); lo_r = Reg()
    tmp = cx.sb([6, SG], F32, "tmpa"); tmp_r = Reg()
    carry = cx.sb([6, 1], F32, "carry"); carry_r = Reg()
    qa_t = cx.sb([6, SEQ], BF16, "qa"); qa_r = Reg()
    ka_t = cx.sb([6, SEQ], BF16, "ka"); ka_r = Reg()
    aug_done = [False] * HPC

    def build_aug(hh):
        for sg in range(SEQ // SG):
            t0 = sg * SG
            P.dma("sp", "lftk", lambda e, hh=hh, t0=t0: e.dma_start(out=lft[:, :], in_=lf6[hh, :, t0:t0 + SG]), writes=[lft_r])
            P.op("pool", lambda e: e.memset(tmp[:, :], 1.0), writes=[tmp_r])
            if sg == 0:
                P.op("dve", lambda e: e.tensor_tensor_scan(out=c6[:, :], data0=tmp[:, :], data1=lft[:, :], initial=0.0, op0=ALU.mult, op1=ALU.add),
                     reads=[lft_r, tmp_r], writes=[c6_r])
            else:
                P.op("dve", lambda e: e.tensor_tensor_scan(out=c6[:, :], data0=tmp[:, :], data1=lft[:, :], initial=carry[:, 0:1], op0=ALU.mult, op1=ALU.add),
                     reads=[lft_r, tmp_r, carry_r], writes=[c6_r])
            P.op("dve", lambda e: e.tensor_copy(out=carry[:, :], in_=c6[:, SG - 1:SG]), reads=[c6_r], writes=[carry_r])
            P.op("dve", lambda e: e.tensor_copy(out=hi[:, :], in_=c6[:, :]), reads=[c6_r], writes=[hi_r])
            P.op("dve", lambda e: e.tensor_tensor(out=r1[:, :], in0=c6[:, :], in1=hi[:, :], op=ALU.subtract), reads=[c6_r, hi_r], writes=[r1_r])
            P.op("dve", lambda e: e.tensor_copy(out=mid[:, :], in_=r1[:, :]), reads=[r1_r], writes=[mid_r])
            P.op("dve", lambda e: e.tensor_tensor(out=r1[:, :], in0=r1[:, :], in1=mid[:, :], op=ALU.subtract), reads=[r1_r, mid_r], writes=[r1_r])
            P.op("dve", lambda e: e.tensor_copy(out=lo[:, :], in_=r1[:, :]), reads=[r1_r], writes=[lo_r])
            for (dst, dst_r, o) in ((qa_t, qa_r, 0), (ka_t, ka_r, 4)):
                P.op("dve", lambda e, o=o: e.tensor_scalar(out=tmp[:, :], in0=hi[:, :], scalar1=coef[:, o:o + 1], scalar2=coef[:, o + 3:o + 4], op0=ALU.mult, op1=ALU.add),
                     reads=[hi_r, coef_r], writes=[tmp_r])
                P.op("dve", lambda e, o=o: e.scalar_tensor_tensor(out=tmp[:, :], in0=mid[:, :], scalar=coef[:, o + 1:o + 2], in1=tmp[:, :], op0=ALU.mult, op1=ALU.add),
                     reads=[mid_r, coef_r, tmp_r], writes=[tmp_r])
                P.op("dve", lambda e, o=o, dst=dst, t0=t0: e.scalar_tensor_tensor(out=dst[:, t0:t0 + SG], in0=lo[:, :], scalar=coef[:, o + 2:o + 3], in1=tmp[:, :], op0=ALU.mult, op1=ALU.add),
                     reads=[lo_r, coef_r, tmp_r], writes=[dst_r])
    LA = 2
    pr = Ring(cx, LA + 2, [128, 512], BF16, "pT")
    rdr = Ring(cx, 2, [128, 512], F32, "rden")
    aor = Ring(cx, 2, [128, 512], BF16, "ao")
    sbanks = [0, 1, 6, 7]
    blocks = []
    unit = 0
    for hh in range(HPC):
        for qt in range(NQT):
            ob = 2 + (unit % 2) * 2
            unit += 1
            nkb = 4 * qt + 4
            for kb in range(nkb):
                blocks.append(dict(hh=hh, qt=qt, kb=kb, nkb=nkb, ob=ob, db=ob + 1, u=unit))

    def stage_a(i):
        bl = blocks[i]
        hh, qt, kb = bl["hh"], bl["qt"], bl["kb"]
        if not aug_done[hh]:
            build_aug(hh)
            aug_done[hh] = True
        q_t, q_r = qs[hh]; k_t, k_r = ks[hh]
        q0 = qt * 512
        j = kb - 4 * qt
        c0 = 128 * j if j > 0 else 0
        n = 512 - c0
        sbank = sbanks[i % len(sbanks)]
        bl.update(c0=c0, n=n, j=j)
        P.op("pe", lambda e: e.matmul(cx.ps[sbank][:, 0:n], lhsT=k_t[:, kb * 128:(kb + 1) * 128], rhs=q_t[:, q0 + c0:q0 + 512], start=True, stop=False),
             reads=[k_r, q_r], writes=[cx.ps_regs[sbank]])
        P.op("pe", lambda e: e.matmul(cx.ps[sbank][:, 0:n], lhsT=ka_t[:, kb * 128:(kb + 1) * 128], rhs=qa_t[:, q0 + c0:q0 + 512], start=False, stop=True),
             reads=[ka_r, qa_r], writes=[cx.ps_regs[sbank]])
        pt, pt_r, _ = pr.next()
        bl.update(pt=pt, pt_r=pt_r)
        P.op("act", lambda e: e.activation(out=pt[:, 0:n], in_=cx.ps[sbank][:, 0:n], func=AF.Exp), reads=[cx.ps_regs[sbank]], writes=[pt_r])
        if j >= 0:
            P.op("pool", lambda e: e.tensor_tensor(out=pt[:, 0:128], in0=pt[:, 0:128], in1=tri[:, :], op=ALU.mult), reads=[pt_r, tri_r], writes=[pt_r])

    def stage_b(i):
        bl = blocks[i]
        hh, qt, kb, nkb, ob, db = bl["hh"], bl["qt"], bl["kb"], bl["nkb"], bl["ob"], bl["db"]
        c0, n, pt, pt_r = bl["c0"], bl["n"], bl["pt"], bl["pt_r"]
        v_t, v_r = vs[hh]
        q0 = qt * 512
        P.op("pe", lambda e: e.matmul(cx.ps[ob][:, c0:512], lhsT=v_t[:, kb, :], rhs=pt[:, 0:n], start=(kb == 0), stop=(kb == nkb - 1)),
             reads=[v_r, pt_r], writes=[cx.ps_regs[ob]])
        P.op("pe", lambda e: e.matmul(cx.ps[db][:, c0:512], lhsT=cx.ones[:, :], rhs=pt[:, 0:n], start=(kb == 0), stop=(kb == nkb - 1)),
             reads=[cx.ones_r, pt_r], writes=[cx.ps_regs[db]])
        if kb == nkb - 1:
            rd, rd_r, _ = rdr.next()
            P.op("dve", lambda e: e.reciprocal(out=rd[:, :], in_=cx.ps[db][:, :]), reads=[cx.ps_regs[db]], writes=[rd_r])
            ao, ao_r, k = aor.next()
            P.op("dve", lambda e: e.tensor_tensor(out=ao[:, :], in0=cx.ps[ob][:, :], in1=rd[:, :], op=ALU.mult), reads=[cx.ps_regs[ob], rd_r], writes=[ao_r])
            P.dma("sp", k, lambda e: e.dma_start(out=aT[hh, :, q0:q0 + 512], in_=ao[:, :]), reads=[ao_r])
    for i in range(len(blocks) + LA):
        if i < len(blocks):
            stage_a(i)
        if i - LA >= 0:
            stage_b(i - LA)
    P.wait_all_dma("sp")
    P.emit()
    return cx.nc


def launch2a(inp, l1res, trace=False):
    nc = build_l2a()
    qT = np.concatenate([np.asarray(l1res[c]["qT"]) for c in range(NCORES)], axis=2)
    kT = np.concatenate([np.asarray(l1res[c]["kT"]) for c in range(NCORES)], axis=2)
    V = np.concatenate([np.asarray(l1res[c]["V"]) for c in range(NCORES)], axis=0)
    lf = np.concatenate([np.asarray(l1res[c]["logf"]) for c in range(NCORES)], axis=1)
    coef = np.zeros((6, 8), np.float32)
    coef[0, 0] = 1; coef[1, 1] = 1; coef[2, 2] = 1; coef[3:6, 3] = 1
    coef[3, 4] = -1; coef[4, 5] = -1; coef[5, 6] = -1; coef[0:3, 7] = 1
    tri = np.triu(np.ones((128, 128), np.float32))
    maps = []
    for c in range(NCORES):
        hs = slice(c * HPC, (c + 1) * HPC)
        Vh = V.reshape(NKB, 128, NH, 128)[:, :, hs].transpose(2, 1, 0, 3)
        maps.append(dict(qT=np.ascontiguousarray(qT[hs]), kT=np.ascontiguousarray(kT[hs]), V=np.ascontiguousarray(Vh),
                         lf6=np.ascontiguousarray(np.repeat(lf[hs][:, None, :], 6, axis=1)), coef=coef, tri=tri))
    return run(nc, maps, trace)


def build_bc(layer):
    cx = Cx()
    P = cx.P
    xT = cx.din("xT", [D, T])
    hc = cx.din("hcatT", [D, T], BF16)
    w_out = cx.din("w_out", [D, D])
    fg_d = cx.din("fgcol", [128, KC_D])
    w_up = cx.din("w_up", [D, 2 * DFF])
    x1T = cx.dout("x1T", [D, T])
    gT = cx.dout("gT", [DFF, T], BF16)
    uT = cx.dout("uT", [DFF, T], BF16)
    cx.eps_t = cx.sb([128, 1], F32, "eps")
    cx.eps_r = Reg()
    P.op("pool", lambda e: e.memset(cx.eps_t[:, :], EPS), writes=[cx.eps_r])
    fg, fg_r = cx.load(fg_d, [128, KC_D])
    h = cx.sb([128, KC_D, T], BF16, "h")
    h_regs = [Reg() for _ in range(KC_D)]
    hv = hc.rearrange("(c p) t -> p c t", p=128)
    if layer == 0:
        for q in range(4):
            P.dma("sp", cx.uid("hl"), lambda e, q=q: e.dma_start(out=h[:, q * 8:(q + 1) * 8, :], in_=hv[:, q * 8:(q + 1) * 8, :]),
                  writes=h_regs[q * 8:(q + 1) * 8])
    else:
        for q in range(2, 4):
            P.dma("sp", cx.uid("hl"), lambda e, q=q: e.dma_start(out=h[:, q * 8:(q + 1) * 8, :], in_=hv[:, q * 8:(q + 1) * 8, :]),
                  writes=h_regs[q * 8:(q + 1) * 8])
        cv_d = cx.din("cvx", [MIX, 2 + T], BF16)
        gb_d = cx.din("gbT", [MIX, T], BF16)
        cw_d = cx.din("cwcol", [128, 16, 3])
        cw, cw_r = cx.load(cw_d, [128, 16, 3])
        cvr = Ring(cx, 2, [128, 2 + T], BF16, "cv")
        gbr = Ring(cx, 2, [128, T], BF16, "gb")
        yr = Ring(cx, 2, [128, T], F32, "cy")
        for j in range(16):
            cvt, cvt_r, k1 = cvr.next()
            gbt, gbt_r, k2 = gbr.next()
            y, y_r, _ = yr.next()
            P.dma("sp", k1, lambda e, cvt=cvt, j=j: e.dma_start(out=cvt[:, :], in_=cv_d[j * 128:(j + 1) * 128, :]), writes=[cvt_r])
            P.dma("sp", k2, lambda e, gbt=gbt, j=j: e.dma_start(out=gbt[:, :], in_=gb_d[j * 128:(j + 1) * 128, :]), writes=[gbt_r])
            P.op("dve", lambda e, y=y, cvt=cvt, j=j: e.tensor_scalar(out=y[:, :], in0=cvt[:, 2:2 + T], scalar1=cw[:, j, 2:3], scalar2=None, op0=ALU.mult),
                 reads=[cvt_r, cw_r], writes=[y_r])
            P.op("dve", lambda e, y=y, cvt=cvt, j=j: e.scalar_tensor_tensor(out=y[:, :], in0=cvt[:, 1:1 + T], scalar=cw[:, j, 1:2], in1=y[:, :], op0=ALU.mult, op1=ALU.add),
                 reads=[cvt_r, cw_r, y_r], writes=[y_r])
            P.op("dve", lambda e, y=y, cvt=cvt, j=j: e.scalar_tensor_tensor(out=y[:, :], in0=cvt[:, 0:T], scalar=cw[:, j, 0:1], in1=y[:, :], op0=ALU.mult, op1=ALU.add),
                 reads=[cvt_r, cw_r, y_r], writes=[y_r])
            P.op("dve", lambda e, y=y, gbt=gbt, j=j: e.tensor_tensor(out=h[:, j, :], in0=y[:, :], in1=gbt[:, :], op=ALU.mult),
                 reads=[y_r, gbt_r], writes=[h_regs[j]])
    ws = WStream(cx, KC_D, 384)
    subtiles = [("s0", h, h_regs, 0, 512), ("s1", h, h_regs, 512, 512)]
    x1_regs = [Reg() for _ in range(KC_D)]
    xr = Ring(cx, 3, [128, 512], F32, "xc")
    orr_ = Ring(cx, 3, [128, 512], F32, "xo")

    def epi_res(gi, g, stn, tok0, n, bset):
        for mi in range(len(g["widths"])):
            ch = g["ch0"] + mi
            b = bset[mi]
            xt, xt_r, k = xr.next()
            P.dma("sp", k, lambda e, xt=xt, ch=ch: e.dma_start(out=xt[:, :], in_=xT[ch * 128:(ch + 1) * 128, tok0:tok0 + n]), writes=[xt_r])
            o, o_r, k2 = orr_.next()
            P.op("dve", lambda e, o=o, xt=xt, b=b: e.tensor_tensor(out=o[:, :], in0=cx.ps[b][:, :], in1=xt[:, :], op=ALU.add),
                 reads=[cx.ps_regs[b], xt_r], writes=[o_r])
            P.dma("sp", k2, lambda e, o=o, ch=ch: e.dma_start(out=x1T[ch * 128:(ch + 1) * 128, tok0:tok0 + n], in_=o[:, :]),
                  reads=[o_r], writes=[x1_regs[ch]])
    groups = []
    for ch0 in range(0, KC_D, 3):
        nchk = min(3, KC_D - ch0)
        groups.append(dict(kind="fm", segs=[(0, ch0 * 128, nchk * 128)], widths=[128] * nchk, epi=epi_res, ch0=ch0))
    gemm(cx, ws, w_out, KC_D, groups, subtiles)
    load_norm(cx, x1T, fg, fg_r, T, h, h_regs, x_regs=x1_regs)
    gur = Ring(cx, 4, [128, 512], BF16, "gu")

    def epi_gu(gi, g, stn, tok0, n, bset):
        for mi in range(len(g["widths"])):
            ch = g["ch0"] + mi
            dst = gT if ch < NFC else uT
            row = (ch % NFC) * 128
            b = bset[mi]
            o, o_r, k = gur.next()
            P.op("act", lambda e, o=o, b=b: e.activation(out=o[:, :], in_=cx.ps[b][:, :], func=AF.Copy), reads=[cx.ps_regs[b]], writes=[o_r])
            P.dma("sp", k, lambda e, o=o, dst=dst, row=row: e.dma_start(out=dst[row:row + 128, tok0:tok0 + n], in_=o[:, :]), reads=[o_r])
    groups = []
    for ch0 in range(0, 2 * NFC, 3):
        nchk = min(3, 2 * NFC - ch0)
        groups.append(dict(kind="fm", segs=[(0, ch0 * 128, nchk * 128)], widths=[128] * nchk, epi=epi_gu, ch0=ch0))
    gemm(cx, ws, w_up, KC_D, groups, subtiles)
    P.wait_all_dma("sp")
    P.emit()
    return cx.nc


HK = NFC // 2


def build_d():
    cx = Cx()
    P = cx.P
    x1T = cx.din("x1T", [D, T])
    gx = cx.din("gx", [DFF, 2 + T], BF16)
    uT = cx.din("uT", [DFF, T], BF16)
    cw_d = cx.din("cwcol", [128, NFC, 3])
    w_dn = cx.din("w_down", [DFF, D])
    x2T = cx.dout("x2T", [D, T])
    cw, cw_r = cx.load(cw_d, [128, NFC, 3])
    act = cx.sb([128, HK, T], BF16, "act")
    act_regs = [Reg() for _ in range(HK)]
    ws = WStream(cx, HK, 256)
    gr = Ring(cx, 3, [128, 2 + T], BF16, "g")
    ur = Ring(cx, 3, [128, T], BF16, "u")
    yr = Ring(cx, 3, [128, T], F32, "y")
    xr = Ring(cx, 4, [128, 512], F32, "xc")
    orr_ = Ring(cx, 4, [128, 512], F32, "xo")
    x2_regs = [[Reg(), Reg()] for _ in range(KC_D)]
    for hf in range(2):
        for kc in range(HK):
            ch = hf * HK + kc
            gt, gt_r, k1 = gr.next()
            ut, ut_r, k2 = ur.next()
            y, y_r, _ = yr.next()
            P.dma("sp", k1, lambda e, gt=gt, ch=ch: e.dma_start(out=gt[:, :], in_=gx[ch * 128:(ch + 1) * 128, :]), writes=[gt_r])
            P.dma("sp", k2, lambda e, ut=ut, ch=ch: e.dma_start(out=ut[:, :], in_=uT[ch * 128:(ch + 1) * 128, :]), writes=[ut_r])
            P.op("act", lambda e, y=y, gt=gt, ch=ch: e.activation(out=y[:, :], in_=gt[:, 2:2 + T], func=AF.Copy, scale=cw[:, ch, 2:3]),
                 reads=[gt_r, cw_r], writes=[y_r])
            P.op("dve", lambda e, y=y, gt=gt, ch=ch: e.scalar_tensor_tensor(out=y[:, :], in0=gt[:, 1:1 + T], scalar=cw[:, ch, 1:2], in1=y[:, :], op0=ALU.mult, op1=ALU.add),
                 reads=[gt_r, cw_r, y_r], writes=[y_r])
            P.op("dve", lambda e, y=y, gt=gt, ch=ch: e.scalar_tensor_tensor(out=y[:, :], in0=gt[:, 0:T], scalar=cw[:, ch, 0:1], in1=y[:, :], op0=ALU.mult, op1=ALU.add),
                 reads=[gt_r, cw_r, y_r], writes=[y_r])
            P.op("act", lambda e, y=y: e.activation(out=y[:, :], in_=y[:, :], func=AF.Silu), reads=[y_r], writes=[y_r])
            P.op("dve", lambda e, y=y, ut=ut, kc=kc: e.tensor_tensor(out=act[:, kc, :], in0=y[:, :], in1=ut[:, :], op=ALU.mult),
                 reads=[y_r, ut_r], writes=[act_regs[kc]])
        src = x1T if hf == 0 else x2T

        def epi_res(gi, g, stn, tok0, n, bset, src=src, hf=hf):
            sti = 0 if stn == "s0" else 1
            for mi in range(2):
                ch = g["ch0"] + mi
                b = bset[mi]
                xt, xt_r, k = xr.next()
                P.dma("sp", k, lambda e, xt=xt, ch=ch: e.dma_start(out=xt[:, :], in_=src[ch * 128:(ch + 1) * 128, tok0:tok0 + n]),
                      reads=([x2_regs[ch][sti]] if hf == 1 else []), writes=[xt_r])
                o, o_r, k2 = orr_.next()
                P.op("dve", lambda e, o=o, xt=xt, b=b: e.tensor_tensor(out=o[:, :], in0=cx.ps[b][:, :], in1=xt[:, :], op=ALU.add),
                     reads=[cx.ps_regs[b], xt_r], writes=[o_r])
                P.dma("sp", k2, lambda e, o=o, ch=ch: e.dma_start(out=x2T[ch * 128:(ch + 1) * 128, tok0:tok0 + n], in_=o[:, :]),
                      reads=[o_r], writes=[x2_regs[ch][sti]])
        groups = [dict(kind="fm", segs=[(0, ch0 * 128, 256)], widths=[128] * 2, epi=epi_res, ch0=ch0) for ch0 in range(0, KC_D, 2)]
        gemm(cx, ws, w_dn[hf * HK * 128:(hf + 1) * HK * 128, :], HK, groups,
             [("s0", act, act_regs, 0, 512), ("s1", act, act_regs, 512, 512)])
    P.wait_all_dma("sp")
    P.emit()
    return cx.nc


def build_e():
    cx = Cx()
    P = cx.P
    x2T = cx.din("xT", [D, T])
    g_d = cx.din("gcol", [128, KC_D])
    W = cx.din("w_in", [D, 10240])
    wT_d = cx.din("sgu_wT", [128, 16, 128])
    tri_d = cx.din("tri", [128, 128])
    bsb_d = cx.din("bsb", [128, 16, 128])
    ngb_d = cx.din("ngb", [128, MIX])
    cvT = cx.dout("cvT", [MIX, T], BF16)
    gbT = cx.dout("gbT", [MIX, T], BF16)
    dT = cx.dout("dT", [MIX, T], BF16)
    cx.eps_t = cx.sb([128, 1], F32, "eps")
    cx.eps_r = Reg()
    P.op("pool", lambda e: e.memset(cx.eps_t[:, :], EPS), writes=[cx.eps_r])
    gcol, gcol_r = cx.load(g_d, [128, KC_D])
    tri, tri_r = cx.load(tri_d, [128, 128])
    bsb, bsb_r = cx.load(bsb_d, [128, 16, 128])
    ngb, ngb_r = cx.load(ngb_d, [128, MIX])
    wtm, wtm_r = cx.load(wT_d, [128, 16, 128], BF16, "wtm", eng="pool")
    for hh in range(16):
        P.op("pool", lambda e, hh=hh: e.tensor_tensor(out=wtm[:, hh, :], in0=wtm[:, hh, :], in1=tri[:, :], op=ALU.mult),
             reads=[wtm_r, tri_r], writes=[wtm_r])
    vgel = [cx.sb([128, MIX], BF16, "vgel") for _ in range(8)]
    vgel_r = [Reg() for _ in range(8)]
    ug = cx.sb([128, 16, T], BF16, "ug")
    ug_r = [Reg() for _ in range(16)]
    h = cx.sb([128, KC_D, T], BF16, "h")
    h_regs = [Reg() for _ in range(KC_D)]
    load_norm(cx, x2T, gcol, gcol_r, T, h, h_regs, rings=(Ring(cx, 2, [128, T], F32, "xs"), Ring(cx, 2, [128, T], BF16, "sq")))
    ws = WStream(cx, KC_D, 256)
    subtiles = [("s0", h, h_regs, 0, 512), ("s1", h, h_regs, 512, 512)]

    def epi_zv(gi, g, stn, tok0, n, bset):
        c0 = g["c0"]
        for tb in range(4):
            tbg = tok0 // 128 + tb
            b = bset[tb]
            P.op("act", lambda e, tbg=tbg, b=b: e.activation(out=vgel[tbg][:, c0:c0 + 256], in_=cx.ps[b][:, 0:256], func=AF.Gelu),
                 reads=[cx.ps_regs[b]], writes=[vgel_r[tbg]])

    def epi_zu(gi, g, stn, tok0, n, bset):
        for mi in range(2):
            ch = g["ch0"] + mi
            b = bset[mi]
            P.op("act", lambda e, ch=ch, b=b: e.activation(out=ug[:, ch, tok0:tok0 + n], in_=cx.ps[b][:, :], func=AF.Gelu),
                 reads=[cx.ps_regs[b]], writes=[ug_r[ch]])
    tmr = Ring(cx, 2, [128, 512], F32, "cvtmp")
    cvr = Ring(cx, 3, [128, 512], BF16, "cvo")

    def epi_cv(gi, g, stn, tok0, n, bset):
        j = g["j"]
        tm, tm_r, _ = tmr.next()
        P.op("act", lambda e: e.activation(out=tm[:, :], in_=cx.ps[bset[0]][:, :], func=AF.Copy), reads=[cx.ps_regs[bset[0]]], writes=[tm_r])
        o, o_r, k = cvr.next()
        P.op("dve", lambda e: e.tensor_tensor(out=o[:, :], in0=cx.ps[bset[1]][:, :], in1=tm[:, :], op=ALU.mult),
             reads=[cx.ps_regs[bset[1]], tm_r], writes=[o_r])
        P.dma("sp", k, lambda e: e.dma_start(out=cvT[j * 128:(j + 1) * 128, tok0:tok0 + n], in_=o[:, :]), reads=[o_r])

    def epi_gb(gi, g, stn, tok0, n, bset):
        for mi in range(2):
            ch = g["ch0"] + mi
            b = bset[mi]
            o, o_r, k = cvr.next()
            P.op("act", lambda e, o=o, b=b: e.activation(out=o[:, :], in_=cx.ps[b][:, :], func=AF.Copy), reads=[cx.ps_regs[b]], writes=[o_r])
            P.dma("sp", k, lambda e, o=o, ch=ch: e.dma_start(out=gbT[ch * 128:(ch + 1) * 128, tok0:tok0 + n], in_=o[:, :]), reads=[o_r])
    groups = []
    for vg in range(8):
        groups.append(dict(kind="tm", segs=[(0, 8192 + vg * 256, 256)], gw=256, epi=epi_zv, c0=vg * 256))
    for ch0 in range(0, 16, 2):
        groups.append(dict(kind="fm", segs=[(0, 6144 + ch0 * 128, 256)], widths=[128] * 2, epi=epi_zu, ch0=ch0))
    for j in range(16):
        groups.append(dict(kind="fm", segs=[(0, j * 128, 128), (128, 4096 + j * 128, 128)], widths=[128] * 2, epi=epi_cv, j=j))
    for ch0 in range(0, 16, 2):
        groups.append(dict(kind="fm", segs=[(0, 2048 + ch0 * 128, 256)], widths=[128] * 2, epi=epi_gb, ch0=ch0))
    gemm(cx, ws, W, KC_D, groups, subtiles)
    def hflat(c):
        return h[:, c:c + 2, :].rearrange("p a b -> p (a b)"), [h_regs[c], h_regs[c + 1]]
    junk, junk_r = hflat(0)
    vns = [hflat(2), hflat(4)]
    dous = [hflat(6), hflat(8)]
    ss = cx.sb([128, 8], F32, "ss")
    ss_r = Reg()
    for tbg in range(8):
        P.op("act", lambda e, tbg=tbg: e.activation(out=junk, in_=vgel[tbg][:, :], func=AF.Square, accum_out=ss[:, tbg:tbg + 1]),
             reads=[vgel_r[tbg]], writes=junk_r + [ss_r])
    P.op("act", lambda e: e.activation(out=ss[:, :], in_=ss[:, :], func=AF.Sqrt, bias=cx.eps_t[:, 0:1], scale=1.0 / MIX),
         reads=[ss_r, cx.eps_r], writes=[ss_r])
    P.op("dve", lambda e: e.reciprocal(out=ss[:, :], in_=ss[:, :]), reads=[ss_r], writes=[ss_r])
    t1r = Ring(cx, 3, [128, 128], F32, "t1")
    dkeys = [cx.uid("dk"), cx.uid("dk")]
    dTv = dT.rearrange("(c p) t -> p c t", p=128)
    for tbg in range(8):
        vn, vn_r = vns[tbg % 2]
        do, do_r = dous[tbg % 2]
        P.op("dve", lambda e, vn=vn, tbg=tbg: e.scalar_tensor_tensor(out=vn, in0=vgel[tbg][:, :], scalar=ss[:, tbg:tbg + 1], in1=ngb[:, :],
                                                                    op0=ALU.mult, op1=ALU.mult),
             reads=[vgel_r[tbg], ss_r, ngb_r], writes=vn_r)
        bset = [0, 1, 2, 3] if tbg % 2 == 0 else [4, 5, 6, 7]
        for hh in range(16):
            b = bset[hh // 4]
            cc = (hh % 4) * 128
            P.op("pe", lambda e, vn=vn, hh=hh, b=b, cc=cc: e.matmul(cx.ps[b][:, cc:cc + 128], lhsT=vn[:, hh * 128:(hh + 1) * 128], rhs=wtm[:, hh, :],
                                                                   start=True, stop=True),
                 reads=vn_r + [wtm_r], writes=[cx.ps_regs[b]])
        for hh in range(16):
            b = bset[hh // 4]
            cc = (hh % 4) * 128
            t1, t1_r, _ = t1r.next()
            P.op("dve", lambda e, t1=t1, hh=hh, b=b, cc=cc: e.tensor_tensor(out=t1[:, :], in0=cx.ps[b][:, cc:cc + 128], in1=bsb[:, hh, :], op=ALU.add),
                 reads=[cx.ps_regs[b], bsb_r], writes=[t1_r])
            P.op("pool", lambda e, t1=t1, hh=hh, do=do, tbg=tbg: e.tensor_tensor(out=do[:, hh * 128:(hh + 1) * 128], in0=t1[:, :],
                                                                               in1=ug[:, hh, tbg * 128:(tbg + 1) * 128], op=ALU.mult),
                 reads=[t1_r, ug_r[hh]], writes=do_r)
        P.dma("sp", dkeys[tbg % 2], lambda e, do=do, tbg=tbg: e.dma_start(out=dTv[:, :, tbg * 128:(tbg + 1) * 128],
                                                                        in_=do.rearrange("p (c t) -> p c t", t=128)), reads=do_r)
    P.wait_all_dma("sp")
    P.emit()
    return cx.nc


def _cat(res, key, axis):
    return np.concatenate([np.asarray(res[c][key]) for c in range(NCORES)], axis=axis)


def _halo(full, c, n):
    if c == 0:
        return np.ascontiguousarray(np.concatenate([np.zeros((full.shape[0], n), full.dtype), full[:, :T]], axis=1))
    return np.ascontiguousarray(full[:, c * T - n:(c + 1) * T])


def launch_bc(layer, xT_l, hcat_l, w_out, fnorm, w_up, extra=None):
    nc = build_bc(layer)
    w_out = np.ascontiguousarray(np.asarray(w_out, np.float32))
    w_up = np.ascontiguousarray(np.asarray(w_up, np.float32))
    fg = colmajor(fnorm)
    maps = []
    for c in range(NCORES):
        m = dict(xT=xT_l[c], hcatT=hcat_l[c], w_out=w_out, fgcol=fg, w_up=w_up)
        if extra is not None:
            m.update(extra[c])
        maps.append(m)
    return run(nc, maps).results


def launch_d(x1_l, res_bc, conv, w_down):
    nc = build_d()
    gfull = _cat(res_bc, "gT", 1)
    cw = np.ascontiguousarray(np.asarray(conv, np.float32).T.reshape(NFC, 128, 3).transpose(1, 0, 2))
    w_down = np.ascontiguousarray(np.asarray(w_down, np.float32))
    maps = [dict(x1T=x1_l[c], gx=_halo(gfull, c, 2), uT=np.asarray(res_bc[c]["uT"]), cwcol=cw, w_down=w_down) for c in range(NCORES)]
    return run(nc, maps).results


def kernel(**inp):
    x = np.asarray(inp["x"], np.float32)[0]
    xT_l = [np.ascontiguousarray(x[c * T:(c + 1) * T].T) for c in range(NCORES)]
    tri = np.triu(np.ones((128, 128), np.float32))
    r1 = launch1(inp).results
    r2 = launch2a(inp, r1).results
    aT = np.concatenate([np.asarray(r2[c]["aT"]) for c in range(NCORES)], axis=0).reshape(MIX, SEQ)
    hcat_l = [np.ascontiguousarray(np.concatenate([aT[:, c * T:(c + 1) * T], np.asarray(r1[c]["pmT"])], axis=0)) for c in range(NCORES)]
    rbc = launch_bc(0, xT_l, hcat_l, inp["l0_w_out"], inp["l0_ffn_norm"], inp["l0_ffn_w_up"])
    x1_l = [np.asarray(rbc[c]["x1T"]) for c in range(NCORES)]
    rd = launch_d(x1_l, rbc, inp["l0_ffn_conv"], inp["l0_ffn_w_down"])
    x2_l = [np.asarray(rd[c]["x2T"]) for c in range(NCORES)]
    nce = build_e()
    w_in1 = np.ascontiguousarray(np.asarray(inp["l1_w_in"], np.float32))
    wT = np.ascontiguousarray(np.asarray(inp["l1_sgu_w"], np.float32).transpose(2, 0, 1))
    bsb = np.ascontiguousarray(np.broadcast_to(np.asarray(inp["l1_sgu_b"], np.float32)[None], (128, 16, 128)))
    ngb = np.ascontiguousarray(np.broadcast_to(np.asarray(inp["l1_sgu_norm"], np.float32)[None], (128, MIX)))
    g1 = colmajor(inp["l1_mix_norm"])
    re_ = run(nce, [dict(xT=x2_l[c], gcol=g1, w_in=w_in1, sgu_wT=wT, tri=tri, bsb=bsb, ngb=ngb) for c in range(NCORES)]).results
    cvfull = _cat(re_, "cvT", 1)
    cw1 = np.ascontiguousarray(np.asarray(inp["l1_conv_w"], np.float32).T.reshape(16, 128, 3).transpose(1, 0, 2))
    hcat1 = [np.ascontiguousarray(np.concatenate([np.zeros((MIX, T), NPBF), np.asarray(re_[c]["dT"])], axis=0)) for c in range(NCORES)]
    extra = [dict(cvx=_halo(cvfull, c, 2), gbT=np.asarray(re_[c]["gbT"]), cwcol=cw1) for c in range(NCORES)]
    rbc1 = launch_bc(1, x2_l, hcat1, inp["l1_w_out"], inp["l1_ffn_norm"], inp["l1_ffn_w_up"], extra)
    x3_l = [np.asarray(rbc1[c]["x1T"]) for c in range(NCORES)]
    rd1 = launch_d(x3_l, rbc1, inp["l1_ffn_conv"], inp["l1_ffn_w_down"])
    out = np.concatenate([np.asarray(rd1[c]["x2T"]).T for c in range(NCORES)], axis=0)
    return np.ascontiguousarray(out[None].astype(np.float32))
```

```python
import numpy as np
import ml_dtypes
import concourse.bass as bass
import concourse.mybir as mybir
from concourse.bass_utils import run_bass_kernel_spmd

F32 = mybir.dt.float32
BF16 = mybir.dt.bfloat16
AF = mybir.ActivationFunctionType
ALU = mybir.AluOpType
NPBF = ml_dtypes.bfloat16

NCORES = 8
SEQ = 8192
D = 4096
T = SEQ // NCORES
KC_D = D // 128
MIX = 2048
NH = 16
DFF = 11008
NFC = DFF // 128
EPS = 1e-6
ENGS = ("pe", "act", "dve", "pool", "sp")


class Reg:
    __slots__ = ("w", "rs")

    def __init__(self):
        self.w = None
        self.rs = {}


class Op:
    __slots__ = ("fn", "deps", "signal", "dma_sem", "val", "ndma")

    def __init__(self, fn, deps, dma_sem=None, ndma=1):
        self.fn = fn
        self.deps = deps
        self.signal = False
        self.dma_sem = dma_sem
        self.val = None
        self.ndma = ndma


class Prog:
    def __init__(self, nc):
        self.nc = nc
        self.ops = {e: [] for e in ENGS}
        self.dma_cnt = {}
        self.dma_sems = {}
        self.eng_sems = {}

    def _collect(self, eng, reads, writes):
        deps = {}

        def add(tok):
            if tok is None:
                return
            c, s = tok
            if c == "pe" and eng == "pe":
                return
            if deps.get(c, -1) < s:
                deps[c] = s
        for r in reads:
            add(r.w)
        for w in writes:
            add(w.w)
            for c, s in w.rs.items():
                add((c, s))
        return deps

    def _commit(self, tok, reads, writes):
        c, s = tok
        for r in reads:
            if r.rs.get(c, -1) < s:
                r.rs[c] = s
        for w in writes:
            w.w = tok
            w.rs = {}

    def op(self, eng, fn, reads=(), writes=()):
        deps = self._collect(eng, reads, writes)
        idx = len(self.ops[eng])
        self.ops[eng].append(Op(fn, deps))
        tok = (eng, idx)
        self._commit(tok, reads, writes)
        return tok

    def dma(self, eng, semkey, fn, reads=(), writes=(), n=1):
        deps = self._collect(eng, reads, writes)
        cnt = self.dma_cnt.get(semkey, 0) + n
        self.dma_cnt[semkey] = cnt
        self.ops[eng].append(Op(fn, deps, dma_sem=semkey, ndma=n))
        tok = (("dma", semkey), cnt)
        self._commit(tok, reads, writes)
        return tok

    def wait_all_dma(self, eng):
        deps = {("dma", k): v for k, v in self.dma_cnt.items()}
        self.ops[eng].append(Op(None, deps))

    def emit(self):
        nc = self.nc
        for e in ENGS:
            for o in self.ops[e]:
                for c, s in o.deps.items():
                    if isinstance(c, str):
                        self.ops[c][s].signal = True
        for e in ENGS:
            n = 0
            for o in self.ops[e]:
                if o.dma_sem is None and o.signal:
                    n += 1
                    o.val = n
        for e in ENGS:
            self.eng_sems[e] = nc.alloc_semaphore(name=f"s_{e}")
        for k in self.dma_cnt:
            self.dma_sems[k] = nc.alloc_semaphore(name=f"d_{k}")
        prog = self

        def run(e, eng):
            waited = {}
            for o in prog.ops[e]:
                for c, s in o.deps.items():
                    if isinstance(c, str):
                        v = prog.ops[c][s].val
                        sem = prog.eng_sems[c]
                    else:
                        v = 16 * s
                        sem = prog.dma_sems[c[1]]
                    if waited.get(c, 0) < v:
                        eng.wait_ge(sem, v)
                        waited[c] = v
                if o.fn is None:
                    continue
                ins = o.fn(eng)
                if o.dma_sem is not None:
                    if not isinstance(ins, (list, tuple)):
                        ins = [ins]
                    assert len(ins) == o.ndma
                    for i in ins:
                        i.then_inc(prog.dma_sems[o.dma_sem], 16)
                elif o.signal:
                    ins.then_inc(prog.eng_sems[e], 1)

        with nc.Block() as block:
            @block.tensor
            def _(eng):
                run("pe", eng)

            @block.scalar
            def _(eng):
                run("act", eng)

            @block.vector
            def _(eng):
                run("dve", eng)

            @block.gpsimd
            def _(eng):
                run("pool", eng)

            @block.sync
            def _(eng):
                run("sp", eng)


class Cx:
    def __init__(self):
        self.nc = bass.Bass("TRN2", target_bir_lowering=False)
        self.P = Prog(self.nc)
        self.n = 0
        self.ps = [self.nc.alloc_psum_tensor(f"ps{i}", [128, 512], F32) for i in range(8)]
        self.ps_regs = [Reg() for _ in range(8)]
        self.pe_defer = []
        self.ones = self.sb([128, 128], BF16, "ones")
        self.ones_r = Reg()
        self.P.op("pool", lambda e: e.memset(self.ones[:, :], 1.0), writes=[self.ones_r])

    def uid(self, s):
        self.n += 1
        return f"{s}{self.n}"

    def sb(self, shape, dt, name="t"):
        return self.nc.alloc_sbuf_tensor(self.uid(name), list(shape), dt)

    def din(self, name, shape, dt=F32):
        return self.nc.dram_tensor(name, list(shape), dt, kind="ExternalInput").ap()

    def dout(self, name, shape, dt=F32):
        return self.nc.dram_tensor(name, list(shape), dt, kind="ExternalOutput").ap()

    def flush_pe(self):
        d = self.pe_defer
        self.pe_defer = []
        for f in d:
            f()

    def load(self, dram_ap, shape, dt=F32, name="c", eng="sp"):
        t = self.sb(shape, dt, name)
        r = Reg()
        sl = tuple(slice(None) for _ in shape)
        self.P.dma(eng, self.uid("ld"), lambda e: e.dma_start(out=t[sl], in_=dram_ap), writes=[r])
        return t, r


class Ring:
    def __init__(self, cx, n, shape, dt, name="r"):
        self.tiles = [cx.sb(shape, dt, name) for _ in range(n)]
        self.regs = [Reg() for _ in range(n)]
        self.keys = [cx.uid(name + "k") for _ in range(n)]
        self.i = 0
        self.n = n

    @classmethod
    def over(cls, cx, tiles, regs, name="r"):
        o = cls.__new__(cls)
        o.tiles = tiles
        o.regs = regs
        o.keys = [cx.uid(name + "k") for _ in tiles]
        o.i = 0
        o.n = len(tiles)
        return o

    def next(self):
        j = self.i % self.n
        self.i += 1
        return self.tiles[j], self.regs[j], self.keys[j]


def load_norm(cx, xT, gcol, gcol_r, Tn, h, h_regs, rings=None, x_regs=None):
    P = cx.P
    nb = (Tn + 511) // 512
    xs = Ring(cx, 3, [128, Tn], F32, "xs") if rings is None else rings[0]
    sq = Ring(cx, 2, [128, Tn], BF16, "sq") if rings is None else rings[1]
    rstd = cx.sb([128, Tn], F32, "rstd")
    rstd_r = Reg()
    for c in range(KC_D):
        t, r, k = xs.next()
        P.dma("sp", k, lambda e, t=t, c=c: e.dma_start(out=t[:, 0:Tn], in_=xT[c * 128:(c + 1) * 128, :]), reads=([x_regs[c]] if x_regs else []), writes=[r])
        s, sr, _ = sq.next()
        P.op("act", lambda e, t=t, s=s: e.activation(out=s[:, 0:Tn], in_=t[:, 0:Tn], func=AF.Square), reads=[r], writes=[sr])
        for j in range(nb):
            n = min(512, Tn - j * 512)
            P.op("pe", lambda e, s=s, j=j, n=n, c=c: e.matmul(cx.ps[j][:, 0:n], lhsT=cx.ones[:, :], rhs=s[:, j * 512:j * 512 + n],
                                                          start=(c == 0), stop=(c == KC_D - 1)),
                 reads=[sr, cx.ones_r], writes=[cx.ps_regs[j]])
    for j in range(nb):
        n = min(512, Tn - j * 512)
        P.op("act", lambda e, j=j, n=n: e.activation(out=rstd[:, j * 512:j * 512 + n], in_=cx.ps[j][:, 0:n], func=AF.Sqrt,
                                                   bias=cx.eps_t[:, 0:1], scale=1.0 / D),
             reads=[cx.ps_regs[j], cx.eps_r], writes=[rstd_r])
    P.op("dve", lambda e: e.reciprocal(out=rstd[:, :], in_=rstd[:, :]), reads=[rstd_r], writes=[rstd_r])
    for c in range(KC_D):
        t, r, k = xs.next()
        P.dma("sp", k, lambda e, t=t, c=c: e.dma_start(out=t[:, 0:Tn], in_=xT[c * 128:(c + 1) * 128, :]), reads=([x_regs[c]] if x_regs else []), writes=[r])
        P.op("dve", lambda e, t=t, c=c: e.scalar_tensor_tensor(out=h[:, c, :], in0=t[:, 0:Tn], scalar=gcol[:, c:c + 1], in1=rstd[:, :],
                                                              op0=ALU.mult, op1=ALU.mult),
             reads=[r, gcol_r, rstd_r], writes=[h_regs[c]])
    return xs, sq


class WStream:
    def __init__(self, cx, kcmax, gw, nslots=2):
        self.cx = cx
        self.gw = gw
        self.kcmax = kcmax
        self.nparts = (kcmax + 7) // 8
        self.slots = [cx.sb([128, kcmax, gw], BF16, "ws") for _ in range(nslots)]
        self.regs = [[Reg() for _ in range(self.nparts)] for _ in range(nslots)]
        self.keys = [[cx.uid("wk") for _ in range(self.nparts)] for _ in range(nslots)]
        self.i = 0
        self.nslots = nslots

    def load(self, W, kc_n, segs):
        cx = self.cx
        s = self.i % self.nslots
        self.i += 1
        Wv = W.rearrange("(kc p) n -> p kc n", p=128)
        for q in range((kc_n + 7) // 8):
            k0, k1 = q * 8, min(kc_n, q * 8 + 8)

            def fn(e, s=s, k0=k0, k1=k1):
                return [e.dma_start(out=self.slots[s][:, k0:k1, d0:d0 + w], in_=Wv[:, k0:k1, c0:c0 + w]) for (d0, c0, w) in segs]
            cx.P.dma("pool", self.keys[s][q], fn, writes=[self.regs[s][q]], n=len(segs))
        return s


def gemm(cx, ws, W, kc_n, groups, subtiles):
    P = cx.P
    unit = cx.unit if hasattr(cx, "unit") else 0
    for gi, g in enumerate(groups):
        s = ws.load(W, kc_n, g["segs"])
        slot = ws.slots[s]
        for (stn, ht, hr, tok0, n) in subtiles:
            if stn == "halo" and not g.get("halo"):
                continue
            bset = [0, 1, 2, 3] if unit % 2 == 0 else [4, 5, 6, 7]
            unit += 1
            if g["kind"] == "fm":
                widths = g["widths"]
                for kc in range(kc_n):
                    for mi, wd in enumerate(widths):
                        b = bset[mi]
                        P.op("pe", lambda e, b=b, kc=kc, mi=mi, wd=wd, ht=ht, tok0=tok0, n=n, slot=slot: e.matmul(
                            cx.ps[b][0:wd, 0:n], lhsT=slot[:, kc, mi * 128:mi * 128 + wd], rhs=ht[:, kc, tok0:tok0 + n],
                            start=(kc == 0), stop=(kc == kc_n - 1)),
                            reads=[ws.regs[s][kc // 8], hr[kc]], writes=[cx.ps_regs[b]])
                nb = len(widths)
            else:
                ntb = n // 128
                gwid = g["gw"]
                for kc in range(kc_n):
                    for tb in range(ntb):
                        b = bset[tb]
                        P.op("pe", lambda e, b=b, kc=kc, tb=tb, ht=ht, tok0=tok0, slot=slot, gwid=gwid: e.matmul(
                            cx.ps[b][:, 0:gwid], lhsT=ht[:, kc, tok0 + tb * 128:tok0 + (tb + 1) * 128], rhs=slot[:, kc, 0:gwid],
                            start=(kc == 0), stop=(kc == kc_n - 1)),
                            reads=[ws.regs[s][kc // 8], hr[kc]], writes=[cx.ps_regs[b]])
                nb = ntb
            cx.flush_pe()
            g["epi"](gi, g, stn, tok0, n, bset)
            if getattr(cx, "unit_hook", None):
                cx.unit_hook()
    cx.flush_pe()
    cx.unit = unit


def build_l1():
    cx = Cx()
    P = cx.P
    nc = cx.nc
    xT = cx.din("xT", [D, T])
    xh = cx.din("xh", [D, 16])
    W = cx.din("w_in", [D, 8208])
    gcol_d = cx.din("gcol", [128, KC_D])
    qk_d = cx.din("qkg", [128, 2])
    bf_d = cx.din("bf", [16, 1])
    pw_d = cx.din("pool_w", [4, 512, 512])
    psc_d = cx.din("pscol", [128, 16])
    icn_d = cx.din("invcnt", [128, 4, 16])
    qT = cx.dout("qT", [NH, 128, T], BF16)
    kT = cx.dout("kT", [NH, 128, T], BF16)
    Vo = cx.dout("V", [T, MIX], BF16)
    lf = cx.dout("logf", [NH, T])
    pmT = cx.dout("pmT", [MIX, T], BF16)

    cx.eps_t = cx.sb([128, 1], F32, "eps")
    cx.eps_r = Reg()
    P.op("pool", lambda e: e.memset(cx.eps_t[:, :], EPS), writes=[cx.eps_r])
    gcol, gcol_r = cx.load(gcol_d, [128, KC_D])
    qkg, qkg_r = cx.load(qk_d, [128, 2])
    bft, bft_r = cx.load(bf_d, [16, 1])
    psc, psc_r = cx.load(psc_d, [128, 16])
    icn, icn_r = cx.load(icn_d, [128, 4, 16])
    P.op("dve", lambda e: e.tensor_scalar(out=qkg[:, 0:1], in0=qkg[:, 0:1], scalar1=float(128 ** -0.5), scalar2=None, op0=ALU.mult),
         reads=[qkg_r], writes=[qkg_r])
    P.op("dve", lambda e: e.tensor_scalar(out=bft[:, :], in0=bft[:, :], scalar1=-1.0, scalar2=None, op0=ALU.mult),
         reads=[bft_r], writes=[bft_r])

    h = cx.sb([128, KC_D, T], BF16, "h")
    h_regs = [Reg() for _ in range(KC_D)]
    hh = cx.sb([128, KC_D, 16], BF16, "hh")
    hh_regs = [Reg() for _ in range(KC_D)]
    zb = [cx.sb([128, 16 + T], F32, "zb") for _ in range(2)]
    zb_r = [Reg() for _ in range(2)]
    pa = [cx.sb([128, 16 + T], F32, "pa") for _ in range(1)]
    pa_r = [Reg() for _ in range(1)]
    pbb = [cx.sb([128, 16 + T], F32, "pb") for _ in range(1)]
    pb_r = [Reg() for _ in range(1)]
    sqring = Ring(cx, 2, [128, T], BF16, "sq")
    xsring = Ring.over(cx, [zb[0], zb[1], pa[0]], [zb_r[0], zb_r[1], pa_r[0]], "xs")
    load_norm(cx, xh, gcol, gcol_r, 16, hh, hh_regs, rings=(xsring, sqring))
    load_norm(cx, xT, gcol, gcol_r, T, h, h_regs, rings=(xsring, sqring))

    ws = WStream(cx, KC_D, 384)
    subtiles = [("halo", hh, hh_regs, 0, 16), ("s0", h, h_regs, 0, 512), ("s1", h, h_regs, 512, 512)]

    pooled = cx.sb([128, 16, T], BF16, "pooled")
    pooled_regs = [Reg() for _ in range(16)]
    pcnt = [0]

    def epi_zp(gi, g, stn, tok0, n, bset):
        grp = g["pg"]
        wlen = (2, 4, 8, 16)[grp]
        for mi in range(2):
            off = 0 if stn == "halo" else 16 + tok0
            P.op("act", lambda e, mi=mi, off=off, n=n, b=bset[mi]: e.activation(out=zb[mi][:, off:off + n], in_=cx.ps[b][:, 0:n], func=AF.Copy),
                 reads=[cx.ps_regs[bset[mi]]], writes=[zb_r[mi]])
        if stn != "s1":
            return
        L = 16 + T
        for mi in range(2):
            ch = g["ch0"] + mi
            j = 0
            src, src_r = zb[mi], zb_r[mi]
            bufs = [(pa[j], pa_r[j]), (pbb[j], pb_r[j])]
            sh = 1
            bi = 0
            while sh < wlen:
                dst, dst_r = bufs[bi]
                eng = "dve"
                P.op(eng, lambda e, dst=dst, src=src, sh=sh: e.tensor_tensor(out=dst[:, sh:L], in0=src[:, sh:L], in1=src[:, 0:L - sh], op=ALU.add),
                     reads=[src_r], writes=[dst_r])
                src, src_r = dst, dst_r
                bi ^= 1
                sh *= 2
            P.op("dve", lambda e, src=src, mi=mi, ch=ch, wlen=wlen: e.scalar_tensor_tensor(
                out=pooled[:, ch, :], in0=src[:, 16:L], scalar=1.0 / wlen, in1=zb[mi][:, 16:L], op0=ALU.mult, op1=ALU.subtract),
                reads=[src_r, zb_r[mi]], writes=[pooled_regs[ch]])
            dst, dst_r = bufs[bi]
            P.op("dve", lambda e, dst=dst, src=src, grp=grp: e.tensor_tensor(out=dst[:, 0:16], in0=src[:, 16:32], in1=icn[:, grp, :], op=ALU.mult),
                 reads=[src_r, icn_r], writes=[dst_r])
            P.op("dve", lambda e, dst=dst, mi=mi, ch=ch: e.tensor_tensor(out=pooled[:, ch, 0:16], in0=dst[:, 0:16], in1=zb[mi][:, 16:32], op=ALU.subtract),
                 reads=[dst_r, zb_r[mi]], writes=[pooled_regs[ch]])

    lfr = Ring(cx, 2, [16, 512], F32, "lf")

    def epi_f(gi, g, stn, tok0, n, bset):
        t, r, k = lfr.next()
        b = bset[0]
        P.op("act", lambda e: e.activation(out=t[:, :], in_=cx.ps[b][0:16, 0:n], func=AF.Exp, bias=bft[:, 0:1], scale=-1.0),
             reads=[cx.ps_regs[b], bft_r], writes=[r])
        P.op("act", lambda e: e.activation(out=t[:, :], in_=t[:, :], func=AF.Ln, bias=1.0, scale=1.0), reads=[r], writes=[r])
        P.op("dve", lambda e: e.tensor_scalar(out=t[:, :], in0=t[:, :], scalar1=-1.0, scalar2=None, op0=ALU.mult), reads=[r], writes=[r])
        P.dma("sp", k, lambda e: e.dma_start(out=lf[:, tok0:tok0 + n], in_=t[:, :]), reads=[r])

    vr = Ring(cx, 4, [128, 256], BF16, "vo")

    def epi_v(gi, g, stn, tok0, n, bset):
        c0 = g["c0"]
        for tb in range(4):
            t, r, k = vr.next()
            b = bset[tb]
            P.op("act", lambda e, t=t, b=b: e.activation(out=t[:, :], in_=cx.ps[b][:, 0:256], func=AF.Copy), reads=[cx.ps_regs[b]], writes=[r])
            r0 = tok0 + tb * 128
            P.dma("sp", k, lambda e, t=t, r0=r0: e.dma_start(out=Vo[r0:r0 + 128, c0:c0 + 256], in_=t[:, :]), reads=[r])

    sqr = Ring(cx, 3, [128, 512], BF16, "qsq")
    rtr = Ring(cx, 3, [128, 512], F32, "qrt")
    qor = Ring(cx, 3, [128, 512], BF16, "qo")

    def epi_qk(gi, g, stn, tok0, n, bset):
        which = g["which"]
        dst = qT if which == 0 else kT
        nbk = bset[3]
        for mi in range(len(g["widths"])):
            hd = g["h0"] + mi
            b = bset[mi]
            s, sr, _ = sqr.next()
            P.op("act", lambda e, s=s, b=b: e.activation(out=s[:, :], in_=cx.ps[b][:, :], func=AF.Square), reads=[cx.ps_regs[b]], writes=[sr])

            def later(s=s, sr=sr, b=b, hd=hd, nbk=nbk):
                P.op("pe", lambda e: e.matmul(cx.ps[nbk][:, :], lhsT=cx.ones[:, :], rhs=s[:, :], start=True, stop=True),
                     reads=[sr, cx.ones_r], writes=[cx.ps_regs[nbk]])
                rt, rr, _ = rtr.next()
                P.op("act", lambda e: e.activation(out=rt[:, :], in_=cx.ps[nbk][:, :], func=AF.Sqrt, bias=cx.eps_t[:, 0:1], scale=1.0 / 128),
                     reads=[cx.ps_regs[nbk], cx.eps_r], writes=[rr])
                P.op("dve", lambda e: e.reciprocal(out=rt[:, :], in_=rt[:, :]), reads=[rr], writes=[rr])
                o, orr, k = qor.next()
                P.op("dve", lambda e: e.scalar_tensor_tensor(out=o[:, :], in0=cx.ps[b][:, :], scalar=qkg[:, which:which + 1], in1=rt[:, :],
                                                             op0=ALU.mult, op1=ALU.mult),
                     reads=[cx.ps_regs[b], qkg_r, rr], writes=[orr])
                P.dma("sp", k, lambda e: e.dma_start(out=dst[hd, :, tok0:tok0 + n], in_=o[:, :]), reads=[orr])
            cx.pe_defer.append(later)

    groups = []
    for cp in range(8):
        c0 = 6160 + cp * 256
        groups.append(dict(kind="fm", segs=[(0, c0, 256)], widths=[128] * 2, epi=epi_zp, halo=True, pg=cp // 2, ch0=cp * 2))
    groups.append(dict(kind="fm", segs=[(0, 6144, 16)], widths=[16], epi=epi_f))
    for vg in range(8):
        groups.append(dict(kind="tm", segs=[(0, 4096 + vg * 256, 256)], gw=256, epi=epi_v, c0=vg * 256))
    for which in range(2):
        for h0 in range(0, 16, 3):
            nh = min(3, 16 - h0)
            c0 = which * 2048 + h0 * 128
            groups.append(dict(kind="fm", segs=[(0, c0, nh * 128)], widths=[128] * nh, epi=epi_qk, which=which, h0=h0))
    gemm(cx, ws, W, KC_D, groups, subtiles)

    pmr = Ring(cx, 3, [128, 512], BF16, "pmo")
    for pg in range(4):
        def epi_pm(gi, g, stn, tok0, n, bset, pg=pg):
            for mi in range(2):
                ch = pg * 4 + g["m0"] + mi
                o, orr, k = pmr.next()
                b = bset[mi]
                P.op("act", lambda e, o=o, b=b, ch=ch: e.activation(out=o[:, :], in_=cx.ps[b][:, :], func=AF.Copy, scale=psc[:, ch:ch + 1]),
                     reads=[cx.ps_regs[b], psc_r], writes=[orr])
                P.dma("sp", k, lambda e, o=o, ch=ch: e.dma_start(out=pmT[ch * 128:(ch + 1) * 128, tok0:tok0 + n], in_=o[:, :]), reads=[orr])
        pview = pooled[:, pg * 4:(pg + 1) * 4, :]
        gemm(cx, ws, pw_d[pg], 4, [dict(kind="fm", segs=[(0, m0 * 128, 256)], widths=[128] * 2, epi=epi_pm, m0=m0) for m0 in (0, 2)],
             [("s0", pview, pooled_regs[pg * 4:(pg + 1) * 4], 0, 512), ("s1", pview, pooled_regs[pg * 4:(pg + 1) * 4], 512, 512)])
    P.wait_all_dma("sp")
    P.emit()
    return nc


def colmajor(v, n=128):
    return np.ascontiguousarray(np.asarray(v, np.float32).reshape(-1, n).T)


def run(nc, in_maps, trace=False):
    res = run_bass_kernel_spmd(nc, in_maps, core_ids=list(range(NCORES)), trace=trace)
    return res


def launch1(inp, trace=False):
    x = np.asarray(inp["x"], np.float32)[0]
    nc = build_l1()
    w_in = np.ascontiguousarray(np.asarray(inp["l0_w_in"], np.float32))
    gcol = colmajor(inp["l0_mix_norm"])
    qkg = np.stack([np.asarray(inp["l0_q_gain"], np.float32), np.asarray(inp["l0_k_gain"], np.float32)], 1)
    bf = np.asarray(inp["l0_b_f"], np.float32).reshape(16, 1)
    pw = np.ascontiguousarray(np.asarray(inp["l0_pool_w"], np.float32))
    psc = colmajor(inp["l0_pool_scale"])
    maps = []
    for c in range(NCORES):
        xs = x[c * T:(c + 1) * T]
        xh = x[c * T - 16:c * T] if c > 0 else np.zeros((16, D), np.float32)
        icn = np.zeros((128, 4, 16), np.float32)
        for g, w in enumerate((2, 4, 8, 16)):
            pos = np.arange(16) + 1 + c * T
            icn[:, g, :] = 1.0 / np.minimum(pos, w)
        maps.append(dict(xT=np.ascontiguousarray(xs.T), xh=np.ascontiguousarray(xh.T), w_in=w_in, gcol=gcol,
                         qkg=np.ascontiguousarray(qkg), bf=bf, pool_w=pw, pscol=psc, invcnt=icn))
    return run(nc, maps, trace)


HPC = NH // NCORES
NQT = SEQ // 512
NKB = SEQ // 128


def build_l2a():
    cx = Cx()
    P = cx.P
    qT = cx.din("qT", [HPC, 128, SEQ], BF16)
    kT = cx.din("kT", [HPC, 128, SEQ], BF16)
    Vd = cx.din("V", [HPC, 128, NKB, 128], BF16)
    lf6 = cx.din("lf6", [HPC, 6, SEQ])
    coef_d = cx.din("coef", [6, 8])
    tri_d = cx.din("tri", [128, 128])
    aT = cx.dout("aT", [HPC, 128, SEQ], BF16)
    coef, coef_r = cx.load(coef_d, [6, 8])
    tri, tri_r = cx.load(tri_d, [128, 128])
    qs, ks, vs, qa, ka = [], [], [], [], []
    for hh in range(HPC):
        q_t, q_r = cx.load(qT[hh], [128, SEQ], BF16, "q")
        k_t, k_r = cx.load(kT[hh], [128, SEQ], BF16, "k")
        v_t, v_r = cx.load(Vd[hh], [128, NKB, 128], BF16, "v")
        qs.append((q_t, q_r)); ks.append((k_t, k_r)); vs.append((v_t, v_r))
    SG = 2048
    lft = cx.sb([6, SG], F32, "lft"); lft_r = Reg()
    c6 = cx.sb([6, SG], F32, "c6"); c6_r = Reg()
    r1 = cx.sb([6, SG], F32, "r1"); r1_r = Reg()
    hi = cx.sb([6, SG], BF16, "hi"); hi_r = Reg()
    mid = cx.sb([6, SG], BF16, "mid"); mid_r = Reg()
    lo = cx.sb([6, SG], BF16, "lo"); lo_r = Reg()
    tmp = cx.sb([6, SG], F32, "tmpa"); tmp_r = Reg()
    carry = cx.sb([6, 1], F32, "carry"); carry_r = Reg()
    qa_t = cx.sb([6, SEQ], BF16, "qa"); qa_r = Reg()
    ka_t = cx.sb([6, SEQ], BF16, "ka"); ka_r = Reg()
    aug_done = [False] * HPC

    def build_aug(hh):
        for sg in range(SEQ // SG):
            t0 = sg * SG
            P.dma("sp", "lftk", lambda e, hh=hh, t0=t0: e.dma_start(out=lft[:, :], in_=lf6[hh, :, t0:t0 + SG]), writes=[lft_r])
            P.op("pool", lambda e: e.memset(tmp[:, :], 1.0), writes=[tmp_r])
            if sg == 0:
                P.op("dve", lambda e: e.tensor_tensor_scan(out=c6[:, :], data0=tmp[:, :], data1=lft[:, :], initial=0.0, op0=ALU.mult, op1=ALU.add),
                     reads=[lft_r, tmp_r], writes=[c6_r])
            else:
                P.op("dve", lambda e: e.tensor_tensor_scan(out=c6[:, :], data0=tmp[:, :], data1=lft[:, :], initial=carry[:, 0:1], op0=ALU.mult, op1=ALU.add),
                     reads=[lft_r, tmp_r, carry_r], writes=[c6_r])
            P.op("dve", lambda e: e.tensor_copy(out=carry[:, :], in_=c6[:, SG - 1:SG]), reads=[c6_r], writes=[carry_r])
            P.op("dve", lambda e: e.tensor_copy(out=hi[:, :], in_=c6[:, :]), reads=[c6_r], writes=[hi_r])
            P.op("dve", lambda e: e.tensor_tensor(out=r1[:, :], in0=c6[:, :], in1=hi[:, :], op=ALU.subtract), reads=[c6_r, hi_r], writes=[r1_r])
            P.op("dve", lambda e: e.tensor_copy(out=mid[:, :], in_=r1[:, :]), reads=[r1_r], writes=[mid_r])
            P.op("dve", lambda e: e.tensor_tensor(out=r1[:, :], in0=r1[:, :], in1=mid[:, :], op=ALU.subtract), reads=[r1_r, mid_r], writes=[r1_r])
            P.op("dve", lambda e: e.tensor_copy(out=lo[:, :], in_=r1[:, :]), reads=[r1_r], writes=[lo_r])
            for (dst, dst_r, o) in ((qa_t, qa_r, 0), (ka_t, ka_r, 4)):
                P.op("dve", lambda e, o=o: e.tensor_scalar(out=tmp[:, :], in0=hi[:, :], scalar1=coef[:, o:o + 1], scalar2=coef[:, o + 3:o + 4], op0=ALU.mult, op1=ALU.add),
                     reads=[hi_r, coef_r], writes=[tmp_r])
                P.op("dve", lambda e, o=o: e.scalar_tensor_tensor(out=tmp[:, :], in0=mid[:, :], scalar=coef[:, o + 1:o + 2], in1=tmp[:, :], op0=ALU.mult, op1=ALU.add),
                     reads=[mid_r, coef_r, tmp_r], writes=[tmp_r])
                P.op("dve", lambda e, o=o, dst=dst, t0=t0: e.scalar_tensor_tensor(out=dst[:, t0:t0 + SG], in0=lo[:, :], scalar=coef[:, o + 2:o + 3], in1=tmp[:, :], op0=ALU.mult, op1=ALU.add),
                     reads=[lo_r, coef_r, tmp_r], writes=[dst_r])
    LA = 2
    pr = Ring(cx, LA + 2, [128, 512], BF16, "pT")
    rdr = Ring(cx, 2, [128, 512], F32, "rden")
    aor = Ring(cx, 2, [128, 512], BF16, "ao")
    sbanks = [0, 1, 6, 7]
    blocks = []
    unit = 0
    for hh in range(HPC):
        for qt in range(NQT):
            ob = 2 + (unit % 2) * 2
            unit += 1
            nkb = 4 * qt + 4
            for kb in range(nkb):
                blocks.append(dict(hh=hh, qt=qt, kb=kb, nkb=nkb, ob=ob, db=ob + 1, u=unit))

    def stage_a(i):
        bl = blocks[i]
        hh, qt, kb = bl["hh"], bl["qt"], bl["kb"]
        if not aug_done[hh]:
            build_aug(hh)
            aug_done[hh] = True
        q_t, q_r = qs[hh]; k_t, k_r = ks[hh]
        q0 = qt * 512
        j = kb - 4 * qt
        c0 = 128 * j if j > 0 else 0
        n = 512 - c0
        sbank = sbanks[i % len(sbanks)]
        bl.update(c0=c0, n=n, j=j)
        P.op("pe", lambda e: e.matmul(cx.ps[sbank][:, 0:n], lhsT=k_t[:, kb * 128:(kb + 1) * 128], rhs=q_t[:, q0 + c0:q0 + 512], start=True, stop=False),
             reads=[k_r, q_r], writes=[cx.ps_regs[sbank]])
        P.op("pe", lambda e: e.matmul(cx.ps[sbank][:, 0:n], lhsT=ka_t[:, kb * 128:(kb + 1) * 128], rhs=qa_t[:, q0 + c0:q0 + 512], start=False, stop=True),
             reads=[ka_r, qa_r], writes=[cx.ps_regs[sbank]])
        pt, pt_r, _ = pr.next()
        bl.update(pt=pt, pt_r=pt_r)
        P.op("act", lambda e: e.activation(out=pt[:, 0:n], in_=cx.ps[sbank][:, 0:n], func=AF.Exp), reads=[cx.ps_regs[sbank]], writes=[pt_r])
        if j >= 0:
            P.op("pool", lambda e: e.tensor_tensor(out=pt[:, 0:128], in0=pt[:, 0:128], in1=tri[:, :], op=ALU.mult), reads=[pt_r, tri_r], writes=[pt_r])

    def stage_b(i):
        bl = blocks[i]
        hh, qt, kb, nkb, ob, db = bl["hh"], bl["qt"], bl["kb"], bl["nkb"], bl["ob"], bl["db"]
        c0, n, pt, pt_r = bl["c0"], bl["n"], bl["pt"], bl["pt_r"]
        v_t, v_r = vs[hh]
        q0 = qt * 512
        P.op("pe", lambda e: e.matmul(cx.ps[ob][:, c0:512], lhsT=v_t[:, kb, :], rhs=pt[:, 0:n], start=(kb == 0), stop=(kb == nkb - 1)),
             reads=[v_r, pt_r], writes=[cx.ps_regs[ob]])
        P.op("pe", lambda e: e.matmul(cx.ps[db][:, c0:512], lhsT=cx.ones[:, :], rhs=pt[:, 0:n], start=(kb == 0), stop=(kb == nkb - 1)),
             reads=[cx.ones_r, pt_r], writes=[cx.ps_regs[db]])
        if kb == nkb - 1:
            rd, rd_r, _ = rdr.next()
            P.op("dve", lambda e: e.reciprocal(out=rd[:, :], in_=cx.ps[db][:, :]), reads=[cx.ps_regs[db]], writes=[rd_r])
            ao, ao_r, k = aor.next()
            P.op("dve", lambda e: e.tensor_tensor(out=ao[:, :], in0=cx.ps[ob][:, :], in1=rd[:, :], op=ALU.mult), reads=[cx.ps_regs[ob], rd_r], writes=[ao_r])
            P.dma("sp", k, lambda e: e.dma_start(out=aT[hh, :, q0:q0 + 512], in_=ao[:, :]), reads=[ao_r])
    for i in range(len(blocks) + LA):
        if i < len(blocks):
            stage_a(i)
        if i - LA >= 0:
            stage_b(i - LA)
    P.wait_all_dma("sp")
    P.emit()
    return cx.nc


def launch2a(inp, l1res, trace=False):
    nc = build_l2a()
    qT = np.concatenate([np.asarray(l1res[c]["qT"]) for c in range(NCORES)], axis=2)
    kT = np.concatenate([np.asarray(l1res[c]["kT"]) for c in range(NCORES)], axis=2)
    V = np.concatenate([np.asarray(l1res[c]["V"]) for c in range(NCORES)], axis=0)
    lf = np.concatenate([np.asarray(l1res[c]["logf"]) for c in range(NCORES)], axis=1)
    coef = np.zeros((6, 8), np.float32)
    coef[0, 0] = 1; coef[1, 1] = 1; coef[2, 2] = 1; coef[3:6, 3] = 1
    coef[3, 4] = -1; coef[4, 5] = -1; coef[5, 6] = -1; coef[0:3, 7] = 1
    tri = np.triu(np.ones((128, 128), np.float32))
    maps = []
    for c in range(NCORES):
        hs = slice(c * HPC, (c + 1) * HPC)
        Vh = V.reshape(NKB, 128, NH, 128)[:, :, hs].transpose(2, 1, 0, 3)
        maps.append(dict(qT=np.ascontiguousarray(qT[hs]), kT=np.ascontiguousarray(kT[hs]), V=np.ascontiguousarray(Vh),
                         lf6=np.ascontiguousarray(np.repeat(lf[hs][:, None, :], 6, axis=1)), coef=coef, tri=tri))
    return run(nc, maps, trace)


def build_bc(layer):
    cx = Cx()
    P = cx.P
    xT = cx.din("xT", [D, T])
    hc = cx.din("hcatT", [D, T], BF16)
    w_out = cx.din("w_out", [D, D])
    fg_d = cx.din("fgcol", [128, KC_D])
    w_up = cx.din("w_up", [D, 2 * DFF])
    x1T = cx.dout("x1T", [D, T])
    actT = cx.dout("actT", [DFF, T], BF16)
    gf_o = cx.dout("gf", [128, NFC, 2])
    uf_o = cx.dout("uf", [128, NFC, 2])
    gl_o = cx.dout("gl", [128, NFC, 2])
    fcw_d = cx.din("fcwcol", [128, NFC, 3])
    fcw, fcw_r = cx.load(fcw_d, [128, NFC, 3])
    cx.eps_t = cx.sb([128, 1], F32, "eps")
    cx.eps_r = Reg()
    P.op("pool", lambda e: e.memset(cx.eps_t[:, :], EPS), writes=[cx.eps_r])
    fg, fg_r = cx.load(fg_d, [128, KC_D])
    h = cx.sb([128, KC_D, T], BF16, "h")
    h_regs = [Reg() for _ in range(KC_D)]
    hv = hc.rearrange("(c p) t -> p c t", p=128)
    if layer == 0:
        for q in range(4):
            P.dma("sp", cx.uid("hl"), lambda e, q=q: e.dma_start(out=h[:, q * 8:(q + 1) * 8, :], in_=hv[:, q * 8:(q + 1) * 8, :]),
                  writes=h_regs[q * 8:(q + 1) * 8])
    else:
        for q in range(2, 4):
            P.dma("sp", cx.uid("hl"), lambda e, q=q: e.dma_start(out=h[:, q * 8:(q + 1) * 8, :], in_=hv[:, q * 8:(q + 1) * 8, :]),
                  writes=h_regs[q * 8:(q + 1) * 8])
        cv_d = cx.din("cvx", [MIX, 2 + T], BF16)
        gb_d = cx.din("gbT", [MIX, T], BF16)
        cw_d = cx.din("cwcol", [128, 16, 3])
        cw, cw_r = cx.load(cw_d, [128, 16, 3])
        cvr = Ring(cx, 2, [128, 2 + T], BF16, "cv")
        gbr = Ring(cx, 2, [128, T], BF16, "gb")
        yr = Ring(cx, 2, [128, T], F32, "cy")
        for j in range(16):
            cvt, cvt_r, k1 = cvr.next()
            gbt, gbt_r, k2 = gbr.next()
            y, y_r, _ = yr.next()
            P.dma("sp", k1, lambda e, cvt=cvt, j=j: e.dma_start(out=cvt[:, :], in_=cv_d[j * 128:(j + 1) * 128, :]), writes=[cvt_r])
            P.dma("sp", k2, lambda e, gbt=gbt, j=j: e.dma_start(out=gbt[:, :], in_=gb_d[j * 128:(j + 1) * 128, :]), writes=[gbt_r])
            P.op("dve", lambda e, y=y, cvt=cvt, j=j: e.tensor_scalar(out=y[:, :], in0=cvt[:, 2:2 + T], scalar1=cw[:, j, 2:3], scalar2=None, op0=ALU.mult),
                 reads=[cvt_r, cw_r], writes=[y_r])
            P.op("dve", lambda e, y=y, cvt=cvt, j=j: e.scalar_tensor_tensor(out=y[:, :], in0=cvt[:, 1:1 + T], scalar=cw[:, j, 1:2], in1=y[:, :], op0=ALU.mult, op1=ALU.add),
                 reads=[cvt_r, cw_r, y_r], writes=[y_r])
            P.op("dve", lambda e, y=y, cvt=cvt, j=j: e.scalar_tensor_tensor(out=y[:, :], in0=cvt[:, 0:T], scalar=cw[:, j, 0:1], in1=y[:, :], op0=ALU.mult, op1=ALU.add),
                 reads=[cvt_r, cw_r, y_r], writes=[y_r])
            P.op("dve", lambda e, y=y, gbt=gbt, j=j: e.tensor_tensor(out=h[:, j, :], in0=y[:, :], in1=gbt[:, :], op=ALU.mult),
                 reads=[y_r, gbt_r], writes=[h_regs[j]])
    ws = WStream(cx, KC_D, 512)
    subtiles = [("s0", h, h_regs, 0, 512), ("s1", h, h_regs, 512, 512)]
    x1_regs = [Reg() for _ in range(KC_D)]
    xr = Ring(cx, 3, [128, 512], F32, "xc")
    orr_ = Ring(cx, 3, [128, 512], F32, "xo")

    def epi_res(gi, g, stn, tok0, n, bset):
        for mi in range(len(g["widths"])):
            ch = g["ch0"] + mi
            b = bset[mi]
            xt, xt_r, k = xr.next()
            P.dma("sp", k, lambda e, xt=xt, ch=ch: e.dma_start(out=xt[:, :], in_=xT[ch * 128:(ch + 1) * 128, tok0:tok0 + n]), writes=[xt_r])
            o, o_r, k2 = orr_.next()
            P.op("dve", lambda e, o=o, xt=xt, b=b: e.tensor_tensor(out=o[:, :], in0=cx.ps[b][:, :], in1=xt[:, :], op=ALU.add),
                 reads=[cx.ps_regs[b], xt_r], writes=[o_r])
            P.dma("sp", k2, lambda e, o=o, ch=ch: e.dma_start(out=x1T[ch * 128:(ch + 1) * 128, tok0:tok0 + n], in_=o[:, :]),
                  reads=[o_r], writes=[x1_regs[ch]])
    groups = []
    for ch0 in range(0, KC_D, 4):
        groups.append(dict(kind="fm", segs=[(0, ch0 * 128, 512)], widths=[128] * 4, epi=epi_res, ch0=ch0))
    gemm(cx, ws, w_out, KC_D, groups, subtiles)
    load_norm(cx, x1T, fg, fg_r, T, h, h_regs, x_regs=x1_regs, rings=(Ring(cx, 2, [128, T], F32, "xs"), Ring(cx, 2, [128, T], BF16, "sq")))
    gbuf = [cx.sb([128, 2 + 512], F32, "gbuf") for _ in range(2)]
    gbuf_r = [Reg() for _ in range(2)]
    yr2 = Ring(cx, 3, [128, 512], F32, "fy")
    aor = Ring(cx, 4, [128, 512], BF16, "ao")
    gf = cx.sb([128, NFC, 2], F32, "gf"); gf_r = Reg()
    uf = cx.sb([128, NFC, 2], F32, "uf"); uf_r = Reg()
    gl = cx.sb([128, NFC, 2], F32, "gl"); gl_r = Reg()

    def epi_act(gi, g, stn, tok0, n, bset):
        for i in range(2):
            ch = g["ch0"] + i
            bg, bu = bset[i], bset[2 + i]
            gb, gb_r = gbuf[i], gbuf_r[i]
            import os
            if os.environ.get("NOHALO"):
                pass
            elif stn == "s0":
                P.op("dve", lambda e, gb=gb: e.memset(gb[:, 0:2], 0.0), writes=[gb_r])
            else:
                P.op("dve", lambda e, gb=gb: e.tensor_copy(out=gb[:, 0:2], in_=gb[:, 512:514]), reads=[gb_r], writes=[gb_r])
            P.op("act", lambda e, gb=gb, bg=bg: e.activation(out=gb[:, 2:514], in_=cx.ps[bg][:, :], func=AF.Copy), reads=[cx.ps_regs[bg]], writes=[gb_r])
            y, y_r, _ = yr2.next()
            P.op("act", lambda e, y=y, bg=bg, ch=ch: e.activation(out=y[:, :], in_=cx.ps[bg][:, :], func=AF.Copy, scale=fcw[:, ch, 2:3]),
                 reads=[cx.ps_regs[bg], fcw_r], writes=[y_r])
            P.op("dve", lambda e, y=y, gb=gb, ch=ch: e.scalar_tensor_tensor(out=y[:, :], in0=gb[:, 1:513], scalar=fcw[:, ch, 1:2], in1=y[:, :], op0=ALU.mult, op1=ALU.add),
                 reads=[gb_r, fcw_r, y_r], writes=[y_r])
            P.op("dve", lambda e, y=y, gb=gb, ch=ch: e.scalar_tensor_tensor(out=y[:, :], in0=gb[:, 0:512], scalar=fcw[:, ch, 0:1], in1=y[:, :], op0=ALU.mult, op1=ALU.add),
                 reads=[gb_r, fcw_r, y_r], writes=[y_r])
            P.op("act", lambda e, y=y: e.activation(out=y[:, :], in_=y[:, :], func=AF.Silu), reads=[y_r], writes=[y_r])
            o, o_r, k = aor.next()
            P.op("dve", lambda e, o=o, y=y, bu=bu: e.tensor_tensor(out=o[:, :], in0=cx.ps[bu][:, :], in1=y[:, :], op=ALU.mult),
                 reads=[cx.ps_regs[bu], y_r], writes=[o_r])
            P.dma("sp", k, lambda e, o=o, ch=ch: e.dma_start(out=actT[ch * 128:(ch + 1) * 128, tok0:tok0 + n], in_=o[:, :]), reads=[o_r])
            if os.environ.get("NOSIDE"):
                pass
            elif stn == "s0":
                P.op("dve", lambda e, gb=gb, ch=ch: e.tensor_copy(out=gf[:, ch, :], in_=gb[:, 2:4]), reads=[gb_r], writes=[gf_r])
                P.op("dve", lambda e, bu=bu, ch=ch: e.tensor_copy(out=uf[:, ch, :], in_=cx.ps[bu][:, 0:2]), reads=[cx.ps_regs[bu]], writes=[uf_r])
            else:
                P.op("dve", lambda e, gb=gb, ch=ch: e.tensor_copy(out=gl[:, ch, :], in_=gb[:, 512:514]), reads=[gb_r], writes=[gl_r])
    groups = []
    for j2 in range(NFC // 2):
        groups.append(dict(kind="fm", segs=[(0, j2 * 256, 256), (256, DFF + j2 * 256, 256)], widths=[128] * 4, epi=epi_act, ch0=2 * j2))
    gemm(cx, ws, w_up, KC_D, groups, subtiles)
    P.dma("sp", "gfo", lambda e: e.dma_start(out=gf_o[:, :, :], in_=gf[:, :, :]), reads=[gf_r])
    P.dma("sp", "ufo", lambda e: e.dma_start(out=uf_o[:, :, :], in_=uf[:, :, :]), reads=[uf_r])
    P.dma("sp", "glo", lambda e: e.dma_start(out=gl_o[:, :, :], in_=gl[:, :, :]), reads=[gl_r])
    P.wait_all_dma("sp")
    P.emit()
    return cx.nc


HK = NFC // 2


def build_d():
    cx = Cx()
    P = cx.P
    x1T = cx.din("x1T", [D, T])
    actT = cx.din("actT", [DFF, T], BF16)
    gf_d = cx.din("gf", [128, NFC, 2])
    uf_d = cx.din("uf", [128, NFC, 2])
    glp_d = cx.din("glp", [128, NFC, 2])
    cw_d = cx.din("cwcol", [128, NFC, 3])
    w_dn = cx.din("w_down", [DFF, D])
    x2T = cx.dout("x2T", [D, T])
    cw, cw_r = cx.load(cw_d, [128, NFC, 3])
    uf, uf_r = cx.load(uf_d, [128, NFC, 2])
    ext = cx.sb([128, NFC, 4], F32, "ext")
    ext_r = Reg()
    P.dma("sp", "extk", lambda e: [e.dma_start(out=ext[:, :, 0:2], in_=glp_d[:, :, :]), e.dma_start(out=ext[:, :, 2:4], in_=gf_d[:, :, :])],
          writes=[ext_r], n=2)
    yf = cx.sb([128, NFC, 2], F32, "yf"); yf_r = Reg()
    tf = cx.sb([128, NFC], F32, "tf"); tf_r = Reg()
    afix = cx.sb([128, NFC, 2], BF16, "afix"); afix_r = Reg()
    for t in range(2):
        P.op("dve", lambda e, t=t: e.tensor_tensor(out=yf[:, :, t], in0=ext[:, :, t + 2], in1=cw[:, :, 2], op=ALU.mult), reads=[ext_r, cw_r], writes=[yf_r])
        for i in (1, 0):
            P.op("dve", lambda e, t=t, i=i: e.tensor_tensor(out=tf[:, :], in0=ext[:, :, t + i], in1=cw[:, :, i], op=ALU.mult), reads=[ext_r, cw_r], writes=[tf_r])
            P.op("dve", lambda e, t=t: e.tensor_tensor(out=yf[:, :, t], in0=yf[:, :, t], in1=tf[:, :], op=ALU.add), reads=[yf_r, tf_r], writes=[yf_r])
    P.op("act", lambda e: e.activation(out=yf[:, :, :], in_=yf[:, :, :], func=AF.Silu), reads=[yf_r], writes=[yf_r])
    P.op("dve", lambda e: e.tensor_tensor(out=afix[:, :, :], in0=yf[:, :, :], in1=uf[:, :, :], op=ALU.mult), reads=[yf_r, uf_r], writes=[afix_r])
    act = cx.sb([128, HK, T], BF16, "act")
    act_regs = [Reg() for _ in range(HK)]
    ws = WStream(cx, HK, 256, nslots=3)
    xr = Ring(cx, 4, [128, 512], F32, "xc")
    orr_ = Ring(cx, 4, [128, 512], F32, "xo")
    x2_regs = [[Reg(), Reg()] for _ in range(KC_D)]
    av = actT.rearrange("(c p) t -> p c t", p=128)
    for hf in range(2):
        for q in range(0, HK, 8):
            q1 = min(HK, q + 8)
            P.dma("sp", f"actl{q}", lambda e, q=q, q1=q1, hf=hf: e.dma_start(out=act[:, q:q1, :], in_=av[:, hf * HK + q:hf * HK + q1, :]),
                  writes=act_regs[q:q1])
        P.op("dve", lambda e, hf=hf: e.tensor_copy(out=act[:, :, 0:2], in_=afix[:, hf * HK:(hf + 1) * HK, :]), reads=[afix_r], writes=act_regs)
        src = x1T if hf == 0 else x2T

        def epi_res(gi, g, stn, tok0, n, bset, src=src, hf=hf):
            sti = 0 if stn == "s0" else 1
            for mi in range(2):
                ch = g["ch0"] + mi
                b = bset[mi]
                xt, xt_r, k = xr.next()
                P.dma("sp", k, lambda e, xt=xt, ch=ch: e.dma_start(out=xt[:, :], in_=src[ch * 128:(ch + 1) * 128, tok0:tok0 + n]),
                      reads=([x2_regs[ch][sti]] if hf == 1 else []), writes=[xt_r])
                o, o_r, k2 = orr_.next()
                P.op("dve", lambda e, o=o, xt=xt, b=b: e.tensor_tensor(out=o[:, :], in0=cx.ps[b][:, :], in1=xt[:, :], op=ALU.add),
                     reads=[cx.ps_regs[b], xt_r], writes=[o_r])
                P.dma("sp", k2, lambda e, o=o, ch=ch: e.dma_start(out=x2T[ch * 128:(ch + 1) * 128, tok0:tok0 + n], in_=o[:, :]),
                      reads=[o_r], writes=[x2_regs[ch][sti]])
        groups = [dict(kind="fm", segs=[(0, ch0 * 128, 256)], widths=[128] * 2, epi=epi_res, ch0=ch0) for ch0 in range(0, KC_D, 2)]
        gemm(cx, ws, w_dn[hf * HK * 128:(hf + 1) * HK * 128, :], HK, groups,
             [("s0", act, act_regs, 0, 512), ("s1", act, act_regs, 512, 512)])
    P.wait_all_dma("sp")
    P.emit()
    return cx.nc


def build_e():
    cx = Cx()
    P = cx.P
    x2T = cx.din("xT", [D, T])
    g_d = cx.din("gcol", [128, KC_D])
    W = cx.din("w_in", [D, 10240])
    wT_d = cx.din("sgu_wT", [128, 16, 128])
    tri_d = cx.din("tri", [128, 128])
    bsb_d = cx.din("bsb", [128, 16, 128])
    ngb_d = cx.din("ngb", [128, MIX])
    cvT = cx.dout("cvT", [MIX, T], BF16)
    gbT = cx.dout("gbT", [MIX, T], BF16)
    dT = cx.dout("dT", [MIX, T], BF16)
    cx.eps_t = cx.sb([128, 1], F32, "eps")
    cx.eps_r = Reg()
    P.op("pool", lambda e: e.memset(cx.eps_t[:, :], EPS), writes=[cx.eps_r])
    gcol, gcol_r = cx.load(g_d, [128, KC_D])
    tri, tri_r = cx.load(tri_d, [128, 128])
    bsb, bsb_r = cx.load(bsb_d, [128, 16, 128])
    ngb, ngb_r = cx.load(ngb_d, [128, MIX])
    wtm, wtm_r = cx.load(wT_d, [128, 16, 128], BF16, "wtm", eng="pool")
    for hh in range(16):
        P.op("pool", lambda e, hh=hh: e.tensor_tensor(out=wtm[:, hh, :], in0=wtm[:, hh, :], in1=tri[:, :], op=ALU.mult),
             reads=[wtm_r, tri_r], writes=[wtm_r])
    vgel = [cx.sb([128, MIX], BF16, "vgel") for _ in range(8)]
    vgel_r = [Reg() for _ in range(8)]
    ug = cx.sb([128, 16, T], BF16, "ug")
    ug_r = [Reg() for _ in range(16)]
    h = cx.sb([128, KC_D, T], BF16, "h")
    h_regs = [Reg() for _ in range(KC_D)]
    load_norm(cx, x2T, gcol, gcol_r, T, h, h_regs, rings=(Ring(cx, 2, [128, T], F32, "xs"), Ring(cx, 2, [128, T], BF16, "sq")))
    ws = WStream(cx, KC_D, 256)
    subtiles = [("s0", h, h_regs, 0, 512), ("s1", h, h_regs, 512, 512)]

    def epi_zv(gi, g, stn, tok0, n, bset):
        c0 = g["c0"]
        for tb in range(4):
            tbg = tok0 // 128 + tb
            b = bset[tb]
            P.op("act", lambda e, tbg=tbg, b=b: e.activation(out=vgel[tbg][:, c0:c0 + 256], in_=cx.ps[b][:, 0:256], func=AF.Gelu),
                 reads=[cx.ps_regs[b]], writes=[vgel_r[tbg]])

    def epi_zu(gi, g, stn, tok0, n, bset):
        for mi in range(2):
            ch = g["ch0"] + mi
            b = bset[mi]
            P.op("act", lambda e, ch=ch, b=b: e.activation(out=ug[:, ch, tok0:tok0 + n], in_=cx.ps[b][:, :], func=AF.Gelu),
                 reads=[cx.ps_regs[b]], writes=[ug_r[ch]])
    tmr = Ring(cx, 2, [128, 512], F32, "cvtmp")
    cvr = Ring(cx, 3, [128, 512], BF16, "cvo")

    def epi_cv(gi, g, stn, tok0, n, bset):
        j = g["j"]
        tm, tm_r, _ = tmr.next()
        P.op("act", lambda e: e.activation(out=tm[:, :], in_=cx.ps[bset[0]][:, :], func=AF.Copy), reads=[cx.ps_regs[bset[0]]], writes=[tm_r])
        o, o_r, k = cvr.next()
        P.op("dve", lambda e: e.tensor_tensor(out=o[:, :], in0=cx.ps[bset[1]][:, :], in1=tm[:, :], op=ALU.mult),
             reads=[cx.ps_regs[bset[1]], tm_r], writes=[o_r])
        P.dma("sp", k, lambda e: e.dma_start(out=cvT[j * 128:(j + 1) * 128, tok0:tok0 + n], in_=o[:, :]), reads=[o_r])

    def epi_gb(gi, g, stn, tok0, n, bset):
        for mi in range(2):
            ch = g["ch0"] + mi
            b = bset[mi]
            o, o_r, k = cvr.next()
            P.op("act", lambda e, o=o, b=b: e.activation(out=o[:, :], in_=cx.ps[b][:, :], func=AF.Copy), reads=[cx.ps_regs[b]], writes=[o_r])
            P.dma("sp", k, lambda e, o=o, ch=ch: e.dma_start(out=gbT[ch * 128:(ch + 1) * 128, tok0:tok0 + n], in_=o[:, :]), reads=[o_r])
    groups = []
    for vg in range(8):
        groups.append(dict(kind="tm", segs=[(0, 8192 + vg * 256, 256)], gw=256, epi=epi_zv, c0=vg * 256))
    for ch0 in range(0, 16, 2):
        groups.append(dict(kind="fm", segs=[(0, 6144 + ch0 * 128, 256)], widths=[128] * 2, epi=epi_zu, ch0=ch0))
    for j in range(16):
        groups.append(dict(kind="fm", segs=[(0, j * 128, 128), (128, 4096 + j * 128, 128)], widths=[128] * 2, epi=epi_cv, j=j))
    for ch0 in range(0, 16, 2):
        groups.append(dict(kind="fm", segs=[(0, 2048 + ch0 * 128, 256)], widths=[128] * 2, epi=epi_gb, ch0=ch0))
    gemm(cx, ws, W, KC_D, groups, subtiles)
    def hflat(c):
        return h[:, c:c + 2, :].rearrange("p a b -> p (a b)"), [h_regs[c], h_regs[c + 1]]
    junk, junk_r = hflat(0)
    vns = [hflat(2), hflat(4)]
    dous = [hflat(6), hflat(8)]
    ss = cx.sb([128, 8], F32, "ss")
    ss_r = Reg()
    for tbg in range(8):
        P.op("act", lambda e, tbg=tbg: e.activation(out=junk, in_=vgel[tbg][:, :], func=AF.Square, accum_out=ss[:, tbg:tbg + 1]),
             reads=[vgel_r[tbg]], writes=junk_r + [ss_r])
    P.op("act", lambda e: e.activation(out=ss[:, :], in_=ss[:, :], func=AF.Sqrt, bias=cx.eps_t[:, 0:1], scale=1.0 / MIX),
         reads=[ss_r, cx.eps_r], writes=[ss_r])
    P.op("dve", lambda e: e.reciprocal(out=ss[:, :], in_=ss[:, :]), reads=[ss_r], writes=[ss_r])
    t1r = Ring(cx, 3, [128, 128], F32, "t1")
    dkeys = [cx.uid("dk"), cx.uid("dk")]
    dTv = dT.rearrange("(c p) t -> p c t", p=128)
    for tbg in range(8):
        vn, vn_r = vns[tbg % 2]
        do, do_r = dous[tbg % 2]
        P.op("dve", lambda e, vn=vn, tbg=tbg: e.scalar_tensor_tensor(out=vn, in0=vgel[tbg][:, :], scalar=ss[:, tbg:tbg + 1], in1=ngb[:, :],
                                                                    op0=ALU.mult, op1=ALU.mult),
             reads=[vgel_r[tbg], ss_r, ngb_r], writes=vn_r)
        bset = [0, 1, 2, 3] if tbg % 2 == 0 else [4, 5, 6, 7]
        for hh in range(16):
            b = bset[hh // 4]
            cc = (hh % 4) * 128
            P.op("pe", lambda e, vn=vn, hh=hh, b=b, cc=cc: e.matmul(cx.ps[b][:, cc:cc + 128], lhsT=vn[:, hh * 128:(hh + 1) * 128], rhs=wtm[:, hh, :],
                                                                   start=True, stop=True),
                 reads=vn_r + [wtm_r], writes=[cx.ps_regs[b]])
        for hh in range(16):
            b = bset[hh // 4]
            cc = (hh % 4) * 128
            t1, t1_r, _ = t1r.next()
            P.op("dve", lambda e, t1=t1, hh=hh, b=b, cc=cc: e.tensor_tensor(out=t1[:, :], in0=cx.ps[b][:, cc:cc + 128], in1=bsb[:, hh, :], op=ALU.add),
                 reads=[cx.ps_regs[b], bsb_r], writes=[t1_r])
            P.op("pool", lambda e, t1=t1, hh=hh, do=do, tbg=tbg: e.tensor_tensor(out=do[:, hh * 128:(hh + 1) * 128], in0=t1[:, :],
                                                                               in1=ug[:, hh, tbg * 128:(tbg + 1) * 128], op=ALU.mult),
                 reads=[t1_r, ug_r[hh]], writes=do_r)
        P.dma("sp", dkeys[tbg % 2], lambda e, do=do, tbg=tbg: e.dma_start(out=dTv[:, :, tbg * 128:(tbg + 1) * 128],
                                                                        in_=do.rearrange("p (c t) -> p c t", t=128)), reads=do_r)
    P.wait_all_dma("sp")
    P.emit()
    return cx.nc


def _cat(res, key, axis):
    return np.concatenate([np.asarray(res[c][key]) for c in range(NCORES)], axis=axis)


def _halo(full, c, n):
    if c == 0:
        return np.ascontiguousarray(np.concatenate([np.zeros((full.shape[0], n), full.dtype), full[:, :T]], axis=1))
    return np.ascontiguousarray(full[:, c * T - n:(c + 1) * T])


def convcol(conv, nchunks):
    return np.ascontiguousarray(np.asarray(conv, np.float32).T.reshape(nchunks, 128, 3).transpose(1, 0, 2))


def launch_bc(layer, xT_l, hcat_l, w_out, fnorm, w_up, fconv, extra=None):
    nc = build_bc(layer)
    fcw = convcol(fconv, NFC)
    w_out = np.ascontiguousarray(np.asarray(w_out, np.float32))
    w_up = np.ascontiguousarray(np.asarray(w_up, np.float32))
    fg = colmajor(fnorm)
    maps = []
    for c in range(NCORES):
        m = dict(xT=xT_l[c], hcatT=hcat_l[c], w_out=w_out, fgcol=fg, w_up=w_up, fcwcol=fcw)
        if extra is not None:
            m.update(extra[c])
        maps.append(m)
    return run(nc, maps).results


def launch_d(x1_l, res_bc, conv, w_down):
    nc = build_d()
    cw = convcol(conv, NFC)
    w_down = np.ascontiguousarray(np.asarray(w_down, np.float32))
    maps = []
    for c in range(NCORES):
        glp = np.asarray(res_bc[c - 1]["gl"]) if c > 0 else np.zeros((128, NFC, 2), np.float32)
        maps.append(dict(x1T=x1_l[c], actT=np.asarray(res_bc[c]["actT"]), gf=np.asarray(res_bc[c]["gf"]), uf=np.asarray(res_bc[c]["uf"]),
                         glp=np.ascontiguousarray(glp), cwcol=cw, w_down=w_down))
    return run(nc, maps).results


def kernel(**inp):
    x = np.asarray(inp["x"], np.float32)[0]
    xT_l = [np.ascontiguousarray(x[c * T:(c + 1) * T].T) for c in range(NCORES)]
    tri = np.triu(np.ones((128, 128), np.float32))
    r1 = launch1(inp).results
    r2 = launch2a(inp, r1).results
    aT = np.concatenate([np.asarray(r2[c]["aT"]) for c in range(NCORES)], axis=0).reshape(MIX, SEQ)
    hcat_l = [np.ascontiguousarray(np.concatenate([aT[:, c * T:(c + 1) * T], np.asarray(r1[c]["pmT"])], axis=0)) for c in range(NCORES)]
    rbc = launch_bc(0, xT_l, hcat_l, inp["l0_w_out"], inp["l0_ffn_norm"], inp["l0_ffn_w_up"], inp["l0_ffn_conv"])
    x1_l = [np.asarray(rbc[c]["x1T"]) for c in range(NCORES)]
    rd = launch_d(x1_l, rbc, inp["l0_ffn_conv"], inp["l0_ffn_w_down"])
    x2_l = [np.asarray(rd[c]["x2T"]) for c in range(NCORES)]
    nce = build_e()
    w_in1 = np.ascontiguousarray(np.asarray(inp["l1_w_in"], np.float32))
    wT = np.ascontiguousarray(np.asarray(inp["l1_sgu_w"], np.float32).transpose(2, 0, 1))
    bsb = np.ascontiguousarray(np.broadcast_to(np.asarray(inp["l1_sgu_b"], np.float32)[None], (128, 16, 128)))
    ngb = np.ascontiguousarray(np.broadcast_to(np.asarray(inp["l1_sgu_norm"], np.float32)[None], (128, MIX)))
    g1 = colmajor(inp["l1_mix_norm"])
    re_ = run(nce, [dict(xT=x2_l[c], gcol=g1, w_in=w_in1, sgu_wT=wT, tri=tri, bsb=bsb, ngb=ngb) for c in range(NCORES)]).results
    cvfull = _cat(re_, "cvT", 1)
    cw1 = np.ascontiguousarray(np.asarray(inp["l1_conv_w"], np.float32).T.reshape(16, 128, 3).transpose(1, 0, 2))
    hcat1 = [np.ascontiguousarray(np.concatenate([np.zeros((MIX, T), NPBF), np.asarray(re_[c]["dT"])], axis=0)) for c in range(NCORES)]
    extra = [dict(cvx=_halo(cvfull, c, 2), gbT=np.asarray(re_[c]["gbT"]), cwcol=cw1) for c in range(NCORES)]
    rbc1 = launch_bc(1, x2_l, hcat1, inp["l1_w_out"], inp["l1_ffn_norm"], inp["l1_ffn_w_up"], inp["l1_ffn_conv"], extra)
    x3_l = [np.asarray(rbc1[c]["x1T"]) for c in range(NCORES)]
    rd1 = launch_d(x3_l, rbc1, inp["l1_ffn_conv"], inp["l1_ffn_w_down"])
    out = np.concatenate([np.asarray(rd1[c]["x2T"]).T for c in range(NCORES)], axis=0)
    return np.ascontiguousarray(out[None].astype(np.float32))
```

```python
import numpy as np
import ml_dtypes
import concourse.bass as bass
import concourse.mybir as mybir
from concourse.bass_utils import run_bass_kernel_spmd

F32 = mybir.dt.float32
BF16 = mybir.dt.bfloat16
AF = mybir.ActivationFunctionType
ALU = mybir.AluOpType
NPBF = ml_dtypes.bfloat16

NCORES = 8
SEQ = 8192
D = 4096
T = SEQ // NCORES
KC_D = D // 128
MIX = 2048
NH = 16
DFF = 11008
NFC = DFF // 128
EPS = 1e-6
ENGS = ("pe", "act", "dve", "pool", "sp")


class Reg:
    __slots__ = ("w", "rs")

    def __init__(self):
        self.w = None
        self.rs = {}


class Op:
    __slots__ = ("fn", "deps", "signal", "dma_sem", "val", "ndma")

    def __init__(self, fn, deps, dma_sem=None, ndma=1):
        self.fn = fn
        self.deps = deps
        self.signal = False
        self.dma_sem = dma_sem
        self.val = None
        self.ndma = ndma


class Prog:
    def __init__(self, nc):
        self.nc = nc
        self.ops = {e: [] for e in ENGS}
        self.dma_cnt = {}
        self.dma_sems = {}
        self.eng_sems = {}

    def _collect(self, eng, reads, writes):
        deps = {}

        def add(tok):
            if tok is None:
                return
            c, s = tok
            if c == "pe" and eng == "pe":
                return
            if deps.get(c, -1) < s:
                deps[c] = s
        for r in reads:
            add(r.w)
        for w in writes:
            add(w.w)
            for c, s in w.rs.items():
                add((c, s))
        return deps

    def _commit(self, tok, reads, writes):
        c, s = tok
        for r in reads:
            if r.rs.get(c, -1) < s:
                r.rs[c] = s
        for w in writes:
            w.w = tok
            w.rs = {}

    def op(self, eng, fn, reads=(), writes=()):
        deps = self._collect(eng, reads, writes)
        idx = len(self.ops[eng])
        self.ops[eng].append(Op(fn, deps))
        tok = (eng, idx)
        self._commit(tok, reads, writes)
        return tok

    def dma(self, eng, semkey, fn, reads=(), writes=(), n=1):
        deps = self._collect(eng, reads, writes)
        cnt = self.dma_cnt.get(semkey, 0) + n
        self.dma_cnt[semkey] = cnt
        self.ops[eng].append(Op(fn, deps, dma_sem=semkey, ndma=n))
        tok = (("dma", semkey), cnt)
        self._commit(tok, reads, writes)
        return tok

    def wait_all_dma(self, eng):
        deps = {("dma", k): v for k, v in self.dma_cnt.items()}
        self.ops[eng].append(Op(None, deps))

    def emit(self):
        nc = self.nc
        for e in ENGS:
            for o in self.ops[e]:
                for c, s in o.deps.items():
                    if isinstance(c, str):
                        self.ops[c][s].signal = True
        for e in ENGS:
            n = 0
            for o in self.ops[e]:
                if o.dma_sem is None and o.signal:
                    n += 1
                    o.val = n
        for e in ENGS:
            self.eng_sems[e] = nc.alloc_semaphore(name=f"s_{e}")
        for k in self.dma_cnt:
            self.dma_sems[k] = nc.alloc_semaphore(name=f"d_{k}")
        prog = self

        def run(e, eng):
            waited = {}
            for o in prog.ops[e]:
                for c, s in o.deps.items():
                    if isinstance(c, str):
                        v = prog.ops[c][s].val
                        sem = prog.eng_sems[c]
                    else:
                        v = 16 * s
                        sem = prog.dma_sems[c[1]]
                    if waited.get(c, 0) < v:
                        eng.wait_ge(sem, v)
                        waited[c] = v
                if o.fn is None:
                    continue
                ins = o.fn(eng)
                if o.dma_sem is not None:
                    if not isinstance(ins, (list, tuple)):
                        ins = [ins]
                    assert len(ins) == o.ndma
                    for i in ins:
                        i.then_inc(prog.dma_sems[o.dma_sem], 16)
                elif o.signal:
                    ins.then_inc(prog.eng_sems[e], 1)

        with nc.Block() as block:
            @block.tensor
            def _(eng):
                run("pe", eng)

            @block.scalar
            def _(eng):
                run("act", eng)

            @block.vector
            def _(eng):
                run("dve", eng)

            @block.gpsimd
            def _(eng):
                run("pool", eng)

            @block.sync
            def _(eng):
                run("sp", eng)


class Cx:
    def __init__(self):
        self.nc = bass.Bass("TRN2", target_bir_lowering=False)
        self.P = Prog(self.nc)
        self.n = 0
        self.ps = [self.nc.alloc_psum_tensor(f"ps{i}", [128, 512], F32) for i in range(8)]
        self.ps_regs = [Reg() for _ in range(8)]
        self.pe_defer = []
        self.ones = self.sb([128, 128], BF16, "ones")
        self.ones_r = Reg()
        self.P.op("pool", lambda e: e.memset(self.ones[:, :], 1.0), writes=[self.ones_r])

    def uid(self, s):
        self.n += 1
        return f"{s}{self.n}"

    def sb(self, shape, dt, name="t"):
        return self.nc.alloc_sbuf_tensor(self.uid(name), list(shape), dt)

    def din(self, name, shape, dt=F32):
        return self.nc.dram_tensor(name, list(shape), dt, kind="ExternalInput").ap()

    def dout(self, name, shape, dt=F32):
        return self.nc.dram_tensor(name, list(shape), dt, kind="ExternalOutput").ap()

    def flush_pe(self):
        d = self.pe_defer
        self.pe_defer = []
        for f in d:
            f()

    def load(self, dram_ap, shape, dt=F32, name="c", eng="sp"):
        t = self.sb(shape, dt, name)
        r = Reg()
        sl = tuple(slice(None) for _ in shape)
        self.P.dma(eng, self.uid("ld"), lambda e: e.dma_start(out=t[sl], in_=dram_ap), writes=[r])
        return t, r


class Ring:
    def __init__(self, cx, n, shape, dt, name="r"):
        self.tiles = [cx.sb(shape, dt, name) for _ in range(n)]
        self.regs = [Reg() for _ in range(n)]
        self.keys = [cx.uid(name + "k") for _ in range(n)]
        self.i = 0
        self.n = n

    @classmethod
    def over(cls, cx, tiles, regs, name="r"):
        o = cls.__new__(cls)
        o.tiles = tiles
        o.regs = regs
        o.keys = [cx.uid(name + "k") for _ in tiles]
        o.i = 0
        o.n = len(tiles)
        return o

    def next(self):
        j = self.i % self.n
        self.i += 1
        return self.tiles[j], self.regs[j], self.keys[j]


def load_norm(cx, xT, gcol, gcol_r, Tn, h, h_regs, rings=None, x_regs=None):
    P = cx.P
    nb = (Tn + 511) // 512
    xs = Ring(cx, 3, [128, Tn], F32, "xs") if rings is None else rings[0]
    sq = Ring(cx, 2, [128, Tn], BF16, "sq") if rings is None else rings[1]
    rstd = cx.sb([128, Tn], F32, "rstd")
    rstd_r = Reg()
    for c in range(KC_D):
        t, r, k = xs.next()
        P.dma("sp", k, lambda e, t=t, c=c: e.dma_start(out=t[:, 0:Tn], in_=xT[c * 128:(c + 1) * 128, :]), reads=([x_regs[c]] if x_regs else []), writes=[r])
        s, sr, _ = sq.next()
        P.op("act", lambda e, t=t, s=s: e.activation(out=s[:, 0:Tn], in_=t[:, 0:Tn], func=AF.Square), reads=[r], writes=[sr])
        P.op("dve", lambda e, t=t, c=c: e.tensor_scalar(out=h[:, c, :], in0=t[:, 0:Tn], scalar1=gcol[:, c:c + 1], scalar2=None, op0=ALU.mult),
             reads=[r, gcol_r], writes=[h_regs[c]])
        for j in range(nb):
            n = min(512, Tn - j * 512)
            P.op("pe", lambda e, s=s, j=j, n=n, c=c: e.matmul(cx.ps[j][:, 0:n], lhsT=cx.ones[:, :], rhs=s[:, j * 512:j * 512 + n],
                                                          start=(c == 0), stop=(c == KC_D - 1)),
                 reads=[sr, cx.ones_r], writes=[cx.ps_regs[j]])
    for j in range(nb):
        n = min(512, Tn - j * 512)
        P.op("act", lambda e, j=j, n=n: e.activation(out=rstd[:, j * 512:j * 512 + n], in_=cx.ps[j][:, 0:n], func=AF.Sqrt,
                                                   bias=cx.eps_t[:, 0:1], scale=1.0 / D),
             reads=[cx.ps_regs[j], cx.eps_r], writes=[rstd_r])
    P.op("dve", lambda e: e.reciprocal(out=rstd[:, :], in_=rstd[:, :]), reads=[rstd_r], writes=[rstd_r])
    for c in range(KC_D):
        P.op("dve", lambda e, c=c: e.tensor_tensor(out=h[:, c, :], in0=h[:, c, :], in1=rstd[:, :], op=ALU.mult),
             reads=[h_regs[c], rstd_r], writes=[h_regs[c]])
    return xs, sq


class WStream:
    def __init__(self, cx, kcmax, gw, nslots=2):
        self.cx = cx
        self.gw = gw
        self.kcmax = kcmax
        self.nparts = (kcmax + 7) // 8
        self.slots = [cx.sb([128, kcmax, gw], BF16, "ws") for _ in range(nslots)]
        self.regs = [[Reg() for _ in range(self.nparts)] for _ in range(nslots)]
        self.keys = [[cx.uid("wk") for _ in range(self.nparts)] for _ in range(nslots)]
        self.i = 0
        self.nslots = nslots

    def load(self, W, kc_n, segs):
        cx = self.cx
        s = self.i % self.nslots
        self.i += 1
        Wv = W.rearrange("(kc p) n -> p kc n", p=128)
        for q in range((kc_n + 7) // 8):
            k0, k1 = q * 8, min(kc_n, q * 8 + 8)

            def fn(e, s=s, k0=k0, k1=k1):
                return [e.dma_start(out=self.slots[s][:, k0:k1, d0:d0 + w], in_=Wv[:, k0:k1, c0:c0 + w]) for (d0, c0, w) in segs]
            cx.P.dma("pool", self.keys[s][q], fn, writes=[self.regs[s][q]], n=len(segs))
        return s


def gemm(cx, ws, W, kc_n, groups, subtiles):
    P = cx.P
    unit = cx.unit if hasattr(cx, "unit") else 0
    for gi, g in enumerate(groups):
        s = ws.load(W, kc_n, g["segs"])
        slot = ws.slots[s]
        for (stn, ht, hr, tok0, n) in subtiles:
            if stn == "halo" and not g.get("halo"):
                continue
            bset = [0, 1, 2, 3] if unit % 2 == 0 else [4, 5, 6, 7]
            unit += 1
            if g["kind"] == "fm":
                widths = g["widths"]
                for kc in range(kc_n):
                    for mi, wd in enumerate(widths):
                        b = bset[mi]
                        P.op("pe", lambda e, b=b, kc=kc, mi=mi, wd=wd, ht=ht, tok0=tok0, n=n, slot=slot: e.matmul(
                            cx.ps[b][0:wd, 0:n], lhsT=slot[:, kc, mi * 128:mi * 128 + wd], rhs=ht[:, kc, tok0:tok0 + n],
                            start=(kc == 0), stop=(kc == kc_n - 1)),
                            reads=[ws.regs[s][kc // 8], hr[kc]], writes=[cx.ps_regs[b]])
                    if cx.pe_defer and kc % 8 == 7:
                        cx.pe_defer.pop(0)()
                nb = len(widths)
            else:
                ntb = n // 128
                gwid = g["gw"]
                for kc in range(kc_n):
                    for tb in range(ntb):
                        b = bset[tb]
                        P.op("pe", lambda e, b=b, kc=kc, tb=tb, ht=ht, tok0=tok0, slot=slot, gwid=gwid: e.matmul(
                            cx.ps[b][:, 0:gwid], lhsT=ht[:, kc, tok0 + tb * 128:tok0 + (tb + 1) * 128], rhs=slot[:, kc, 0:gwid],
                            start=(kc == 0), stop=(kc == kc_n - 1)),
                            reads=[ws.regs[s][kc // 8], hr[kc]], writes=[cx.ps_regs[b]])
                nb = ntb
            cx.flush_pe()
            g["epi"](gi, g, stn, tok0, n, bset)
            if getattr(cx, "unit_hook", None):
                cx.unit_hook()
    cx.flush_pe()
    cx.unit = unit


def build_l1():
    cx = Cx()
    P = cx.P
    nc = cx.nc
    xT = cx.din("xT", [D, T])
    xh = cx.din("xh", [D, 16])
    W = cx.din("w_in", [D, 8208])
    gcol_d = cx.din("gcol", [128, KC_D])
    qk_d = cx.din("qkg", [128, 2])
    bf_d = cx.din("bf", [16, 1])
    pw_d = cx.din("pool_w", [4, 512, 512])
    psc_d = cx.din("pscol", [128, 16])
    icn_d = cx.din("invcnt", [128, 4, 16])
    qT = cx.dout("qT", [NH, 128, T], BF16)
    kT = cx.dout("kT", [NH, 128, T], BF16)
    Vo = cx.dout("V", [T, MIX], BF16)
    lf = cx.dout("logf", [NH, T])
    pmT = cx.dout("pmT", [MIX, T], BF16)

    cx.eps_t = cx.sb([128, 1], F32, "eps")
    cx.eps_r = Reg()
    P.op("pool", lambda e: e.memset(cx.eps_t[:, :], EPS), writes=[cx.eps_r])
    gcol, gcol_r = cx.load(gcol_d, [128, KC_D])
    qkg, qkg_r = cx.load(qk_d, [128, 2])
    bft, bft_r = cx.load(bf_d, [16, 1])
    psc, psc_r = cx.load(psc_d, [128, 16])
    icn, icn_r = cx.load(icn_d, [128, 4, 16])
    P.op("dve", lambda e: e.tensor_scalar(out=qkg[:, 0:1], in0=qkg[:, 0:1], scalar1=float(128 ** -0.5), scalar2=None, op0=ALU.mult),
         reads=[qkg_r], writes=[qkg_r])
    P.op("dve", lambda e: e.tensor_scalar(out=bft[:, :], in0=bft[:, :], scalar1=-1.0, scalar2=None, op0=ALU.mult),
         reads=[bft_r], writes=[bft_r])

    h = cx.sb([128, KC_D, T], BF16, "h")
    h_regs = [Reg() for _ in range(KC_D)]
    hh = cx.sb([128, KC_D, 16], BF16, "hh")
    hh_regs = [Reg() for _ in range(KC_D)]
    zb = [cx.sb([128, 16 + T], F32, "zb") for _ in range(2)]
    zb_r = [Reg() for _ in range(2)]
    pa = [cx.sb([128, 16 + T], F32, "pa") for _ in range(1)]
    pa_r = [Reg() for _ in range(1)]
    pbb = [cx.sb([128, 16 + T], F32, "pb") for _ in range(1)]
    pb_r = [Reg() for _ in range(1)]
    sqring = Ring(cx, 2, [128, T], BF16, "sq")
    xsring = Ring.over(cx, [zb[0], zb[1], pa[0]], [zb_r[0], zb_r[1], pa_r[0]], "xs")
    load_norm(cx, xh, gcol, gcol_r, 16, hh, hh_regs, rings=(xsring, sqring))
    load_norm(cx, xT, gcol, gcol_r, T, h, h_regs, rings=(xsring, sqring))

    ws = WStream(cx, KC_D, 384)
    subtiles = [("halo", hh, hh_regs, 0, 16), ("s0", h, h_regs, 0, 512), ("s1", h, h_regs, 512, 512)]

    pooled = cx.sb([128, 16, T], BF16, "pooled")
    pooled_regs = [Reg() for _ in range(16)]
    pcnt = [0]

    def epi_zp(gi, g, stn, tok0, n, bset):
        grp = g["pg"]
        wlen = (2, 4, 8, 16)[grp]
        for mi in range(2):
            off = 0 if stn == "halo" else 16 + tok0
            P.op("act", lambda e, mi=mi, off=off, n=n, b=bset[mi]: e.activation(out=zb[mi][:, off:off + n], in_=cx.ps[b][:, 0:n], func=AF.Copy),
                 reads=[cx.ps_regs[bset[mi]]], writes=[zb_r[mi]])
        if stn != "s1":
            return
        L = 16 + T
        for mi in range(2):
            ch = g["ch0"] + mi
            j = 0
            src, src_r = zb[mi], zb_r[mi]
            bufs = [(pa[j], pa_r[j]), (pbb[j], pb_r[j])]
            sh = 1
            bi = 0
            while sh < wlen:
                dst, dst_r = bufs[bi]
                eng = "dve"
                P.op(eng, lambda e, dst=dst, src=src, sh=sh: e.tensor_tensor(out=dst[:, sh:L], in0=src[:, sh:L], in1=src[:, 0:L - sh], op=ALU.add),
                     reads=[src_r], writes=[dst_r])
                src, src_r = dst, dst_r
                bi ^= 1
                sh *= 2
            P.op("dve", lambda e, src=src, mi=mi, ch=ch, wlen=wlen: e.scalar_tensor_tensor(
                out=pooled[:, ch, :], in0=src[:, 16:L], scalar=1.0 / wlen, in1=zb[mi][:, 16:L], op0=ALU.mult, op1=ALU.subtract),
                reads=[src_r, zb_r[mi]], writes=[pooled_regs[ch]])
            dst, dst_r = bufs[bi]
            P.op("dve", lambda e, dst=dst, src=src, grp=grp: e.tensor_tensor(out=dst[:, 0:16], in0=src[:, 16:32], in1=icn[:, grp, :], op=ALU.mult),
                 reads=[src_r, icn_r], writes=[dst_r])
            P.op("dve", lambda e, dst=dst, mi=mi, ch=ch: e.tensor_tensor(out=pooled[:, ch, 0:16], in0=dst[:, 0:16], in1=zb[mi][:, 16:32], op=ALU.subtract),
                 reads=[dst_r, zb_r[mi]], writes=[pooled_regs[ch]])

    lfr = Ring(cx, 2, [16, 512], F32, "lf")

    def epi_f(gi, g, stn, tok0, n, bset):
        t, r, k = lfr.next()
        b = bset[0]
        P.op("act", lambda e: e.activation(out=t[:, :], in_=cx.ps[b][0:16, 0:n], func=AF.Exp, bias=bft[:, 0:1], scale=-1.0),
             reads=[cx.ps_regs[b], bft_r], writes=[r])
        P.op("act", lambda e: e.activation(out=t[:, :], in_=t[:, :], func=AF.Ln, bias=1.0, scale=1.0), reads=[r], writes=[r])
        P.op("dve", lambda e: e.tensor_scalar(out=t[:, :], in0=t[:, :], scalar1=-1.0, scalar2=None, op0=ALU.mult), reads=[r], writes=[r])
        P.dma("sp", k, lambda e: e.dma_start(out=lf[:, tok0:tok0 + n], in_=t[:, :]), reads=[r])

    vr = Ring(cx, 4, [128, 256], BF16, "vo")

    def epi_v(gi, g, stn, tok0, n, bset):
        c0 = g["c0"]
        for tb in range(4):
            t, r, k = vr.next()
            b = bset[tb]
            P.op("act", lambda e, t=t, b=b: e.activation(out=t[:, :], in_=cx.ps[b][:, 0:256], func=AF.Copy), reads=[cx.ps_regs[b]], writes=[r])
            r0 = tok0 + tb * 128
            P.dma("sp", k, lambda e, t=t, r0=r0: e.dma_start(out=Vo[r0:r0 + 128, c0:c0 + 256], in_=t[:, :]), reads=[r])

    sqr = Ring(cx, 3, [128, 512], BF16, "qsq")
    rtr = Ring(cx, 3, [128, 512], F32, "qrt")
    qor = Ring(cx, 3, [128, 512], BF16, "qo")

    def epi_qk(gi, g, stn, tok0, n, bset):
        which = g["which"]
        dst = qT if which == 0 else kT
        nbk = bset[3]
        for mi in range(len(g["widths"])):
            hd = g["h0"] + mi
            b = bset[mi]
            s, sr, _ = sqr.next()
            P.op("act", lambda e, s=s, b=b: e.activation(out=s[:, :], in_=cx.ps[b][:, :], func=AF.Square), reads=[cx.ps_regs[b]], writes=[sr])

            def later(s=s, sr=sr, b=b, hd=hd, nbk=nbk):
                P.op("pe", lambda e: e.matmul(cx.ps[nbk][:, :], lhsT=cx.ones[:, :], rhs=s[:, :], start=True, stop=True),
                     reads=[sr, cx.ones_r], writes=[cx.ps_regs[nbk]])
                rt, rr, _ = rtr.next()
                P.op("act", lambda e: e.activation(out=rt[:, :], in_=cx.ps[nbk][:, :], func=AF.Sqrt, bias=cx.eps_t[:, 0:1], scale=1.0 / 128),
                     reads=[cx.ps_regs[nbk], cx.eps_r], writes=[rr])
                P.op("dve", lambda e: e.reciprocal(out=rt[:, :], in_=rt[:, :]), reads=[rr], writes=[rr])
                o, orr, k = qor.next()
                P.op("dve", lambda e: e.scalar_tensor_tensor(out=o[:, :], in0=cx.ps[b][:, :], scalar=qkg[:, which:which + 1], in1=rt[:, :],
                                                             op0=ALU.mult, op1=ALU.mult),
                     reads=[cx.ps_regs[b], qkg_r, rr], writes=[orr])
                P.dma("sp", k, lambda e: e.dma_start(out=dst[hd, :, tok0:tok0 + n], in_=o[:, :]), reads=[orr])
            cx.pe_defer.append(later)

    groups = []
    for cp in range(8):
        c0 = 6160 + cp * 256
        groups.append(dict(kind="fm", segs=[(0, c0, 256)], widths=[128] * 2, epi=epi_zp, halo=True, pg=cp // 2, ch0=cp * 2))
    groups.append(dict(kind="fm", segs=[(0, 6144, 16)], widths=[16], epi=epi_f))
    for vg in range(8):
        groups.append(dict(kind="tm", segs=[(0, 4096 + vg * 256, 256)], gw=256, epi=epi_v, c0=vg * 256))
    for which in range(2):
        for h0 in range(0, 16, 3):
            nh = min(3, 16 - h0)
            c0 = which * 2048 + h0 * 128
            groups.append(dict(kind="fm", segs=[(0, c0, nh * 128)], widths=[128] * nh, epi=epi_qk, which=which, h0=h0))
    gemm(cx, ws, W, KC_D, groups, subtiles)

    pmr = Ring(cx, 3, [128, 512], BF16, "pmo")
    for pg in range(4):
        def epi_pm(gi, g, stn, tok0, n, bset, pg=pg):
            for mi in range(2):
                ch = pg * 4 + g["m0"] + mi
                o, orr, k = pmr.next()
                b = bset[mi]
                P.op("act", lambda e, o=o, b=b, ch=ch: e.activation(out=o[:, :], in_=cx.ps[b][:, :], func=AF.Copy, scale=psc[:, ch:ch + 1]),
                     reads=[cx.ps_regs[b], psc_r], writes=[orr])
                P.dma("sp", k, lambda e, o=o, ch=ch: e.dma_start(out=pmT[ch * 128:(ch + 1) * 128, tok0:tok0 + n], in_=o[:, :]), reads=[orr])
        pview = pooled[:, pg * 4:(pg + 1) * 4, :]
        gemm(cx, ws, pw_d[pg], 4, [dict(kind="fm", segs=[(0, m0 * 128, 256)], widths=[128] * 2, epi=epi_pm, m0=m0) for m0 in (0, 2)],
             [("s0", pview, pooled_regs[pg * 4:(pg + 1) * 4], 0, 512), ("s1", pview, pooled_regs[pg * 4:(pg + 1) * 4], 512, 512)])
    P.wait_all_dma("sp")
    P.emit()
    return nc


def colmajor(v, n=128):
    return np.ascontiguousarray(np.asarray(v, np.float32).reshape(-1, n).T)


def run(nc, in_maps, trace=False):
    res = run_bass_kernel_spmd(nc, in_maps, core_ids=list(range(NCORES)), trace=trace)
    return res


def launch1(inp, trace=False):
    x = np.asarray(inp["x"], np.float32)[0]
    nc = build_l1()
    w_in = np.ascontiguousarray(np.asarray(inp["l0_w_in"], np.float32))
    gcol = colmajor(inp["l0_mix_norm"])
    qkg = np.stack([np.asarray(inp["l0_q_gain"], np.float32), np.asarray(inp["l0_k_gain"], np.float32)], 1)
    bf = np.asarray(inp["l0_b_f"], np.float32).reshape(16, 1)
    pw = np.ascontiguousarray(np.asarray(inp["l0_pool_w"], np.float32))
    psc = colmajor(inp["l0_pool_scale"])
    maps = []
    for c in range(NCORES):
        xs = x[c * T:(c + 1) * T]
        xh = x[c * T - 16:c * T] if c > 0 else np.zeros((16, D), np.float32)
        icn = np.zeros((128, 4, 16), np.float32)
        for g, w in enumerate((2, 4, 8, 16)):
            pos = np.arange(16) + 1 + c * T
            icn[:, g, :] = 1.0 / np.minimum(pos, w)
        maps.append(dict(xT=np.ascontiguousarray(xs.T), xh=np.ascontiguousarray(xh.T), w_in=w_in, gcol=gcol,
                         qkg=np.ascontiguousarray(qkg), bf=bf, pool_w=pw, pscol=psc, invcnt=icn))
    return run(nc, maps, trace)


HPC = NH // NCORES
NQT = SEQ // 512
NKB = SEQ // 128


def build_l2a():
    cx = Cx()
    P = cx.P
    qT = cx.din("qT", [HPC, 128, SEQ], BF16)
    kT = cx.din("kT", [HPC, 128, SEQ], BF16)
    Vd = cx.din("V", [HPC, 128, NKB, 128], BF16)
    lf6 = cx.din("lf6", [HPC, 6, SEQ])
    coef_d = cx.din("coef", [6, 8])
    tri_d = cx.din("tri", [128, 128])
    aT = cx.dout("aT", [HPC, 128, SEQ], BF16)
    coef, coef_r = cx.load(coef_d, [6, 8])
    tri, tri_r = cx.load(tri_d, [128, 128])
    qs, ks, vs, qa, ka = [], [], [], [], []
    for hh in range(HPC):
        q_t, q_r = cx.load(qT[hh], [128, SEQ], BF16, "q")
        k_t, k_r = cx.load(kT[hh], [128, SEQ], BF16, "k")
        v_t, v_r = cx.load(Vd[hh], [128, NKB, 128], BF16, "v")
        qs.append((q_t, q_r)); ks.append((k_t, k_r)); vs.append((v_t, v_r))
    SG = 2048
    lft = cx.sb([6, SG], F32, "lft"); lft_r = Reg()
    c6 = cx.sb([6, SG], F32, "c6"); c6_r = Reg()
    r1 = cx.sb([6, SG], F32, "r1"); r1_r = Reg()
    hi = cx.sb([6, SG], BF16, "hi"); hi_r = Reg()
    mid = cx.sb([6, SG], BF16, "mid"); mid_r = Reg()
    lo = cx.sb([6, SG], BF16, "lo"); lo_r = Reg()
    tmp = cx.sb([6, SG], F32, "tmpa"); tmp_r = Reg()
    carry = cx.sb([6, 1], F32, "carry"); carry_r = Reg()
    qa_t = cx.sb([6, SEQ], BF16, "qa"); qa_r = Reg()
    ka_t = cx.sb([6, SEQ], BF16, "ka"); ka_r = Reg()
    aug_done = [False] * HPC

    def build_aug(hh):
        for sg in range(SEQ // SG):
            t0 = sg * SG
            P.dma("sp", "lftk", lambda e, hh=hh, t0=t0: e.dma_start(out=lft[:, :], in_=lf6[hh, :, t0:t0 + SG]), writes=[lft_r])
            P.op("pool", lambda e: e.memset(tmp[:, :], 1.0), writes=[tmp_r])
            if sg == 0:
                P.op("dve", lambda e: e.tensor_tensor_scan(out=c6[:, :], data0=tmp[:, :], data1=lft[:, :], initial=0.0, op0=ALU.mult, op1=ALU.add),
                     reads=[lft_r, tmp_r], writes=[c6_r])
            else:
                P.op("dve", lambda e: e.tensor_tensor_scan(out=c6[:, :], data0=tmp[:, :], data1=lft[:, :], initial=carry[:, 0:1], op0=ALU.mult, op1=ALU.add),
                     reads=[lft_r, tmp_r, carry_r], writes=[c6_r])
            P.op("dve", lambda e: e.tensor_copy(out=carry[:, :], in_=c6[:, SG - 1:SG]), reads=[c6_r], writes=[carry_r])
            P.op("dve", lambda e: e.tensor_copy(out=hi[:, :], in_=c6[:, :]), reads=[c6_r], writes=[hi_r])
            P.op("dve", lambda e: e.tensor_tensor(out=r1[:, :], in0=c6[:, :], in1=hi[:, :], op=ALU.subtract), reads=[c6_r, hi_r], writes=[r1_r])
            P.op("dve", lambda e: e.tensor_copy(out=mid[:, :], in_=r1[:, :]), reads=[r1_r], writes=[mid_r])
            P.op("dve", lambda e: e.tensor_tensor(out=r1[:, :], in0=r1[:, :], in1=mid[:, :], op=ALU.subtract), reads=[r1_r, mid_r], writes=[r1_r])
            P.op("dve", lambda e: e.tensor_copy(out=lo[:, :], in_=r1[:, :]), reads=[r1_r], writes=[lo_r])
            for (dst, dst_r, o) in ((qa_t, qa_r, 0), (ka_t, ka_r, 4)):
                P.op("dve", lambda e, o=o: e.tensor_scalar(out=tmp[:, :], in0=hi[:, :], scalar1=coef[:, o:o + 1], scalar2=coef[:, o + 3:o + 4], op0=ALU.mult, op1=ALU.add),
                     reads=[hi_r, coef_r], writes=[tmp_r])
                P.op("dve", lambda e, o=o: e.scalar_tensor_tensor(out=tmp[:, :], in0=mid[:, :], scalar=coef[:, o + 1:o + 2], in1=tmp[:, :], op0=ALU.mult, op1=ALU.add),
                     reads=[mid_r, coef_r, tmp_r], writes=[tmp_r])
                P.op("dve", lambda e, o=o, dst=dst, t0=t0: e.scalar_tensor_tensor(out=dst[:, t0:t0 + SG], in0=lo[:, :], scalar=coef[:, o + 2:o + 3], in1=tmp[:, :], op0=ALU.mult, op1=ALU.add),
                     reads=[lo_r, coef_r, tmp_r], writes=[dst_r])
    LA = 2
    pr = Ring(cx, LA + 2, [128, 512], BF16, "pT")
    rdr = Ring(cx, 2, [128, 512], F32, "rden")
    aor = Ring(cx, 2, [128, 512], BF16, "ao")
    sbanks = [0, 1, 6, 7]
    blocks = []
    unit = 0
    for hh in range(HPC):
        for qt in range(NQT):
            ob = 2 + (unit % 2) * 2
            unit += 1
            nkb = 4 * qt + 4
            for kb in range(nkb):
                blocks.append(dict(hh=hh, qt=qt, kb=kb, nkb=nkb, ob=ob, db=ob + 1, u=unit))

    def stage_a(i):
        bl = blocks[i]
        hh, qt, kb = bl["hh"], bl["qt"], bl["kb"]
        if not aug_done[hh]:
            build_aug(hh)
            aug_done[hh] = True
        q_t, q_r = qs[hh]; k_t, k_r = ks[hh]
        q0 = qt * 512
        j = kb - 4 * qt
        c0 = 128 * j if j > 0 else 0
        n = 512 - c0
        sbank = sbanks[i % len(sbanks)]
        bl.update(c0=c0, n=n, j=j)
        P.op("pe", lambda e: e.matmul(cx.ps[sbank][:, 0:n], lhsT=k_t[:, kb * 128:(kb + 1) * 128], rhs=q_t[:, q0 + c0:q0 + 512], start=True, stop=False),
             reads=[k_r, q_r], writes=[cx.ps_regs[sbank]])
        P.op("pe", lambda e: e.matmul(cx.ps[sbank][:, 0:n], lhsT=ka_t[:, kb * 128:(kb + 1) * 128], rhs=qa_t[:, q0 + c0:q0 + 512], start=False, stop=True),
             reads=[ka_r, qa_r], writes=[cx.ps_regs[sbank]])
        pt, pt_r, _ = pr.next()
        bl.update(pt=pt, pt_r=pt_r)
        P.op("act", lambda e: e.activation(out=pt[:, 0:n], in_=cx.ps[sbank][:, 0:n], func=AF.Exp), reads=[cx.ps_regs[sbank]], writes=[pt_r])
        if j >= 0:
            P.op("pool", lambda e: e.tensor_tensor(out=pt[:, 0:128], in0=pt[:, 0:128], in1=tri[:, :], op=ALU.mult), reads=[pt_r, tri_r], writes=[pt_r])

    def stage_b(i):
        bl = blocks[i]
        hh, qt, kb, nkb, ob, db = bl["hh"], bl["qt"], bl["kb"], bl["nkb"], bl["ob"], bl["db"]
        c0, n, pt, pt_r = bl["c0"], bl["n"], bl["pt"], bl["pt_r"]
        v_t, v_r = vs[hh]
        q0 = qt * 512
        P.op("pe", lambda e: e.matmul(cx.ps[ob][:, c0:512], lhsT=v_t[:, kb, :], rhs=pt[:, 0:n], start=(kb == 0), stop=(kb == nkb - 1)),
             reads=[v_r, pt_r], writes=[cx.ps_regs[ob]])
        P.op("pe", lambda e: e.matmul(cx.ps[db][:, c0:512], lhsT=cx.ones[:, :], rhs=pt[:, 0:n], start=(kb == 0), stop=(kb == nkb - 1)),
             reads=[cx.ones_r, pt_r], writes=[cx.ps_regs[db]])
        if kb == nkb - 1:
            rd, rd_r, _ = rdr.next()
            P.op("dve", lambda e: e.reciprocal(out=rd[:, :], in_=cx.ps[db][:, :]), reads=[cx.ps_regs[db]], writes=[rd_r])
            ao, ao_r, k = aor.next()
            P.op("dve", lambda e: e.tensor_tensor(out=ao[:, :], in0=cx.ps[ob][:, :], in1=rd[:, :], op=ALU.mult), reads=[cx.ps_regs[ob], rd_r], writes=[ao_r])
            P.dma("sp", k, lambda e: e.dma_start(out=aT[hh, :, q0:q0 + 512], in_=ao[:, :]), reads=[ao_r])
    for i in range(len(blocks) + LA):
        if i < len(blocks):
            stage_a(i)
        if i - LA >= 0:
            stage_b(i - LA)
    P.wait_all_dma("sp")
    P.emit()
    return cx.nc


def launch2a(inp, l1res, trace=False):
    nc = build_l2a()
    qT = np.concatenate([np.asarray(l1res[c]["qT"]) for c in range(NCORES)], axis=2)
    kT = np.concatenate([np.asarray(l1res[c]["kT"]) for c in range(NCORES)], axis=2)
    V = np.concatenate([np.asarray(l1res[c]["V"]) for c in range(NCORES)], axis=0)
    lf = np.concatenate([np.asarray(l1res[c]["logf"]) for c in range(NCORES)], axis=1)
    coef = np.zeros((6, 8), np.float32)
    coef[0, 0] = 1; coef[1, 1] = 1; coef[2, 2] = 1; coef[3:6, 3] = 1
    coef[3, 4] = -1; coef[4, 5] = -1; coef[5, 6] = -1; coef[0:3, 7] = 1
    tri = np.triu(np.ones((128, 128), np.float32))
    maps = []
    for c in range(NCORES):
        hs = slice(c * HPC, (c + 1) * HPC)
        Vh = V.reshape(NKB, 128, NH, 128)[:, :, hs].transpose(2, 1, 0, 3)
        maps.append(dict(qT=np.ascontiguousarray(qT[hs]), kT=np.ascontiguousarray(kT[hs]), V=np.ascontiguousarray(Vh),
                         lf6=np.ascontiguousarray(np.repeat(lf[hs][:, None, :], 6, axis=1)), coef=coef, tri=tri))
    return run(nc, maps, trace)


def build_bc(layer):
    cx = Cx()
    P = cx.P
    xT = cx.din("xT", [D, T])
    hc = cx.din("hcatT", [D, T], BF16)
    w_out = cx.din("w_out", [D, D])
    fg_d = cx.din("fgcol", [128, KC_D])
    w_up = cx.din("w_up", [D, 2 * DFF])
    x1T = cx.dout("x1T", [D, T])
    actT = cx.dout("actT", [DFF, T], BF16)
    gf_o = cx.dout("gf", [128, NFC, 2])
    uf_o = cx.dout("uf", [128, NFC, 2])
    gl_o = cx.dout("gl", [128, NFC, 2])
    fcw_d = cx.din("fcwcol", [128, NFC, 3])
    fcw, fcw_r = cx.load(fcw_d, [128, NFC, 3])
    cx.eps_t = cx.sb([128, 1], F32, "eps")
    cx.eps_r = Reg()
    P.op("pool", lambda e: e.memset(cx.eps_t[:, :], EPS), writes=[cx.eps_r])
    fg, fg_r = cx.load(fg_d, [128, KC_D])
    h = cx.sb([128, KC_D, T], BF16, "h")
    h_regs = [Reg() for _ in range(KC_D)]
    hv = hc.rearrange("(c p) t -> p c t", p=128)
    if layer == 0:
        for q in range(4):
            P.dma("sp", cx.uid("hl"), lambda e, q=q: e.dma_start(out=h[:, q * 8:(q + 1) * 8, :], in_=hv[:, q * 8:(q + 1) * 8, :]),
                  writes=h_regs[q * 8:(q + 1) * 8])
    else:
        for q in range(2, 4):
            P.dma("sp", cx.uid("hl"), lambda e, q=q: e.dma_start(out=h[:, q * 8:(q + 1) * 8, :], in_=hv[:, q * 8:(q + 1) * 8, :]),
                  writes=h_regs[q * 8:(q + 1) * 8])
        cv_d = cx.din("cvx", [MIX, 2 + T], BF16)
        gb_d = cx.din("gbT", [MIX, T], BF16)
        cw_d = cx.din("cwcol", [128, 16, 3])
        cw, cw_r = cx.load(cw_d, [128, 16, 3])
        cvr = Ring(cx, 2, [128, 2 + T], BF16, "cv")
        gbr = Ring(cx, 2, [128, T], BF16, "gb")
        yr = Ring(cx, 2, [128, T], F32, "cy")
        for j in range(16):
            cvt, cvt_r, k1 = cvr.next()
            gbt, gbt_r, k2 = gbr.next()
            y, y_r, _ = yr.next()
            P.dma("sp", k1, lambda e, cvt=cvt, j=j: e.dma_start(out=cvt[:, :], in_=cv_d[j * 128:(j + 1) * 128, :]), writes=[cvt_r])
            P.dma("sp", k2, lambda e, gbt=gbt, j=j: e.dma_start(out=gbt[:, :], in_=gb_d[j * 128:(j + 1) * 128, :]), writes=[gbt_r])
            P.op("dve", lambda e, y=y, cvt=cvt, j=j: e.tensor_scalar(out=y[:, :], in0=cvt[:, 2:2 + T], scalar1=cw[:, j, 2:3], scalar2=None, op0=ALU.mult),
                 reads=[cvt_r, cw_r], writes=[y_r])
            P.op("dve", lambda e, y=y, cvt=cvt, j=j: e.scalar_tensor_tensor(out=y[:, :], in0=cvt[:, 1:1 + T], scalar=cw[:, j, 1:2], in1=y[:, :], op0=ALU.mult, op1=ALU.add),
                 reads=[cvt_r, cw_r, y_r], writes=[y_r])
            P.op("dve", lambda e, y=y, cvt=cvt, j=j: e.scalar_tensor_tensor(out=y[:, :], in0=cvt[:, 0:T], scalar=cw[:, j, 0:1], in1=y[:, :], op0=ALU.mult, op1=ALU.add),
                 reads=[cvt_r, cw_r, y_r], writes=[y_r])
            P.op("dve", lambda e, y=y, gbt=gbt, j=j: e.tensor_tensor(out=h[:, j, :], in0=y[:, :], in1=gbt[:, :], op=ALU.mult),
                 reads=[y_r, gbt_r], writes=[h_regs[j]])
    ws = WStream(cx, KC_D, 512)
    subtiles = [("s0", h, h_regs, 0, 512), ("s1", h, h_regs, 512, 512)]
    x1_regs = [Reg() for _ in range(KC_D)]
    xr = Ring(cx, 3, [128, 512], F32, "xc")
    orr_ = Ring(cx, 3, [128, 512], F32, "xo")

    def epi_res(gi, g, stn, tok0, n, bset):
        for mi in range(len(g["widths"])):
            ch = g["ch0"] + mi
            b = bset[mi]
            xt, xt_r, k = xr.next()
            P.dma("sp", k, lambda e, xt=xt, ch=ch: e.dma_start(out=xt[:, :], in_=xT[ch * 128:(ch + 1) * 128, tok0:tok0 + n]), writes=[xt_r])
            o, o_r, k2 = orr_.next()
            P.op("dve", lambda e, o=o, xt=xt, b=b: e.tensor_tensor(out=o[:, :], in0=cx.ps[b][:, :], in1=xt[:, :], op=ALU.add),
                 reads=[cx.ps_regs[b], xt_r], writes=[o_r])
            P.dma("sp", k2, lambda e, o=o, ch=ch: e.dma_start(out=x1T[ch * 128:(ch + 1) * 128, tok0:tok0 + n], in_=o[:, :]),
                  reads=[o_r], writes=[x1_regs[ch]])
    groups = []
    for ch0 in range(0, KC_D, 4):
        groups.append(dict(kind="fm", segs=[(0, ch0 * 128, 512)], widths=[128] * 4, epi=epi_res, ch0=ch0))
    gemm(cx, ws, w_out, KC_D, groups, subtiles)
    load_norm(cx, x1T, fg, fg_r, T, h, h_regs, x_regs=x1_regs, rings=(Ring(cx, 2, [128, T], F32, "xs"), Ring(cx, 2, [128, T], BF16, "sq")))
    gbuf = [cx.sb([128, 2 + 512], F32, "gbuf") for _ in range(2)]
    gbuf_r = [Reg() for _ in range(2)]
    yr2 = Ring(cx, 3, [128, 512], F32, "fy")
    aor = Ring(cx, 4, [128, 512], BF16, "ao")
    gf = cx.sb([128, NFC, 2], F32, "gf"); gf_r = Reg()
    uf = cx.sb([128, NFC, 2], F32, "uf"); uf_r = Reg()
    gl = cx.sb([128, NFC, 2], F32, "gl"); gl_r = Reg()

    def epi_act(gi, g, stn, tok0, n, bset):
        for i in range(2):
            ch = g["ch0"] + i
            bg, bu = bset[i], bset[2 + i]
            gb, gb_r = gbuf[i], gbuf_r[i]
            import os
            if os.environ.get("NOHALO"):
                pass
            elif stn == "s0":
                P.op("dve", lambda e, gb=gb: e.memset(gb[:, 0:2], 0.0), writes=[gb_r])
            else:
                P.op("dve", lambda e, gb=gb: e.tensor_copy(out=gb[:, 0:2], in_=gb[:, 512:514]), reads=[gb_r], writes=[gb_r])
            P.op("act", lambda e, gb=gb, bg=bg: e.activation(out=gb[:, 2:514], in_=cx.ps[bg][:, :], func=AF.Copy), reads=[cx.ps_regs[bg]], writes=[gb_r])
            y, y_r, _ = yr2.next()
            P.op("act", lambda e, y=y, bg=bg, ch=ch: e.activation(out=y[:, :], in_=cx.ps[bg][:, :], func=AF.Copy, scale=fcw[:, ch, 2:3]),
                 reads=[cx.ps_regs[bg], fcw_r], writes=[y_r])
            P.op("dve", lambda e, y=y, gb=gb, ch=ch: e.scalar_tensor_tensor(out=y[:, :], in0=gb[:, 1:513], scalar=fcw[:, ch, 1:2], in1=y[:, :], op0=ALU.mult, op1=ALU.add),
                 reads=[gb_r, fcw_r, y_r], writes=[y_r])
            P.op("dve", lambda e, y=y, gb=gb, ch=ch: e.scalar_tensor_tensor(out=y[:, :], in0=gb[:, 0:512], scalar=fcw[:, ch, 0:1], in1=y[:, :], op0=ALU.mult, op1=ALU.add),
                 reads=[gb_r, fcw_r, y_r], writes=[y_r])
            P.op("act", lambda e, y=y: e.activation(out=y[:, :], in_=y[:, :], func=AF.Silu), reads=[y_r], writes=[y_r])
            o, o_r, k = aor.next()
            P.op("dve", lambda e, o=o, y=y, bu=bu: e.tensor_tensor(out=o[:, :], in0=cx.ps[bu][:, :], in1=y[:, :], op=ALU.mult),
                 reads=[cx.ps_regs[bu], y_r], writes=[o_r])
            P.dma("sp", k, lambda e, o=o, ch=ch: e.dma_start(out=actT[ch * 128:(ch + 1) * 128, tok0:tok0 + n], in_=o[:, :]), reads=[o_r])
            if os.environ.get("NOSIDE"):
                pass
            elif stn == "s0":
                P.op("dve", lambda e, gb=gb, ch=ch: e.tensor_copy(out=gf[:, ch, :], in_=gb[:, 2:4]), reads=[gb_r], writes=[gf_r])
                P.op("dve", lambda e, bu=bu, ch=ch: e.tensor_copy(out=uf[:, ch, :], in_=cx.ps[bu][:, 0:2]), reads=[cx.ps_regs[bu]], writes=[uf_r])
            else:
                P.op("dve", lambda e, gb=gb, ch=ch: e.tensor_copy(out=gl[:, ch, :], in_=gb[:, 512:514]), reads=[gb_r], writes=[gl_r])
    groups = []
    for j2 in range(NFC // 2):
        groups.append(dict(kind="fm", segs=[(0, j2 * 256, 256), (256, DFF + j2 * 256, 256)], widths=[128] * 4, epi=epi_act, ch0=2 * j2))
    gemm(cx, ws, w_up, KC_D, groups, subtiles)
    P.dma("sp", "gfo", lambda e: e.dma_start(out=gf_o[:, :, :], in_=gf[:, :, :]), reads=[gf_r])
    P.dma("sp", "ufo", lambda e: e.dma_start(out=uf_o[:, :, :], in_=uf[:, :, :]), reads=[uf_r])
    P.dma("sp", "glo", lambda e: e.dma_start(out=gl_o[:, :, :], in_=gl[:, :, :]), reads=[gl_r])
    P.wait_all_dma("sp")
    P.emit()
    return cx.nc


HK = NFC // 2


def build_d():
    cx = Cx()
    P = cx.P
    x1T = cx.din("x1T", [D, T])
    actT = cx.din("actT", [DFF, T], BF16)
    gf_d = cx.din("gf", [128, NFC, 2])
    uf_d = cx.din("uf", [128, NFC, 2])
    glp_d = cx.din("glp", [128, NFC, 2])
    cw_d = cx.din("cwcol", [128, NFC, 3])
    w_dn = cx.din("w_down", [DFF, D])
    x2T = cx.dout("x2T", [D, T])
    cw, cw_r = cx.load(cw_d, [128, NFC, 3])
    uf, uf_r = cx.load(uf_d, [128, NFC, 2])
    ext = cx.sb([128, NFC, 4], F32, "ext")
    ext_r = Reg()
    P.dma("sp", "extk", lambda e: [e.dma_start(out=ext[:, :, 0:2], in_=glp_d[:, :, :]), e.dma_start(out=ext[:, :, 2:4], in_=gf_d[:, :, :])],
          writes=[ext_r], n=2)
    yf = cx.sb([128, NFC, 2], F32, "yf"); yf_r = Reg()
    tf = cx.sb([128, NFC], F32, "tf"); tf_r = Reg()
    afix = cx.sb([128, NFC, 2], BF16, "afix"); afix_r = Reg()
    for t in range(2):
        P.op("dve", lambda e, t=t: e.tensor_tensor(out=yf[:, :, t], in0=ext[:, :, t + 2], in1=cw[:, :, 2], op=ALU.mult), reads=[ext_r, cw_r], writes=[yf_r])
        for i in (1, 0):
            P.op("dve", lambda e, t=t, i=i: e.tensor_tensor(out=tf[:, :], in0=ext[:, :, t + i], in1=cw[:, :, i], op=ALU.mult), reads=[ext_r, cw_r], writes=[tf_r])
            P.op("dve", lambda e, t=t: e.tensor_tensor(out=yf[:, :, t], in0=yf[:, :, t], in1=tf[:, :], op=ALU.add), reads=[yf_r, tf_r], writes=[yf_r])
    P.op("act", lambda e: e.activation(out=yf[:, :, :], in_=yf[:, :, :], func=AF.Silu), reads=[yf_r], writes=[yf_r])
    P.op("dve", lambda e: e.tensor_tensor(out=afix[:, :, :], in0=yf[:, :, :], in1=uf[:, :, :], op=ALU.mult), reads=[yf_r, uf_r], writes=[afix_r])
    act = cx.sb([128, HK, T], BF16, "act")
    act_regs = [Reg() for _ in range(HK)]
    ws = WStream(cx, HK, 256, nslots=3)
    xr = Ring(cx, 4, [128, 512], F32, "xc")
    orr_ = Ring(cx, 4, [128, 512], F32, "xo")
    x2_regs = [[Reg(), Reg()] for _ in range(KC_D)]
    av = actT.rearrange("(c p) t -> p c t", p=128)
    for hf in range(2):
        for q in range(0, HK, 8):
            q1 = min(HK, q + 8)
            P.dma("sp", f"actl{q}", lambda e, q=q, q1=q1, hf=hf: e.dma_start(out=act[:, q:q1, :], in_=av[:, hf * HK + q:hf * HK + q1, :]),
                  writes=act_regs[q:q1])
        P.op("dve", lambda e, hf=hf: e.tensor_copy(out=act[:, :, 0:2], in_=afix[:, hf * HK:(hf + 1) * HK, :]), reads=[afix_r], writes=act_regs)
        src = x1T if hf == 0 else x2T

        def epi_res(gi, g, stn, tok0, n, bset, src=src, hf=hf):
            sti = 0 if stn == "s0" else 1
            for mi in range(2):
                ch = g["ch0"] + mi
                b = bset[mi]
                xt, xt_r, k = xr.next()
                P.dma("sp", k, lambda e, xt=xt, ch=ch: e.dma_start(out=xt[:, :], in_=src[ch * 128:(ch + 1) * 128, tok0:tok0 + n]),
                      reads=([x2_regs[ch][sti]] if hf == 1 else []), writes=[xt_r])
                o, o_r, k2 = orr_.next()
                P.op("dve", lambda e, o=o, xt=xt, b=b: e.tensor_tensor(out=o[:, :], in0=cx.ps[b][:, :], in1=xt[:, :], op=ALU.add),
                     reads=[cx.ps_regs[b], xt_r], writes=[o_r])
                P.dma("sp", k2, lambda e, o=o, ch=ch: e.dma_start(out=x2T[ch * 128:(ch + 1) * 128, tok0:tok0 + n], in_=o[:, :]),
                      reads=[o_r], writes=[x2_regs[ch][sti]])
        groups = [dict(kind="fm", segs=[(0, ch0 * 128, 256)], widths=[128] * 2, epi=epi_res, ch0=ch0) for ch0 in range(0, KC_D, 2)]
        gemm(cx, ws, w_dn[hf * HK * 128:(hf + 1) * HK * 128, :], HK, groups,
             [("s0", act, act_regs, 0, 512), ("s1", act, act_regs, 512, 512)])
    P.wait_all_dma("sp")
    P.emit()
    return cx.nc


def build_e():
    cx = Cx()
    P = cx.P
    x2T = cx.din("xT", [D, T])
    g_d = cx.din("gcol", [128, KC_D])
    W = cx.din("w_in", [D, 10240])
    wT_d = cx.din("sgu_wT", [128, 16, 128])
    tri_d = cx.din("tri", [128, 128])
    bsb_d = cx.din("bsb", [128, 16, 128])
    ngb_d = cx.din("ngb", [128, MIX])
    cvT = cx.dout("cvT", [MIX, T], BF16)
    gbT = cx.dout("gbT", [MIX, T], BF16)
    dT = cx.dout("dT", [MIX, T], BF16)
    cx.eps_t = cx.sb([128, 1], F32, "eps")
    cx.eps_r = Reg()
    P.op("pool", lambda e: e.memset(cx.eps_t[:, :], EPS), writes=[cx.eps_r])
    gcol, gcol_r = cx.load(g_d, [128, KC_D])
    tri, tri_r = cx.load(tri_d, [128, 128])
    bsb, bsb_r = cx.load(bsb_d, [128, 16, 128])
    ngb, ngb_r = cx.load(ngb_d, [128, MIX])
    wtm, wtm_r = cx.load(wT_d, [128, 16, 128], BF16, "wtm", eng="pool")
    for hh in range(16):
        P.op("pool", lambda e, hh=hh: e.tensor_tensor(out=wtm[:, hh, :], in0=wtm[:, hh, :], in1=tri[:, :], op=ALU.mult),
             reads=[wtm_r, tri_r], writes=[wtm_r])
    vgel = [cx.sb([128, MIX], BF16, "vgel") for _ in range(8)]
    vgel_r = [Reg() for _ in range(8)]
    ug = cx.sb([128, 16, T], BF16, "ug")
    ug_r = [Reg() for _ in range(16)]
    h = cx.sb([128, KC_D, T], BF16, "h")
    h_regs = [Reg() for _ in range(KC_D)]
    load_norm(cx, x2T, gcol, gcol_r, T, h, h_regs, rings=(Ring(cx, 2, [128, T], F32, "xs"), Ring(cx, 2, [128, T], BF16, "sq")))
    ws = WStream(cx, KC_D, 256)
    subtiles = [("s0", h, h_regs, 0, 512), ("s1", h, h_regs, 512, 512)]

    def epi_zv(gi, g, stn, tok0, n, bset):
        c0 = g["c0"]
        for tb in range(4):
            tbg = tok0 // 128 + tb
            b = bset[tb]
            P.op("act", lambda e, tbg=tbg, b=b: e.activation(out=vgel[tbg][:, c0:c0 + 256], in_=cx.ps[b][:, 0:256], func=AF.Gelu),
                 reads=[cx.ps_regs[b]], writes=[vgel_r[tbg]])

    def epi_zu(gi, g, stn, tok0, n, bset):
        for mi in range(2):
            ch = g["ch0"] + mi
            b = bset[mi]
            P.op("act", lambda e, ch=ch, b=b: e.activation(out=ug[:, ch, tok0:tok0 + n], in_=cx.ps[b][:, :], func=AF.Gelu),
                 reads=[cx.ps_regs[b]], writes=[ug_r[ch]])
    tmr = Ring(cx, 2, [128, 512], F32, "cvtmp")
    cvr = Ring(cx, 3, [128, 512], BF16, "cvo")

    def epi_cv(gi, g, stn, tok0, n, bset):
        j = g["j"]
        tm, tm_r, _ = tmr.next()
        P.op("act", lambda e: e.activation(out=tm[:, :], in_=cx.ps[bset[0]][:, :], func=AF.Copy), reads=[cx.ps_regs[bset[0]]], writes=[tm_r])
        o, o_r, k = cvr.next()
        P.op("dve", lambda e: e.tensor_tensor(out=o[:, :], in0=cx.ps[bset[1]][:, :], in1=tm[:, :], op=ALU.mult),
             reads=[cx.ps_regs[bset[1]], tm_r], writes=[o_r])
        P.dma("sp", k, lambda e: e.dma_start(out=cvT[j * 128:(j + 1) * 128, tok0:tok0 + n], in_=o[:, :]), reads=[o_r])

    def epi_gb(gi, g, stn, tok0, n, bset):
        for mi in range(2):
            ch = g["ch0"] + mi
            b = bset[mi]
            o, o_r, k = cvr.next()
            P.op("act", lambda e, o=o, b=b: e.activation(out=o[:, :], in_=cx.ps[b][:, :], func=AF.Copy), reads=[cx.ps_regs[b]], writes=[o_r])
            P.dma("sp", k, lambda e, o=o, ch=ch: e.dma_start(out=gbT[ch * 128:(ch + 1) * 128, tok0:tok0 + n], in_=o[:, :]), reads=[o_r])
    groups = []
    for vg in range(8):
        groups.append(dict(kind="tm", segs=[(0, 8192 + vg * 256, 256)], gw=256, epi=epi_zv, c0=vg * 256))
    for ch0 in range(0, 16, 2):
        groups.append(dict(kind="fm", segs=[(0, 6144 + ch0 * 128, 256)], widths=[128] * 2, epi=epi_zu, ch0=ch0))
    for j in range(16):
        groups.append(dict(kind="fm", segs=[(0, j * 128, 128), (128, 4096 + j * 128, 128)], widths=[128] * 2, epi=epi_cv, j=j))
    for ch0 in range(0, 16, 2):
        groups.append(dict(kind="fm", segs=[(0, 2048 + ch0 * 128, 256)], widths=[128] * 2, epi=epi_gb, ch0=ch0))
    gemm(cx, ws, W, KC_D, groups, subtiles)
    def hflat(c):
        return h[:, c:c + 2, :].rearrange("p a b -> p (a b)"), [h_regs[c], h_regs[c + 1]]
    junk, junk_r = hflat(0)
    vns = [hflat(2), hflat(4)]
    dous = [hflat(6), hflat(8)]
    ss = cx.sb([128, 8], F32, "ss")
    ss_r = Reg()
    for tbg in range(8):
        P.op("act", lambda e, tbg=tbg: e.activation(out=junk, in_=vgel[tbg][:, :], func=AF.Square, accum_out=ss[:, tbg:tbg + 1]),
             reads=[vgel_r[tbg]], writes=junk_r + [ss_r])
    P.op("act", lambda e: e.activation(out=ss[:, :], in_=ss[:, :], func=AF.Sqrt, bias=cx.eps_t[:, 0:1], scale=1.0 / MIX),
         reads=[ss_r, cx.eps_r], writes=[ss_r])
    P.op("dve", lambda e: e.reciprocal(out=ss[:, :], in_=ss[:, :]), reads=[ss_r], writes=[ss_r])
    t1r = Ring(cx, 3, [128, 128], F32, "t1")
    dkeys = [cx.uid("dk"), cx.uid("dk")]
    dTv = dT.rearrange("(c p) t -> p c t", p=128)
    for tbg in range(8):
        vn, vn_r = vns[tbg % 2]
        do, do_r = dous[tbg % 2]
        P.op("dve", lambda e, vn=vn, tbg=tbg: e.scalar_tensor_tensor(out=vn, in0=vgel[tbg][:, :], scalar=ss[:, tbg:tbg + 1], in1=ngb[:, :],
                                                                    op0=ALU.mult, op1=ALU.mult),
             reads=[vgel_r[tbg], ss_r, ngb_r], writes=vn_r)
        bset = [0, 1, 2, 3] if tbg % 2 == 0 else [4, 5, 6, 7]
        for hh in range(16):
            b = bset[hh // 4]
            cc = (hh % 4) * 128
            P.op("pe", lambda e, vn=vn, hh=hh, b=b, cc=cc: e.matmul(cx.ps[b][:, cc:cc + 128], lhsT=vn[:, hh * 128:(hh + 1) * 128], rhs=wtm[:, hh, :],
                                                                   start=True, stop=True),
                 reads=vn_r + [wtm_r], writes=[cx.ps_regs[b]])
        for hh in range(16):
            b = bset[hh // 4]
            cc = (hh % 4) * 128
            t1, t1_r, _ = t1r.next()
            P.op("dve", lambda e, t1=t1, hh=hh, b=b, cc=cc: e.tensor_tensor(out=t1[:, :], in0=cx.ps[b][:, cc:cc + 128], in1=bsb[:, hh, :], op=ALU.add),
                 reads=[cx.ps_regs[b], bsb_r], writes=[t1_r])
            P.op("pool", lambda e, t1=t1, hh=hh, do=do, tbg=tbg: e.tensor_tensor(out=do[:, hh * 128:(hh + 1) * 128], in0=t1[:, :],
                                                                               in1=ug[:, hh, tbg * 128:(tbg + 1) * 128], op=ALU.mult),
                 reads=[t1_r, ug_r[hh]], writes=do_r)
        P.dma("sp", dkeys[tbg % 2], lambda e, do=do, tbg=tbg: e.dma_start(out=dTv[:, :, tbg * 128:(tbg + 1) * 128],
                                                                        in_=do.rearrange("p (c t) -> p c t", t=128)), reads=do_r)
    P.wait_all_dma("sp")
    P.emit()
    return cx.nc


def _cat(res, key, axis):
    return np.concatenate([np.asarray(res[c][key]) for c in range(NCORES)], axis=axis)


def _halo(full, c, n):
    if c == 0:
        return np.ascontiguousarray(np.concatenate([np.zeros((full.shape[0], n), full.dtype), full[:, :T]], axis=1))
    return np.ascontiguousarray(full[:, c * T - n:(c + 1) * T])


def convcol(conv, nchunks):
    return np.ascontiguousarray(np.asarray(conv, np.float32).T.reshape(nchunks, 128, 3).transpose(1, 0, 2))


def launch_bc(layer, xT_l, hcat_l, w_out, fnorm, w_up, fconv, extra=None):
    nc = build_bc(layer)
    fcw = convcol(fconv, NFC)
    w_out = np.ascontiguousarray(np.asarray(w_out, np.float32))
    w_up = np.ascontiguousarray(np.asarray(w_up, np.float32))
    fg = colmajor(fnorm)
    maps = []
    for c in range(NCORES):
        m = dict(xT=xT_l[c], hcatT=hcat_l[c], w_out=w_out, fgcol=fg, w_up=w_up, fcwcol=fcw)
        if extra is not None:
            m.update(extra[c])
        maps.append(m)
    return run(nc, maps).results


def launch_d(x1_l, res_bc, conv, w_down):
    nc = build_d()
    cw = convcol(conv, NFC)
    w_down = np.ascontiguousarray(np.asarray(w_down, np.float32))
    maps = []
    for c in range(NCORES):
        glp = np.asarray(res_bc[c - 1]["gl"]) if c > 0 else np.zeros((128, NFC, 2), np.float32)
        maps.append(dict(x1T=x1_l[c], actT=np.asarray(res_bc[c]["actT"]), gf=np.asarray(res_bc[c]["gf"]), uf=np.asarray(res_bc[c]["uf"]),
                         glp=np.ascontiguousarray(glp), cwcol=cw, w_down=w_down))
    return run(nc, maps).results


def kernel(**inp):
    x = np.asarray(inp["x"], np.float32)[0]
    xT_l = [np.ascontiguousarray(x[c * T:(c + 1) * T].T) for c in range(NCORES)]
    tri = np.triu(np.ones((128, 128), np.float32))
    r1 = launch1(inp).results
    r2 = launch2a(inp, r1).results
    aT = np.concatenate([np.asarray(r2[c]["aT"]) for c in range(NCORES)], axis=0).reshape(MIX, SEQ)
    hcat_l = [np.ascontiguousarray(np.concatenate([aT[:, c * T:(c + 1) * T], np.asarray(r1[c]["pmT"])], axis=0)) for c in range(NCORES)]
    rbc = launch_bc(0, xT_l, hcat_l, inp["l0_w_out"], inp["l0_ffn_norm"], inp["l0_ffn_w_up"], inp["l0_ffn_conv"])
    x1_l = [np.asarray(rbc[c]["x1T"]) for c in range(NCORES)]
    rd = launch_d(x1_l, rbc, inp["l0_ffn_conv"], inp["l0_ffn_w_down"])
    x2_l = [np.asarray(rd[c]["x2T"]) for c in range(NCORES)]
    nce = build_e()
    w_in1 = np.ascontiguousarray(np.asarray(inp["l1_w_in"], np.float32))
    wT = np.ascontiguousarray(np.asarray(inp["l1_sgu_w"], np.float32).transpose(2, 0, 1))
    bsb = np.ascontiguousarray(np.broadcast_to(np.asarray(inp["l1_sgu_b"], np.float32)[None], (128, 16, 128)))
    ngb = np.ascontiguousarray(np.broadcast_to(np.asarray(inp["l1_sgu_norm"], np.float32)[None], (128, MIX)))
    g1 = colmajor(inp["l1_mix_norm"])
    re_ = run(nce, [dict(xT=x2_l[c], gcol=g1, w_in=w_in1, sgu_wT=wT, tri=tri, bsb=bsb, ngb=ngb) for c in range(NCORES)]).results
    cvfull = _cat(re_, "cvT", 1)
    cw1 = np.ascontiguousarray(np.asarray(inp["l1_conv_w"], np.float32).T.reshape(16, 128, 3).transpose(1, 0, 2))
    hcat1 = [np.ascontiguousarray(np.concatenate([np.zeros((MIX, T), NPBF), np.asarray(re_[c]["dT"])], axis=0)) for c in range(NCORES)]
    extra = [dict(cvx=_halo(cvfull, c, 2), gbT=np.asarray(re_[c]["gbT"]), cwcol=cw1) for c in range(NCORES)]
    rbc1 = launch_bc(1, x2_l, hcat1, inp["l1_w_out"], inp["l1_ffn_norm"], inp["l1_ffn_w_up"], inp["l1_ffn_conv"], extra)
    x3_l = [np.asarray(rbc1[c]["x1T"]) for c in range(NCORES)]
    rd1 = launch_d(x3_l, rbc1, inp["l1_ffn_conv"], inp["l1_ffn_w_down"])
    out = np.concatenate([np.asarray(rd1[c]["x2T"]).T for c in range(NCORES)], axis=0)
    return np.ascontiguousarray(out[None].astype(np.float32))
```
